# Optimizing a Trainium2 kernel written in Bass

```python
import jax, jax.numpy as jnp
from jax import lax
import numpy as np

D_MODEL = 1024
BATCH = 8
SEQ = 2048
DEPTH = 2
DEC_BATCH = 128
DEC_SEQ = 8
PAST_LEN = 16384
PAGE_SIZE = 128

N_A_LAYERS = DEPTH // 2
N_B_LAYERS = DEPTH - N_A_LAYERS
D_RNN = D_MODEL
N_RNN_BLOCKS = 8
RNN_BLOCK = D_RNN // N_RNN_BLOCKS
CONV_WIDTH = 4
RG_C = 8.0
HEAD_DIM = 64
N_HEADS = D_MODEL // HEAD_DIM
N_KV_HEADS = 4
GROUP = N_HEADS // N_KV_HEADS
WINDOW = 128
ROT_DIM = HEAD_DIM // 4
ROPE_THETA = 500000.0
D_FF = ((8 * D_MODEL + 3 * 256 - 1) // (3 * 256)) * 256
EPS = 1e-6
NEG_INF = -1e30

kernel_name = 'yoco_rglru_swa_sink_adaln_step'


def rms_norm(x, g):
    xf = x.astype(jnp.float32)
    y = xf * lax.rsqrt(jnp.mean(xf * xf, axis=-1, keepdims=True) + EPS)
    return (y * g.astype(jnp.float32)).astype(x.dtype)


def modulate(h, shift, scale):
    return h * (1 + scale[:, None, :]) + shift[:, None, :]


def ada_mod(c, w, b, n):
    return jnp.split(jax.nn.silu(c) @ w + b, n, axis=-1)


def swiglu(h, w_in, w_out):
    gate, up = jnp.split(h @ w_in, 2, axis=-1)
    return (jax.nn.silu(gate) * up) @ w_out


def apply_partial_rope(x, pos):
    inv = ROPE_THETA ** (-jnp.arange(0, ROT_DIM, 2, dtype=jnp.float32) / ROT_DIM)
    ang = pos.astype(jnp.float32)[:, None] * inv[None, :]
    cos = jnp.cos(ang)[None, :, None, :]
    sin = jnp.sin(ang)[None, :, None, :]
    xf = x.astype(jnp.float32)
    x1 = xf[..., :ROT_DIM // 2]
    x2 = xf[..., ROT_DIM // 2:ROT_DIM]
    out = jnp.concatenate([x1 * cos - x2 * sin, x2 * cos + x1 * sin, xf[..., ROT_DIM:]], axis=-1)
    return out.astype(x.dtype)


def causal_dwconv(x, buf, w, b):
    T = x.shape[1]
    xpad = jnp.concatenate([buf.astype(x.dtype), x], axis=1)
    y = b
    for k in range(CONV_WIDTH):
        y = y + w[k] * xpad[:, k:k + T]
    return y, xpad[:, xpad.shape[1] - (CONV_WIDTH - 1):]


def rg_lru(x, gate_w, gate_b, lam, h0):
    B, T, _ = x.shape
    xb = x.reshape(B, T, N_RNN_BLOCKS, RNN_BLOCK)
    g = jnp.einsum('btnc,ncd->btnd', xb, gate_w) + gate_b
    g = jax.nn.sigmoid(g.astype(jnp.float32))
    r = g[..., :RNN_BLOCK].reshape(B, T, D_RNN)
    i = g[..., RNN_BLOCK:].reshape(B, T, D_RNN)
    log_a = RG_C * r * jax.nn.log_sigmoid(lam.astype(jnp.float32))
    a = jnp.exp(log_a)
    u = jnp.sqrt(-jnp.expm1(2.0 * log_a)) * (i * x.astype(jnp.float32))

    def step(h, au):
        h = au[0] * h + au[1]
        return h, h

    h_last, hs = lax.scan(step, h0.astype(jnp.float32),
                          (jnp.swapaxes(a, 0, 1), jnp.swapaxes(u, 0, 1)))
    return jnp.swapaxes(hs, 0, 1).astype(x.dtype), h_last


def recurrent_block(h, w_in, conv_w, conv_b, gate_w, gate_b, lam, w_out, conv_buf, h0):
    xr, yg = jnp.split(h @ w_in, 2, axis=-1)
    xr, new_buf = causal_dwconv(xr, conv_buf, conv_w, conv_b)
    o, h_last = rg_lru(xr, gate_w, gate_b, lam, h0)
    return (o * jax.nn.gelu(yg, approximate=True)) @ w_out, new_buf, h_last


def sliding_sink_attention(q, k_all, v_all, sinks, p0):
    B, T = q.shape[:2]
    qb = WINDOW if T % WINDOW == 0 else T
    nb = T // qb
    span = qb + WINDOW
    idx = (jnp.arange(nb) * qb)[:, None] + jnp.arange(span)[None, :]
    kb = k_all[:, idx]
    vb = v_all[:, idx]
    qg = q.reshape(B, nb, qb, N_KV_HEADS, GROUP, HEAD_DIM)
    s = jnp.einsum('bnqkgd,bnskd->bnkgqs', qg, kb).astype(jnp.float32) * (HEAD_DIM ** -0.5)
    rel = WINDOW + jnp.arange(qb)[:, None] - jnp.arange(span)[None, :]
    mask = ((rel >= 0) & (rel <= WINDOW))[None] & ((p0 - WINDOW + idx) >= 0)[:, None, :]
    s = jnp.where(mask[None, :, None, None], s, NEG_INF)
    sink = sinks.astype(jnp.float32).reshape(1, 1, N_KV_HEADS, GROUP, 1, 1)
    m = jnp.maximum(jnp.max(s, axis=-1, keepdims=True), sink)
    e = jnp.exp(s - m)
    p = e / (jnp.sum(e, axis=-1, keepdims=True) + jnp.exp(sink - m))
    o = jnp.einsum('bnkgqs,bnskd->bnqkgd', p.astype(vb.dtype), vb)
    return o.reshape(B, T, N_HEADS * HEAD_DIM)


def trunk(x, c, p0, conv_bufs, h0s, k_prev, v_prev, params):
    (ada_w, ada_b, norm_g, rnn_w_in, rnn_conv_w, rnn_conv_b, rnn_gate_w, rnn_gate_b,
     rnn_lambda, rnn_w_out, kv_ada_w, kv_ada_b, kv_norm_g, w_kv, attn_w_q, attn_sinks,
     attn_w_o, ffn_w_in, ffn_w_out, final_g) = params
    B, T, _ = x.shape
    pos = p0 + jnp.arange(T, dtype=jnp.int32)
    new_conv, new_h = [], []
    k_all = v_all = None
    for l in range(DEPTH):
        sh1, sc1, g1, sh2, sc2, g2 = ada_mod(c, ada_w[l], ada_b[l], 6)
        h = modulate(rms_norm(x, norm_g[l, 0]), sh1, sc1)
        if l < N_A_LAYERS:
            out, nbuf, h_last = recurrent_block(h, rnn_w_in[l], rnn_conv_w[l], rnn_conv_b[l],
                                                rnn_gate_w[l], rnn_gate_b[l], rnn_lambda[l],
                                                rnn_w_out[l], conv_bufs[l], h0s[l])
            new_conv.append(nbuf)
            new_h.append(h_last)
        else:
            j = l - N_A_LAYERS
            q = apply_partial_rope((h @ attn_w_q[j]).reshape(B, T, N_HEADS, HEAD_DIM), pos)
            out = sliding_sink_attention(q, k_all, v_all, attn_sinks[j], p0) @ attn_w_o[j]
        x = x + g1[:, None, :] * out
        h = modulate(rms_norm(x, norm_g[l, 1]), sh2, sc2)
        x = x + g2[:, None, :] * swiglu(h, ffn_w_in[l], ffn_w_out[l])
        if l == N_A_LAYERS - 1:
            kv_shift, kv_scale = ada_mod(c, kv_ada_w, kv_ada_b, 2)
            hk = modulate(rms_norm(x, kv_norm_g), kv_shift, kv_scale)
            kv = (hk @ w_kv).reshape(B, T, 2, N_KV_HEADS, HEAD_DIM)
            k_new = apply_partial_rope(kv[:, :, 0], pos)
            k_all = jnp.concatenate([k_prev.astype(x.dtype), k_new], axis=1)
            v_all = jnp.concatenate([v_prev.astype(x.dtype), kv[:, :, 1]], axis=1)
    y = rms_norm(x, final_g)
    L = k_all.shape[1]
    return y, jnp.stack(new_conv), jnp.stack(new_h), k_all[:, L - WINDOW:], v_all[:, L - WINDOW:]


def setup_inputs(seed: int = 0) -> dict:
    key = jax.random.key(seed)
    ks = iter(jax.random.split(key, 32))
    f32 = jnp.float32
    D = D_MODEL

    def nrm(shape, scale):
        return jax.random.normal(next(ks), shape, f32) * scale

    u = jax.random.uniform(next(ks), (N_A_LAYERS, D_RNN), f32, 0.9, 0.999)
    a0 = u ** (1.0 / RG_C)
    rnn_lambda = jnp.log(a0) - jnp.log1p(-a0)
    return {
        'x_prompt': nrm((BATCH, SEQ, D), 1.0),
        'x_sample': nrm((DEC_BATCH, DEC_SEQ, D), 1.0),
        'c_prompt': nrm((BATCH, D), 1.0),
        'c_sample': nrm((DEC_BATCH, D), 1.0),
        'state_conv': nrm((N_A_LAYERS, DEC_BATCH, CONV_WIDTH - 1, D_RNN), 1.0),
        'state_h': nrm((N_A_LAYERS, DEC_BATCH, D_RNN), 0.5),
        'cache_k': nrm((DEC_BATCH, WINDOW, N_KV_HEADS, HEAD_DIM), 1.0),
        'cache_v': nrm((DEC_BATCH, WINDOW, N_KV_HEADS, HEAD_DIM), 1.0),
        'ada_w': nrm((DEPTH, D, 6 * D), 0.5 * D ** -0.5),
        'ada_b': nrm((DEPTH, 6 * D), 0.05),
        'norm_g': 1.0 + nrm((DEPTH, 2, D), 0.02),
        'rnn_w_in': nrm((N_A_LAYERS, D, 2 * D_RNN), D ** -0.5),
        'rnn_conv_w': nrm((N_A_LAYERS, CONV_WIDTH, D_RNN), CONV_WIDTH ** -0.5),
        'rnn_conv_b': nrm((N_A_LAYERS, D_RNN), 0.02),
        'rnn_gate_w': nrm((N_A_LAYERS, N_RNN_BLOCKS, RNN_BLOCK, 2 * RNN_BLOCK), RNN_BLOCK ** -0.5),
        'rnn_gate_b': nrm((N_A_LAYERS, N_RNN_BLOCKS, 2 * RNN_BLOCK), 0.1),
        'rnn_lambda': rnn_lambda,
        'rnn_w_out': nrm((N_A_LAYERS, D_RNN, D), D_RNN ** -0.5),
        'kv_ada_w': nrm((D, 2 * D), 0.5 * D ** -0.5),
        'kv_ada_b': nrm((2 * D,), 0.05),
        'kv_norm_g': 1.0 + nrm((D,), 0.02),
        'w_kv': nrm((D, 2 * N_KV_HEADS * HEAD_DIM), D ** -0.5),
        'attn_w_q': nrm((N_B_LAYERS, D, N_HEADS * HEAD_DIM), D ** -0.5),
        'attn_sinks': nrm((N_B_LAYERS, N_HEADS), 1.0),
        'attn_w_o': nrm((N_B_LAYERS, N_HEADS * HEAD_DIM, D), (N_HEADS * HEAD_DIM) ** -0.5),
        'ffn_w_in': nrm((DEPTH, D, 2 * D_FF), D ** -0.5),
        'ffn_w_out': nrm((DEPTH, D_FF, D), D_FF ** -0.5),
        'final_g': 1.0 + nrm((D,), 0.02),
    }


def reference(x_prompt, x_sample, c_prompt, c_sample, state_conv, state_h, cache_k, cache_v,
              ada_w, ada_b, norm_g, rnn_w_in, rnn_conv_w, rnn_conv_b, rnn_gate_w, rnn_gate_b,
              rnn_lambda, rnn_w_out, kv_ada_w, kv_ada_b, kv_norm_g, w_kv, attn_w_q, attn_sinks,
              attn_w_o, ffn_w_in, ffn_w_out, final_g):
    params = (ada_w, ada_b, norm_g, rnn_w_in, rnn_conv_w, rnn_conv_b, rnn_gate_w, rnn_gate_b,
              rnn_lambda, rnn_w_out, kv_ada_w, kv_ada_b, kv_norm_g, w_kv, attn_w_q, attn_sinks,
              attn_w_o, ffn_w_in, ffn_w_out, final_g)
    B = x_prompt.shape[0]
    conv0 = jnp.zeros((N_A_LAYERS, B, CONV_WIDTH - 1, D_RNN), x_prompt.dtype)
    h00 = jnp.zeros((N_A_LAYERS, B, D_RNN), jnp.float32)
    kv0 = jnp.zeros((B, WINDOW, N_KV_HEADS, HEAD_DIM), x_prompt.dtype)
    y_prompt, prompt_conv, prompt_h, prompt_k, prompt_v = trunk(
        x_prompt, c_prompt, 0, conv0, h00, kv0, kv0, params)
    y_sample, sample_conv, sample_h, sample_k, sample_v = trunk(
        x_sample, c_sample, PAST_LEN, state_conv, state_h, cache_k, cache_v, params)
    return (y_prompt, y_sample, prompt_conv, prompt_h, prompt_k, prompt_v,
            sample_conv, sample_h, sample_k, sample_v)
```

```python
import sys
import numpy as np
from contextlib import ExitStack
import concourse.bass as bass
import concourse.mybir as mybir
from concourse.bass_utils import run_bass_kernel_spmd

F32 = mybir.dt.float32
BF16 = mybir.dt.bfloat16
AF = mybir.ActivationFunctionType
ALU = mybir.AluOpType
AX = mybir.AxisListType

ENGS = ['pe', 'act', 'dve', 'pool', 'sp']
LOOKBACK = 3

D = 1024
NCH = 8
TP = 2048
TS = 128
T = TP + TS
DFF = 2816
NHC = 22
EPS = 1e-6
GROUPS = [(0, 512), (512, 512), (1024, 512), (1536, 512), (2048, 128)]
FFN_GROUPS = [(0, 4), (4, 4), (8, 4), (12, 4), (16, 3), (19, 3)]
SV_ADAB = 0
SV_KVB = 96
SV_NG = 112
SV_KVG = 144
SV_FG = 152
SV_CW = 160
SV_CB = 192
SV_GB = 200
SV_LM = 216
SV_ROWS = 224
RING_SLOTS = 13
SLOT = 2048
ARENA_W = 15500
TAIL_W = 4352
DUMP_PATH = None
SCHEDULE = True
STRICT_SAME = True
SAME_LAT = 0.25
PRIO_RANK = True
VERBOSE = False


class Ins:
    __slots__ = ('eng', 'fn', 'deps', 'idx', 'signaled', 'count', 'dma', 'semi', 'dval', 'waits', 'tag', 'cost', 'seq', 'bar', 'start', 'fin', 'nun', 'succ', 'stage')


class Sched:
    def __init__(self, nc, n_dma_sems=56):
        self.nc = nc
        self.n_main = 48
        self.rr_b = 0
        self.streams = {e: [] for e in ENGS}
        self.last_w = {}
        self.readers = {}
        self.n_dma_sems = n_dma_sems
        self.dma_rr = 0
        self.dma_last = [None] * n_dma_sems
        self.dma_cnt = [0] * n_dma_sems
        self.pb_acc = {}
        self.seq = 0
        self.cur_bar = {e: None for e in ENGS}
        self.since_bar = {e: [] for e in ENGS}
        self.n_bar = 0
        self.stage = 'init'

    def _new(self, eng, fn, dma):
        ins = Ins()
        ins.eng = eng; ins.fn = fn; ins.dma = dma; ins.signaled = False; ins.count = 0
        ins.semi = None; ins.dval = 0; ins.waits = None; ins.deps = []
        ins.cost = 0.1; ins.bar = None; ins.start = 0.0; ins.fin = 0.0
        ins.seq = self.seq; self.seq += 1
        ins.stage = self.stage
        ins.tag = 0
        try:
            f = sys._getframe(1)
            while f is not None and f.f_code.co_name in ('_new', 'op', 'mm', 'tr', 'act', 'tt', 'ts', 'stt', 'cp', 'dma', 'wload', 'barrier', 'finish'):
                f = f.f_back
            ins.tag = f.f_lineno if f is not None else 0
        except Exception:
            ins.tag = 0
        return ins

    def op(self, eng, fn, reads=(), writes=(), dma=False, cost=0.1, sem_b=False):
        ins = self._new(eng, fn, dma)
        ins.cost = cost
        deps = {}
        cb = self.cur_bar[eng]
        if cb is not None:
            deps[id(cb)] = (cb, 'BAR')
        for r in reads:
            w = self.last_w.get(r)
            if w is not None:
                deps[id(w)] = (w, 'RAW')
        for r in writes:
            w = self.last_w.get(r)
            if w is not None and id(w) not in deps:
                deps[id(w)] = (w, 'WAW')
            for rd in self.readers.get(r, ()):
                if id(rd) not in deps:
                    deps[id(rd)] = (rd, 'WAR')
        banks = set(r for r in list(reads) + list(writes) if isinstance(r, tuple) and r[0] == 'pb')
        for bnk in banks:
            st = self.pb_acc.get(bnk)
            if st is None:
                st = self.pb_acc[bnk] = {'eng': eng, 'cur': [], 'prev': []}
            if st['eng'] != eng:
                st['prev'] = st['cur']
                st['cur'] = []
                st['eng'] = eng
            for d in st['prev']:
                if id(d) not in deps:
                    deps[id(d)] = (d, 'PB')
            st['cur'].append(ins)
        if dma:
            half = self.n_dma_sems // 2
            if eng == 'pool':
                si = self.rr_b % half
                self.rr_b += 1
            else:
                si = half + (self.dma_rr % half)
                self.dma_rr += 1
            prev = self.dma_last[si]
            if prev is not None and id(prev) not in deps:
                deps[id(prev)] = (prev, 'SEM')
            self.dma_cnt[si] += 1
            ins.semi = si
            ins.dval = 16 * self.dma_cnt[si]
            self.dma_last[si] = ins
        ins.deps = list(deps.values())
        for r in reads:
            lst = self.readers.setdefault(r, [])
            lst.append(ins)
        for r in writes:
            self.last_w[r] = ins
            self.readers[r] = []
        ins.idx = len(self.streams[eng])
        self.streams[eng].append(ins)
        self.since_bar[eng].append(ins)
        return ins

    def barrier(self, engines=('pe', 'act', 'dve', 'sp')):
        pend = [d for d in self.dma_last if d is not None]
        prior = []
        for e in ENGS:
            prior += [x for x in self.since_bar[e] if x.fn is not None]
        self.n_bar += 1
        for e in engines:
            ins = self._new(e, None, False)
            ins.cost = 0.0
            ins.bar = self.n_bar
            deps = {}
            for d in prior:
                deps[id(d)] = (d, 'BAR')
            for d in pend:
                deps[id(d)] = (d, 'RAW')
            cb = self.cur_bar[e]
            if cb is not None:
                deps[id(cb)] = (cb, 'BAR')
            ins.deps = list(deps.values())
            ins.idx = len(self.streams[e])
            self.streams[e].append(ins)
            self.cur_bar[e] = ins
        for e in ENGS:
            if e in engines:
                self.since_bar[e] = []

    def finish(self):
        pend = [d for d in self.dma_last if d is not None]
        ins = self._new('sp', None, False)
        ins.cost = 0.0
        ins.deps = [(d, 'RAW') for d in pend] + [(d, 'BAR') for d in self.streams['sp'] if d is not ins]
        ins.idx = len(self.streams['sp'])
        self.streams['sp'].append(ins)

    def schedule(self):
        import heapq
        allins = []
        for e in ENGS:
            allins += self.streams[e]
        for ins in allins:
            ins.succ = []
            ins.nun = 0
        for ins in allins:
            seen = set()
            for d, kind in ins.deps:
                if id(d) in seen:
                    continue
                seen.add(id(d))
                d.succ.append(ins)
                ins.nun += 1
        prio = {}
        if PRIO_RANK:
            order_seq = sorted(allins, key=lambda x: -x.seq)
            rank = {}
            for ins in order_seq:
                r = 0.0
                for sc in ins.succ:
                    rs_ = rank[id(sc)]
                    if rs_ > r:
                        r = rs_
                rank[id(ins)] = r + ins.cost + (0.25 if not ins.dma else 2.0)
            for ins in allins:
                prio[id(ins)] = -rank[id(ins)]
        else:
            for ins in allins:
                prio[id(ins)] = ins.seq
        avail = {e: [] for e in ENGS}
        ready_t = {}
        for ins in allins:
            if ins.nun == 0:
                heapq.heappush(avail[ins.eng], (prio[id(ins)], id(ins), ins))
                ready_t[id(ins)] = 0.0
        free = {e: 0.0 for e in ENGS}
        dma_pipe = [0.0]
        order = {e: [] for e in ENGS}
        remaining = len(allins)
        WIN = 48
        while remaining:
            best = None
            for e in ENGS:
                h = avail[e]
                if not h:
                    continue
                cands = heapq.nsmallest(WIN, h)
                pick = None
                for c in cands:
                    if ready_t[id(c[2])] <= free[e] + 1e-9:
                        pick = c
                        break
                if pick is None:
                    pick = min(cands, key=lambda c: (ready_t[id(c[2])], c[0]))
                st = max(free[e], ready_t[id(pick[2])])
                if best is None or st < best[0] or (st == best[0] and pick[0] < best[1][0]):
                    best = (st, pick, e)
            st, pick, e = best
            ins = pick[2]
            avail[e].remove(pick)
            heapq.heapify(avail[e])
            ins.start = st
            if ins.dma:
                free[e] = st + 0.06
                t0 = max(st + 1.8, dma_pipe[0])
                ins.fin = t0 + ins.cost
                dma_pipe[0] = ins.fin
            else:
                ins.fin = st + ins.cost
                free[e] = ins.fin
            order[e].append(ins)
            remaining -= 1
            for sc in ins.succ:
                sc.nun -= 1
                if sc.eng == ins.eng and not ins.dma:
                    lat = SAME_LAT if ins.eng in ('act', 'dve') else 0.0
                else:
                    lat = 0.40
                rt = max(ready_t.get(id(sc), 0.0), ins.fin + lat)
                ready_t[id(sc)] = rt
                if sc.nun == 0:
                    heapq.heappush(avail[sc.eng], (prio[id(sc)], id(sc), sc))
        for e in ENGS:
            self.streams[e] = order[e]
            for i, ins in enumerate(order[e]):
                ins.idx = i
        self.sim_time = max(free.values())
        print("[sched] simulated time (us): %.1f" % self.sim_time, {e: len(order[e]) for e in ENGS})
        if VERBOSE:
            st = {}
            for e in ENGS:
                for ins in order[e]:
                    d = st.setdefault(ins.stage, {'t0': 1e18, 't1': 0.0, 'busy': {x: 0.0 for x in ENGS}})
                    d['t0'] = min(d['t0'], ins.start); d['t1'] = max(d['t1'], ins.fin)
                    if not ins.dma:
                        d['busy'][e] += ins.cost
            for k, d in sorted(st.items(), key=lambda kv: kv[1]['t0']):
                print("  %-10s t0=%7.1f t1=%7.1f span=%7.1f  busy pe=%6.1f act=%6.1f dve=%6.1f" % (k, d['t0'], d['t1'], d['t1'] - d['t0'], d['busy']['pe'], d['busy']['act'], d['busy']['dve']))

    def plan(self):
        for eng in ENGS:
            known_eng = {e: -1 for e in ENGS}
            known_dma = {}
            for ins in self.streams[eng]:
                waits = []
                tgt = {}
                for d, kind in ins.deps:
                    if d.dma:
                        if known_dma.get(d.semi, 0) >= d.dval:
                            continue
                        known_dma[d.semi] = d.dval
                        waits.append(d)
                    elif d.fn is None:
                        continue
                    elif d.eng == eng:
                        if eng == 'pe':
                            continue
                        if (kind == 'RAW' and ins.idx - d.idx <= LOOKBACK) or STRICT_SAME:
                            if d.idx > tgt.get(eng, (-1, None))[0]:
                                tgt[eng] = (d.idx, d)
                    else:
                        if d.idx > tgt.get(d.eng, (-1, None))[0]:
                            tgt[d.eng] = (d.idx, d)
                for e2, (ix, d) in tgt.items():
                    if known_eng[e2] >= ix:
                        continue
                    known_eng[e2] = ix
                    waits.append(d)
                for d in waits:
                    if not d.dma:
                        d.signaled = True
                ins.waits = waits
        for eng in ENGS:
            c = 0
            for ins in self.streams[eng]:
                if ins.dma:
                    continue
                if ins.fn is None:
                    ins.count = c
                    continue
                if ins.signaled:
                    c += 1
                    ins.count = c

    def dump(self, path):
        with open(path, 'w') as f:
            for eng in ENGS:
                f.write('==== %s\n' % eng)
                for ins in self.streams[eng]:
                    w = ['%s:%s' % (('dma%d' % d.semi) if d.dma else d.eng, d.dval if d.dma else '%d(c%d,L%d)' % (d.idx, d.count, d.tag)) for d in ins.waits]
                    f.write('%5d L%-4d %s%s sig=%d cnt=%d waits=%s\n' % (ins.idx, ins.tag, 'DMA(s%d,v%d) ' % (ins.semi, ins.dval) if ins.dma else '', 'NOP' if ins.fn is None else '', ins.signaled, ins.count, w))

    def emit(self):
        nc = self.nc
        if SCHEDULE:
            self.schedule()
        self.plan()
        if DUMP_PATH:
            self.dump(DUMP_PATH)
        with ExitStack() as es:
            esem = {e: es.enter_context(nc.semaphore("s_" + e)) for e in ENGS}
            dsem = [es.enter_context(nc.semaphore("d_%d" % i)) for i in range(self.n_dma_sems)]
            block = es.enter_context(nc.Block())

            def run(eng_name):
                def body(e):
                    for ins in self.streams[eng_name]:
                        for d in ins.waits:
                            if d.dma:
                                e.wait_ge(dsem[d.semi], d.dval)
                            else:
                                e.wait_ge(esem[d.eng], d.count)
                        if ins.fn is None:
                            continue
                        bi = ins.fn(e)
                        if ins.dma:
                            bi.then_inc(dsem[ins.semi], 16)
                        elif ins.signaled:
                            bi.then_inc(esem[eng_name], 1)
                return body

            block.tensor(run('pe'))
            block.scalar(run('act'))
            block.vector(run('dve'))
            block.gpsimd(run('pool'))
            block.sync(run('sp'))


class _Stop(Exception):
    pass


def build_program(stop=99, dbg=False):
    try:
        return _build(stop, dbg)
    except _Stop as e:
        return e.args[0]


def _build(stop, dbg):
    nc = bass.Bass("TRN2", target_bir_lowering=False)
    S = Sched(nc)
    din = lambda n, shp: nc.dram_tensor(n, list(shp), F32, kind="ExternalInput").ap()
    dout = lambda n, shp: nc.dram_tensor(n, list(shp), F32, kind="ExternalOutput").ap()
    dint = lambda n, shp: nc.dram_tensor(n, list(shp), F32, kind="Internal").ap()
    xp = din("xp", [TP, D]); xs = din("xs", [TS, D]); cc = din("cc", [17, D])
    sconv = din("sconv", [48, D]); shin = din("shin", [16, D])
    ck = din("ck", [16, 128, 256]); cv = din("cv", [16, 128, 256])
    adaP = din("adaP", [2, 6, 128, 8 * 1024]); rnn_w_in = din("rnn_w_in", [128, 8 * 2048]); gate_w = din("gate_w", [128, 8 * 256])
    rnn_w_out = din("rnn_w_out", [128, 8 * 1024]); kvadaP = din("kvadaP", [2, 128, 8 * 1024]); w_kv = din("w_kv", [128, 8 * 512])
    w_q = din("w_q", [128, 8 * 1024]); w_o = din("w_o", [128, 8 * 1024]); ffnP = din("ffnP", [2, 128, NHC * 3072])
    svd = din("sv", [SV_ROWS, 128]); identd = din("ident", [128, 128])
    cosd = din("cos", [128, 17, 8]); sind = din("sin", [128, 17, 8])
    maskpd = din("maskp", [128, 256]); masksd = din("masks", [128, 136])
    sinkbd = din("sinkb", [128, 16]); sinkcd = din("sinkc", [128, 1])
    y_p = dout("y_p", [TP, D]); y_s = dout("y_s", [TS, D]); pconv = dout("pconv", [3, D]); ph = dout("ph", [1, D])
    pk = dout("pk", [128, 256]); pv = dout("pv", [128, 256]); sconv_o = dout("sconv_o", [48, D]); sh_o = dout("sh_o", [16, D])
    sk = dout("sk", [16, 128, 256]); svo = dout("svo", [16, 128, 256])
    scr_v = dint("scr_v", [128, 256]); scr_o = dint("scr_o", [128, 1024])

    with ExitStack() as es:
        sb = lambda n, shp, dt=F32: es.enter_context(nc.sbuf_tensor(n, list(shp), dt))
        xT = sb("xT", [128, NCH, T])
        ring = sb("ring", [128, RING_SLOTS * SLOT], BF16)
        svT = sb("svT", [128, SV_ROWS]); ident = sb("ident_sb", [128, 128])
        ones = sb("ones", [128, 128], BF16)
        modT = sb("modT", [128, 48, 17]); kvmodT = sb("kvmodT", [128, 16, 17])
        cTc = sb("cTc", [128, NCH, 17], BF16)
        cost = sb("cost", [128, 17, 8]); sint = sb("sint", [128, 17, 8])
        maskp = sb("maskp_sb", [128, 256]); masks = sb("masks_sb", [128, 136])
        sinkb = sb("sinkb_sb", [128, 16]); sinkc = sb("sinkc_sb", [128, 1])
        rgc = sb("rgc", [128, 40])
        hstate = sb("hstate", [128, 8])
        arena = sb("arena", [128, ARENA_W + TAIL_W])
        pbs = [es.enter_context(nc.psum_tensor("pb%d" % i, [128, 512], F32)) for i in range(8)]

        class Arena:
            def __init__(self):
                self.off = 0
                self.limit = ARENA_W + TAIL_W
            def reset(self):
                self.off = 0
            def f32(self, shape):
                n = int(np.prod(shape[1:]))
                ap = arena[0:shape[0], self.off:self.off + n]
                self.off += n
                assert self.off <= self.limit, (self.off, self.limit)
                return _view(ap, shape)
            def bf16(self, shape):
                n = int(np.prod(shape[1:]))
                nw = (n + 1) // 2
                ap = arena[0:shape[0], self.off:self.off + nw].bitcast(BF16)[:, 0:n]
                self.off += nw
                assert self.off <= self.limit, (self.off, self.limit)
                return _view(ap, shape)

        def _view(ap, shape):
            if len(shape) == 2:
                return ap
            if len(shape) == 3:
                return ap.rearrange("p (a b) -> p a b", b=shape[2])
            if len(shape) == 4:
                return ap.rearrange("p (a b c) -> p a b c", b=shape[2], c=shape[3])
            raise ValueError

        AR = Arena()
        KT = arena[:, ARENA_W:ARENA_W + 2176].bitcast(BF16).rearrange("p (a b) -> p a b", b=T)
        Vb = arena[:, ARENA_W + 2176:ARENA_W + 4352].bitcast(BF16).rearrange("p (a b) -> p a b", b=256)

        def fsz(ap):
            n = 1
            for d in ap.shape[1:]:
                n *= int(d)
            return n

        def c_act(out):
            return 0.22 + fsz(out) / 1200.0

        def c_dve(out):
            return 0.08 + fsz(out) / 960.0

        def mm(out, lhsT, rhs, start, stop, r, w, **kw):
            f = 4.0 if rhs.dtype == F32 else 1.0
            cost = f * max(fsz(rhs), 64) / 2100.0 + 0.015
            S.op('pe', lambda e: e.matmul(out, lhsT, rhs, start=start, stop=stop, **kw), reads=r, writes=w, cost=cost)

        def tr(out, in_, r, w):
            n = in_.shape[0]
            cost = 0.12
            S.op('pe', lambda e: e.transpose(out, in_, ident[0:n, 0:n]), reads=list(r) + ['ident'], writes=w, cost=cost)

        def act(out, in_, func, r, w, bias=None, scale=None, accum=None):
            kw = {}
            if bias is not None: kw['bias'] = bias
            if scale is not None: kw['scale'] = scale
            if accum is not None: kw['accum_out'] = accum
            S.op('act', lambda e: e.activation(out, in_, func, **kw), reads=r, writes=w, cost=c_act(out))

        def tt(out, in0, in1, op, r, w):
            S.op('dve', lambda e: e.tensor_tensor(out, in0, in1, op), reads=r, writes=w, cost=c_dve(out))

        def ts(out, in0, s1, s2, op0, op1, r, w):
            if s2 is None:
                S.op('dve', lambda e: e.tensor_scalar(out, in0, s1, None, op0), reads=r, writes=w, cost=c_dve(out))
            else:
                S.op('dve', lambda e: e.tensor_scalar(out, in0, s1, s2, op0, op1), reads=r, writes=w, cost=c_dve(out))

        def stt(out, in0, scalar, in1, op0, op1, r, w):
            S.op('dve', lambda e: e.scalar_tensor_tensor(out, in0, scalar, in1, op0, op1), reads=r, writes=w, cost=1.2 * c_dve(out))

        def cp(eng, out, in_, r, w):
            if eng == 'act':
                S.op('act', lambda e: e.copy(out, in_), reads=r, writes=w, cost=c_act(out))
            else:
                S.op('dve', lambda e: e.tensor_copy(out, in_), reads=r, writes=w, cost=c_dve(out))

        def dma(q, out, in_, r, w, sem_b=False):
            nbytes = 4.0 * max(fsz(out), fsz(in_)) * min(int(out.shape[0]), 128)
            S.op(q, lambda e: e.dma_start(out=out, in_=in_), reads=r, writes=w, dma=True, cost=nbytes / 300e3, sem_b=sem_b)

        pb_rr = [0]
        pb_pool = [(0, 1, 2, 3, 4, 5, 6, 7)]
        def pbank():
            pool = pb_pool[0]
            i = pool[pb_rr[0] % len(pool)]
            pb_rr[0] += 1
            return i

        ring_pos = [0]
        def wload(dram_ap, shape, slot0=None):
            n = int(np.prod(shape[1:]))
            ns = (n + SLOT - 1) // SLOT
            if slot0 is not None:
                ring_pos[0] = slot0
            if ring_pos[0] + ns > RING_SLOTS:
                ring_pos[0] = 0
            s0 = ring_pos[0]
            ring_pos[0] += ns
            view = _view(ring[:, s0 * SLOT: s0 * SLOT + n], shape)
            res = [('ws', i) for i in range(s0, s0 + ns)]
            dma('pool', view, dram_ap, [], res)
            return view, res

        def ckpt(x):
            if stop <= x:
                raise _Stop(_finish())

        def _finish():
            if dbg:
                S.barrier()
                dbg_x = nc.dram_tensor("dbg_x", [128, NCH * T], F32, kind="ExternalOutput").ap()
                dma('sp', dbg_x, xT[:, :, :].rearrange("p c t -> p (c t)"), [], [])
                dbg_m = nc.dram_tensor("dbg_m", [128, 48 * 17], F32, kind="ExternalOutput").ap()
                dma('sp', dbg_m, modT[:, :, :].rearrange("p c t -> p (c t)"), [], [])
            S.finish()
            S.emit()
            return nc

        dma('sp', ident[:], identd, [], ['ident'])
        dma('sp', cost[:], cosd, [], ['rope']); dma('sp', sint[:], sind, [], ['rope'])
        dma('sp', maskp[:], maskpd, [], ['mask']); dma('sp', masks[:], masksd, [], ['mask'])
        dma('sp', sinkb[:], sinkbd, [], ['sink']); dma('sp', sinkc[:], sinkcd, [], ['sink'])
        S.op('dve', lambda e: e.memset(ones[:], 1.0), writes=['ones'])
        S.op('dve', lambda e: e.memset(hstate[:], 0.0), writes=['hstate'])
        AR.reset()
        sva = AR.f32([112, 128]); svb = AR.f32([112, 128]); cin = AR.f32([17, 1024]); csl = AR.f32([17, 1024])
        xin = [AR.f32([128, 1024]), AR.f32([128, 1024])]
        dma('sp', sva, svd[0:112, :], [], ['sva']); dma('sp', svb, svd[112:224, :], [], ['svb'])
        dma('sp', cin, cc, [], ['cin'])
        b0 = pbank()
        tr(pbs[b0][:, 0:112], sva, ['sva'], [('pb', b0)])
        tr(pbs[b0][:, 112:224], svb, ['svb'], [('pb', b0)])
        cp('dve', svT[:, :], pbs[b0][:, 0:224], [('pb', b0)], ['svT'])
        act(csl, cin, AF.Silu, ['cin'], ['csl'])
        b0 = pbank()
        for k in range(NCH):
            tr(pbs[b0][:, k * 17:(k + 1) * 17], csl[:, k * 128:(k + 1) * 128], ['csl'], [('pb', b0)])
        cp('dve', cTc[:, :, :], pbs[b0][:, 0:136].rearrange("p (k n) -> p k n", n=17), [('pb', b0)], ['cTc'])
        for t in range(17):
            xi = xin[t % 2]
            src = xp[t * 128:(t + 1) * 128, :] if t < 16 else xs
            dma('sp', xi, src, [], [('xin', t % 2)])
            for hf in range(2):
                b0 = pbank()
                for j in range(4):
                    c = hf * 4 + j
                    tr(pbs[b0][:, j * 128:(j + 1) * 128], xi[:, c * 128:(c + 1) * 128], [('xin', t % 2)], [('pb', b0)])
                cp('act' if hf == 0 else 'dve', xT[:, hf * 4:hf * 4 + 4, t * 128:(t + 1) * 128],
                   pbs[b0][:, :].rearrange("p (j n) -> p j n", n=128), [('pb', b0)],
                   [('xT', c, min(t // 4, 4)) for c in range(hf * 4, hf * 4 + 4)])
        act(rgc[:, 32:40], svT[:, SV_LM:SV_LM + 8], AF.Exp, ['svT'], ['rgc_t'], scale=-1.0)
        act(rgc[:, 32:40], rgc[:, 32:40], AF.Ln, ['rgc_t'], ['rgc_t'], bias=1.0)
        act(rgc[:, 0:8], rgc[:, 32:40], AF.Identity, ['rgc_t'], ['rgc'], scale=-8.0)
        act(rgc[:, 8:16], rgc[:, 32:40], AF.Identity, ['rgc_t'], ['rgc'], scale=-16.0)
        act(rgc[:, 16:32], svT[:, SV_GB:SV_GB + 16], AF.Identity, ['svT'], ['rgc'], scale=-1.0)

        def compute_mod(w_dram, vlist, dst, bias_row0, resfn, slot0=None):
            for v in vlist:
                bk = pbank()
                wv, wres = wload(w_dram[v].rearrange("p (k n) -> p k n", k=NCH), [128, NCH, 1024], slot0=slot0)
                for c in range(NCH):
                    o = pbs[bk][:, c * 17:(c + 1) * 17]
                    for k in range(NCH):
                        mm(o, wv[:, k, c * 128:(c + 1) * 128], cTc[:, k, :], k == 0, k == NCH - 1, wres + ['cTc'], [('pb', bk)])
                tt(dst[:, v * 8:(v + 1) * 8, :], pbs[bk][:, 0:136].rearrange("p (a b) -> p a b", b=17),
                   svT[:, bias_row0 + v * 8: bias_row0 + v * 8 + 8].unsqueeze(2).to_broadcast([128, 8, 17]), ALU.add,
                   [('pb', bk), 'svT'], [resfn(v)])

        def fold_scale(dst, sc0, g_row0, res):
            stt(dst[:, sc0:sc0 + 8, :], dst[:, sc0:sc0 + 8, :], 1.0,
                svT[:, g_row0:g_row0 + 8].unsqueeze(2).to_broadcast([128, 8, 17]), ALU.add, ALU.mult, [res, 'svT'], [res])

        def compute_mod_tail(w_dram, nvec, dst, bias_row0, res, tbs, tmp, tmpres):
            for v in range(nvec):
                bk = pbank()
                for half in range(2):
                    tb = tbs[half]; tres = ('tailb', half)
                    dma('pool', tb, w_dram[v][:, half * 4096:(half + 1) * 4096].rearrange("p (k n) -> p k n", k=4), [('hTf', 0, 0)], [tres])
                    for c in range(NCH):
                        o = pbs[bk][:, half * 136 + c * 17:half * 136 + (c + 1) * 17]
                        for k in range(4):
                            mm(o, tb[:, k, c * 128:(c + 1) * 128], cTc[:, half * 4 + k, :], k == 0, k == 3, [tres, 'cTc'], [('pb', bk)])
                tt(tmp, pbs[bk][:, 0:136].rearrange("p (a b) -> p a b", b=17),
                   svT[:, bias_row0 + v * 8: bias_row0 + v * 8 + 8].unsqueeze(2).to_broadcast([128, 8, 17]), ALU.add, [('pb', bk), 'svT'], [tmpres])
                tt(dst[:, v * 8:(v + 1) * 8, :], tmp, pbs[bk][:, 136:272].rearrange("p (a b) -> p a b", b=17), ALU.add, [('pb', bk), tmpres], [res])

        def bc_s(v):
            return v.unsqueeze(2).to_broadcast([128, 16, 8])

        def v3(ap):
            return ap.rearrange("p (s t) -> p s t", t=8)

        def norm_group(g, A, B, modres, hdst, hres, X, out_f32_scale=None, bank=None):
            t0, n = GROUPS[g]
            mr = list(modres) if isinstance(modres, list) else [modres]
            bk = pbank() if bank is None else bank
            for c in range(NCH):
                q = X[c % 2][:, :].bitcast(BF16); qr = 'X%d' % (c % 2)
                act(q[:, 0:n], xT[:, c, t0:t0 + n], AF.Square, [('xT', c, g)], [qr])
                mm(pbs[bk][:, 0:n], ones[:, :], q[:, 0:n], c == 0, c == NCH - 1, ['ones', qr], [('pb', bk)])
            rstd = X[2]
            act(rstd[:, 0:n], pbs[bk][:, 0:n], AF.Ln, [('pb', bk)], ['X2'], scale=1.0 / D, bias=EPS)
            act(rstd[:, 0:n], rstd[:, 0:n], AF.Exp, ['X2'], ['X2'], scale=-0.5)
            for c in range(NCH):
                tb = X[c % 2]; tr_ = 'X%d' % (c % 2)
                tt(tb[:, 0:n], xT[:, c, t0:t0 + n], rstd[:, 0:n], ALU.mult, [('xT', c, g), 'X2'], [tr_])
                if out_f32_scale is not None:
                    act(hdst[:, c, 0:n], tb[:, 0:n], AF.Identity, [tr_, 'svT'], [hres(c)], scale=out_f32_scale(c))
                elif g < 4:
                    act(hdst[:, c, 0:n], tb[:, 0:n], AF.Identity, [tr_] + mr, [hres(c)],
                        scale=A[:, c, 0:1], bias=B[:, c, 0:1])
                else:
                    tt(v3(tb[:, 0:n]), v3(tb[:, 0:n]), bc_s(A[:, c, 1:17]), ALU.mult, [tr_] + mr, [tr_])
                    tt(v3(hdst[:, c, 0:n]), v3(tb[:, 0:n]), bc_s(B[:, c, 1:17]), ALU.add, [tr_] + mr, [hres(c)])

        def residual(g, fc, bk, Gm, modres, tmp, tmpres):
            t0, n = GROUPS[g]
            mr = list(modres) if isinstance(modres, list) else [modres]
            if g < 4:
                stt(xT[:, fc, t0:t0 + n], pbs[bk][:, 0:n], Gm[:, fc, 0:1], xT[:, fc, t0:t0 + n], ALU.mult, ALU.add,
                    [('pb', bk), ('xT', fc, g)] + mr, [('xT', fc, g)])
            else:
                tt(v3(tmp[:, 0:n]), v3(pbs[bk][:, 0:n]), bc_s(Gm[:, fc, 1:17]), ALU.mult, [('pb', bk)] + mr, [tmpres])
                tt(xT[:, fc, t0:t0 + n], xT[:, fc, t0:t0 + n], tmp[:, 0:n], ALU.add, [tmpres, ('xT', fc, g)], [('xT', fc, g)])

        def ffn_stage(l):
            S.barrier()
            AR.reset()
            hT = AR.bf16([128, NCH, T])
            hid = [AR.bf16([128, 4, 512]), AR.bf16([128, 4, 512])]
            X = [AR.f32([128, 512]), AR.f32([128, 512]), AR.f32([128, 512])]
            A = modT[:, 32:40, :]; B = modT[:, 24:32, :]; Gm = modT[:, 40:48, :]
            for g in range(5):
                t0, n = GROUPS[g]
                norm_group(g, A, B, [('mod', 3), ('mod', 4)], hT[:, :, t0:t0 + n], (lambda g: lambda c: ('hTf', c, g))(g), X)
            if l == 0:
                tbs = [AR.bf16([128, 4, 1024]), AR.bf16([128, 4, 1024])]
                modT1 = AR.f32([128, 48, 17]); mtmp = AR.f32([128, 8, 17])
                compute_mod_tail(kvadaP, 2, kvmodT, SV_KVB, 'kvmod', tbs, mtmp, 'mtmp')
                fold_scale(kvmodT, 8, SV_KVG, 'kvmod')
                compute_mod_tail(adaP[1], 6, modT1, SV_ADAB + 48, 'modT1', tbs, mtmp, 'mtmp')
                fold_scale(modT1, 8, SV_NG + 16, 'modT1')
                fold_scale(modT1, 32, SV_NG + 24, 'modT1')
            it = 0
            for (h0, hc) in FFN_GROUPS:
                o0 = h0 * 3072
                wg, rg_ = wload(ffnP[l][:, o0:o0 + hc * 1024].rearrange("p (k n) -> p k n", k=NCH), [128, NCH, hc * 128])
                wu, ru_ = wload(ffnP[l][:, o0 + hc * 1024:o0 + hc * 2048].rearrange("p (k n) -> p k n", k=NCH), [128, NCH, hc * 128])
                wo, ro_ = wload(ffnP[l][:, o0 + hc * 2048:o0 + hc * 3072].rearrange("p (j n) -> p j n", j=hc), [128, hc, 1024])
                for g in range(5):
                    t0, n = GROUPS[g]
                    hb = hid[it % 2]; hres = ('hid', it % 2); it += 1
                    for j in range(hc):
                        bg = pbank(); bu = pbank()
                        for k in range(NCH):
                            mm(pbs[bg][:, 0:n], wg[:, k, j * 128:(j + 1) * 128], hT[:, k, t0:t0 + n], k == 0, k == NCH - 1,
                               rg_ + [('hTf', k, g)], [('pb', bg)])
                        for k in range(NCH):
                            mm(pbs[bu][:, 0:n], wu[:, k, j * 128:(j + 1) * 128], hT[:, k, t0:t0 + n], k == 0, k == NCH - 1,
                               ru_ + [('hTf', k, g)], [('pb', bu)])
                        s_ = X[j % 2]; sr = 'X%d' % (j % 2)
                        act(s_[:, 0:n], pbs[bg][:, 0:n], AF.Silu, [('pb', bg)], [sr])
                        tt(hb[:, j, 0:n], s_[:, 0:n], pbs[bu][:, 0:n], ALU.mult, [sr, ('pb', bu)], [hres])
                    for fc in range(NCH):
                        bo = pbank()
                        for j in range(hc):
                            mm(pbs[bo][:, 0:n], wo[:, j, fc * 128:(fc + 1) * 128], hb[:, j, 0:n], j == 0, j == hc - 1,
                               ro_ + [hres], [('pb', bo)])
                        residual(g, fc, bo, Gm, [('mod', 5)], X[2], 'X2')
            if l == 0:
                cp('dve', modT[:, :, :], modT1, ['modT1'], [('mod', v) for v in range(6)])

        if stop <= 0:
            return _finish()
        S.stage = 'mod0'
        mres = lambda v: ('mod', v)
        compute_mod(adaP[0], [0, 1, 2], modT, SV_ADAB, mres)
        fold_scale(modT, 8, SV_NG + 0, ('mod', 1))
        ckpt(0.2)
        S.barrier()
        AR.reset()
        hT2 = [AR.bf16([128, NCH, 512]), AR.bf16([128, NCH, 512])]
        xpad = AR.f32([128, NCH, 515])
        xpads = xpad[:, :, 256:432].rearrange("p c (s j) -> p c s j", j=11)
        oT = AR.bf16([128, NCH, 512])
        offX = AR.off
        X = [AR.f32([128, 512]), AR.f32([128, 512]), AR.f32([128, 512])]
        stio2 = arena[0:48, offX:offX + 1024]
        xc2 = [AR.f32([128, 512]), AR.f32([128, 512])]; xcb2 = [AR.bf16([128, 512]), AR.bf16([128, 512])]
        ta2 = [AR.f32([128, 512]), AR.f32([128, 512])]; tb2 = [AR.f32([128, 512]), AR.f32([128, 512])]
        ti2 = [AR.f32([128, 512]), AR.f32([128, 512])]; gq2 = [AR.f32([128, 512]), AR.f32([128, 512])]
        blk2 = [AR.f32([128, 1024]), AR.f32([128, 1024])]
        stio = blk2[0][0:48, :]; ti = ti2[0]
        h0T = AR.f32([128, NCH, 16]); tcc = AR.f32([128, 48])
        S.op('dve', lambda e: e.memset(xpad[:, :, 0:3], 0.0), writes=[('xpad', c) for c in range(NCH)])
        dma('sp', stio[0:16, :], shin, [], ['stio'])
        b0 = pbank()
        for c in range(NCH):
            tr(pbs[b0][:, c * 16:(c + 1) * 16], stio[0:16, c * 128:(c + 1) * 128], ['stio'], [('pb', b0)])
        cp('dve', h0T[:, :, :], pbs[b0][:, 0:128].rearrange("p (c s) -> p c s", s=16), [('pb', b0)], ['h0T'])
        S.barrier()

        ckpt(0.3)
        S.stage = 'rglru'
        win, rwin = wload(rnn_w_in.rearrange("p (k n) -> p k n", k=NCH), [128, NCH, 2048], slot0=0)
        wgt, rwgt = wload(gate_w.rearrange("p (k n) -> p k n", k=8), [128, 8, 256], slot0=8)
        compute_mod(adaP[0], [3, 4, 5], modT, SV_ADAB, mres, slot0=9)
        fold_scale(modT, 32, SV_NG + 8, ('mod', 4))
        wout, rwout = wload(rnn_w_out.rearrange("p (k n) -> p k n", k=NCH), [128, NCH, 1024], slot0=9)
        A1 = modT[:, 8:16, :]; B1 = modT[:, 0:8, :]; G1 = modT[:, 16:24, :]
        for g in range(5):
            t0, n = GROUPS[g]
            hT = hT2[g % 2]
            pb_pool[0] = (0, 1, 2, 3, 4, 5, 6)
            norm_group(g, A1, B1, [('mod', 0), ('mod', 1)], hT, (lambda gp: lambda c: ('hT', gp, c))(g % 2), X, bank=7)
            ckpt(0.4 + g * 0.1)
            if g == 4:
                dma('sp', stio2, sconv, [], ['X0', 'X1'])
                b0 = pbank()
                for c in range(NCH):
                    tr(pbs[b0][:, c * 48:(c + 1) * 48], stio2[:, c * 128:(c + 1) * 128], ['X0', 'X1'], [('pb', b0)])
                cp('dve', xpads[:, :, :, 0:3], pbs[b0][:, 0:384].rearrange("p (c s j) -> p c s j", s=16, j=3), [('pb', b0)],
                   [('xpad', c) for c in range(NCH)])
            for c in range(NCH):
                ckpt(0.4 + g * 0.1 + 0.01 * (c + 1))
                pz = c % 2
                xc = xc2[pz]; xcb = xcb2[pz]; ta = ta2[pz]; tb_ = tb2[pz]; ti = ti2[pz]; gq = gq2[pz]
                uu = blk2[pz][:, 0:512]; hh = blk2[pz][:, 512:1024]
                Rxc = 'xc%d' % pz; Rxcb = 'xcb%d' % pz; Rta = 'ta%d' % pz; Rtb = 'tb%d' % pz; Rti = 'ti%d' % pz
                Rgq = 'gq%d' % pz; Ruu = 'uu%d' % pz; Rhh = 'hh%d' % pz
                bx = pbank()
                for k in range(NCH):
                    mm(pbs[bx][:, 0:n], win[:, k, c * 128:(c + 1) * 128], hT[:, k, 0:n], k == 0, k == NCH - 1, rwin + [('hT', g % 2, k)], [('pb', bx)])
                w_ = (lambda c: lambda kk: svT[:, SV_CW + kk * 8 + c: SV_CW + kk * 8 + c + 1])(c)
                cb_ = svT[:, SV_CB + c:SV_CB + c + 1]
                if g < 4:
                    cp('dve', xpad[:, c, 3:3 + n], pbs[bx][:, 0:n], [('pb', bx)], [('xpad', c)])
                    act(xc[:, 0:n], pbs[bx][:, 0:n], AF.Identity, [('pb', bx), 'svT'], [Rxc], scale=w_(3), bias=cb_)
                    for kk in range(3):
                        stt(xc[:, 0:n], xpad[:, c, kk:kk + n], w_(kk), xc[:, 0:n], ALU.mult, ALU.add, [('xpad', c), Rxc, 'svT'], [Rxc])
                    cp('dve', tcc[:, 0:3], xpad[:, c, n:n + 3], [('xpad', c)], ['tcc'])
                    cp('dve', xpad[:, c, 0:3], tcc[:, 0:3], ['tcc'], [('xpad', c)])
                else:
                    cp('dve', xpads[:, c, :, 3:11], v3(pbs[bx][:, 0:n]), [('pb', bx)], [('xpad', c)])
                    act(xc[:, 0:n], pbs[bx][:, 0:n], AF.Identity, [('pb', bx), 'svT'], [Rxc], scale=w_(3), bias=cb_)
                    for kk in range(3):
                        stt(v3(xc[:, 0:n]), xpads[:, c, :, kk:kk + 8], w_(kk), v3(xc[:, 0:n]), ALU.mult, ALU.add, [('xpad', c), Rxc, 'svT'], [Rxc])
                cp('dve', xcb[:, 0:n], xc[:, 0:n], [Rxc], [Rxcb])
                ckpt(0.411)
                br = pbank(); bi = pbank()
                mm(pbs[br][:, 0:n], wgt[:, c, 0:128], xcb[:, 0:n], True, True, rwgt + [Rxcb], [('pb', br)])
                mm(pbs[bi][:, 0:n], wgt[:, c, 128:256], xcb[:, 0:n], True, True, rwgt + [Rxcb], [('pb', bi)])
                ckpt(0.412)
                act(ta[:, 0:n], pbs[br][:, 0:n], AF.Exp, [('pb', br), 'rgc'], [Rta], scale=-1.0, bias=rgc[:, 16 + 2 * c:17 + 2 * c])
                act(ta[:, 0:n], ta[:, 0:n], AF.Ln, [Rta], [Rta], bias=1.0)
                act(ta[:, 0:n], ta[:, 0:n], AF.Exp, [Rta], [Rta], scale=-1.0)
                act(tb_[:, 0:n], ta[:, 0:n], AF.Exp, [Rta, 'rgc'], [Rtb], scale=rgc[:, 8 + c:9 + c])
                act(ta[:, 0:n], ta[:, 0:n], AF.Exp, [Rta, 'rgc'], [Rta], scale=rgc[:, c:c + 1])
                act(tb_[:, 0:n], tb_[:, 0:n], AF.Ln, [Rtb], [Rtb], scale=-1.0, bias=1.0)
                act(tb_[:, 0:n], tb_[:, 0:n], AF.Exp, [Rtb], [Rtb], scale=0.5)
                ckpt(0.413)
                act(ti[:, 0:n], pbs[bi][:, 0:n], AF.Exp, [('pb', bi), 'rgc'], [Rti], scale=-1.0, bias=rgc[:, 17 + 2 * c:18 + 2 * c])
                act(ti[:, 0:n], ti[:, 0:n], AF.Ln, [Rti], [Rti], bias=1.0)
                act(ti[:, 0:n], ti[:, 0:n], AF.Exp, [Rti], [Rti], scale=-1.0)
                tt(uu[:, 0:n], tb_[:, 0:n], ti[:, 0:n], ALU.mult, [Rtb, Rti], [Ruu])
                tt(uu[:, 0:n], uu[:, 0:n], xc[:, 0:n], ALU.mult, [Ruu, Rxc], [Ruu])
                ckpt(0.414)
                if g < 4:
                    S.op('dve', (lambda c=c, n=n, hh=hh, ta=ta, uu=uu: lambda e: e.tensor_tensor_scan(hh[:, 0:n], ta[:, 0:n], uu[:, 0:n], hstate[:, c:c + 1], ALU.mult, ALU.add))(),
                         reads=[Rta, Ruu, ('hstate', c)], writes=[Rhh], cost=0.1 + n / 420.0)
                    cp('dve', hstate[:, c:c + 1], hh[:, n - 1:n], [Rhh], [('hstate', c)])
                else:
                    a3 = v3(ta[:, 0:n]); u3 = v3(uu[:, 0:n])
                    tt(tcc[:, 0:16], a3[:, :, 0], h0T[:, c, :], ALU.mult, [Rta, 'h0T'], ['tcc'])
                    tt(u3[:, :, 0], u3[:, :, 0], tcc[:, 0:16], ALU.add, [Ruu, 'tcc'], [Ruu])
                    S.op('dve', (lambda a3=a3: lambda e: e.memset(a3[:, :, 0:1], 0.0))(), reads=['tcc'], writes=[Rta])
                    S.op('dve', (lambda n=n, hh=hh, ta=ta, uu=uu: lambda e: e.tensor_tensor_scan(hh[:, 0:n], ta[:, 0:n], uu[:, 0:n], 0.0, ALU.mult, ALU.add))(),
                         reads=[Rta, Ruu], writes=[Rhh], cost=0.1 + n / 420.0)
                    cp('dve', tcc[:, 0:16], v3(hh[:, 0:n])[:, :, 7], [Rhh], ['tcc'])
                    bt = pbank()
                    tr(pbs[bt][0:16, 0:128], tcc[:, 0:16], ['tcc'], [('pb', bt)])
                    cp('dve', xpad[0:16, c, 128:256], pbs[bt][0:16, 0:128], [('pb', bt)], [('xpad', c)])
                    dma('sp', sh_o[:, c * 128:(c + 1) * 128], xpad[0:16, c, 128:256], [('xpad', c)], [])
                    cp('dve', tcc[:, 0:48].rearrange("p (s j) -> p s j", j=3), xpads[:, c, :, 8:11], [('xpad', c)], ['tcc'])
                    bt = pbank()
                    tr(pbs[bt][0:48, 0:128], tcc[:, 0:48], ['tcc'], [('pb', bt)])
                    cp('dve', xpad[0:48, c, 0:128], pbs[bt][0:48, 0:128], [('pb', bt)], [('xpad', c)])
                ckpt(0.415)
                by = pbank()
                for k in range(NCH):
                    mm(pbs[by][:, 0:n], win[:, k, 1024 + c * 128:1024 + (c + 1) * 128], hT[:, k, 0:n], k == 0, k == NCH - 1, rwin + [('hT', g % 2, k)], [('pb', by)])
                act(gq[:, 0:n], pbs[by][:, 0:n], AF.Square, [('pb', by)], [Rgq])
                ts(gq[:, 0:n], gq[:, 0:n], 0.044715, 1.0, ALU.mult, ALU.add, [Rgq], [Rgq])
                tt(gq[:, 0:n], gq[:, 0:n], pbs[by][:, 0:n], ALU.mult, [Rgq, ('pb', by)], [Rgq])
                act(gq[:, 0:n], gq[:, 0:n], AF.Exp, [Rgq], [Rgq], scale=-1.5957691216057308)
                act(gq[:, 0:n], gq[:, 0:n], AF.Ln, [Rgq], [Rgq], bias=1.0)
                act(gq[:, 0:n], gq[:, 0:n], AF.Exp, [Rgq], [Rgq], scale=-1.0)
                tt(gq[:, 0:n], gq[:, 0:n], pbs[by][:, 0:n], ALU.mult, [Rgq, ('pb', by)], [Rgq])
                tt(oT[:, c, 0:n], hh[:, 0:n], gq[:, 0:n], ALU.mult, [Rhh, Rgq], [('oT', c)])
                ckpt(0.416)
            for fc in range(NCH):
                bo = pbank()
                for c in range(NCH):
                    mm(pbs[bo][:, 0:n], wout[:, c, fc * 128:(fc + 1) * 128], oT[:, c, 0:n], c == 0, c == NCH - 1, rwout + [('oT', c)], [('pb', bo)])
                residual(g, fc, bo, G1, [('mod', 2)], X[2], 'X2')
            if g == 3:
                for hf in range(2):
                    b1 = pbank()
                    for j in range(4):
                        c = hf * 4 + j
                        tr(pbs[b1][0:3, j * 128:(j + 1) * 128], xpad[:, c, 0:3], [('xpad', c)], [('pb', b1)])
                    cp('dve', ti2[0][0:3, 0:512], pbs[b1][0:3, 0:512], [('pb', b1)], ['ti0'])
                    dma('sp', pconv[:, hf * 512:(hf + 1) * 512], ti2[0][0:3, 0:512], ['ti0'], [])
                for hf in range(2):
                    b1 = pbank()
                    for j in range(4):
                        c = hf * 4 + j
                        tr(pbs[b1][0:1, j * 128:(j + 1) * 128], hstate[:, c:c + 1], [('hstate', c)], [('pb', b1)])
                    cp('dve', ti2[0][0:1, 0:512], pbs[b1][0:1, 0:512], [('pb', b1)], ['ti0'])
                    dma('sp', ph[:, hf * 512:(hf + 1) * 512], ti2[0][0:1, 0:512], ['ti0'], [])
        for c in range(NCH):
            dma('sp', sconv_o[:, c * 128:(c + 1) * 128], xpad[0:48, c, 0:128], [('xpad', c)], [])

        if stop <= 1:
            return _finish()
        pb_pool[0] = (0, 1, 2, 3, 4, 5, 6, 7)
        S.stage = 'ffn0'
        ffn_stage(0)

        if stop <= 2:
            return _finish()
        S.stage = 'kv'
        S.barrier()
        AR.reset()
        AR.limit = ARENA_W
        hT = AR.bf16([128, NCH, 512])
        X = [AR.f32([128, 512]), AR.f32([128, 512]), AR.f32([128, 512])]
        Kf = [AR.f32([128, 256]), AR.f32([128, 256])]; Vf = [AR.f32([128, 256]), AR.f32([128, 256])]
        rt_kv = AR.f32([128, 4, 16, 8])

        def rope(buf3, bres, t, H, rt_=None):
            rt = rt_ if rt_ is not None else rt_kv
            cs = cost[:, t, :].unsqueeze(1).to_broadcast([128, H, 8])
            sn = sint[:, t, :].unsqueeze(1).to_broadcast([128, H, 8])
            x1 = buf3[:, :, 0:8]; x2 = buf3[:, :, 8:16]
            t1 = rt[:, 0, 0:H, :]; t2 = rt[:, 1, 0:H, :]; t3 = rt[:, 2, 0:H, :]; t4 = rt[:, 3, 0:H, :]
            tt(t1, x1, cs, ALU.mult, [bres, 'rope'], ['rt1'])
            tt(t2, x2, sn, ALU.mult, [bres, 'rope'], ['rt2'])
            tt(t3, x2, cs, ALU.mult, [bres, 'rope'], ['rt3'])
            tt(t4, x1, sn, ALU.mult, [bres, 'rope'], ['rt4'])
            tt(x1, t1, t2, ALU.subtract, ['rt1', 'rt2'], [bres])
            tt(x2, t3, t4, ALU.add, ['rt3', 'rt4'], [bres])

        wkv, rkv = wload(w_kv.rearrange("p (k n) -> p k n", k=NCH), [128, NCH, 512])
        for g in range(5):
            t0, n = GROUPS[g]
            pb_pool[0] = (0, 1, 2, 3, 4, 5, 6)
            norm_group(g, kvmodT[:, 8:16, :], kvmodT[:, 0:8, :], 'kvmod', hT, lambda c: ('hT', c), X, bank=7)
            for tl in range(n // 128):
                t = t0 // 128 + tl
                bk = pbank()
                for k in range(NCH):
                    mm(pbs[bk][:, 0:512], hT[:, k, tl * 128:(tl + 1) * 128], wkv[:, k, :], k == 0, k == NCH - 1, rkv + [('hT', k)], [('pb', bk)])
                kf = Kf[t % 2]; vf = Vf[t % 2]; kr = ('Kf', t % 2); vr = ('Vf', t % 2)
                cp('act', kf[:, :], pbs[bk][:, 0:256], [('pb', bk)], [kr])
                cp('act', vf[:, :], pbs[bk][:, 256:512], [('pb', bk)], [vr])
                rope(kf.rearrange("p (h d) -> p h d", d=64), kr, t, 4)
                cp('dve', Vb[:, t, :], vf[:, :], [vr], [('Vb', t)])
                for gp in range(2):
                    bt = pbank()
                    tr(pbs[bt][:, 0:128], kf[:, gp * 128:(gp + 1) * 128], [kr], [('pb', bt)])
                    cp('act', KT[:, gp, t * 128:(t + 1) * 128], pbs[bt][:, 0:128], [('pb', bt)], [('KT', t)])
                if t == 15:
                    dma('sp', pk, kf[:, :], [kr], []); dma('sp', pv, vf[:, :], [vr], [])
                if t == 16:
                    for s in range(16):
                        dma('sp', sk[s, 120:128, :], kf[8 * s:8 * s + 8, :], [kr], [])
                        dma('sp', svo[s, 120:128, :], vf[8 * s:8 * s + 8, :], [vr], [])
        dma('sp', sk[:, 0:120, :], ck[:, 8:128, :], [], [])
        dma('sp', svo[:, 0:120, :], cv[:, 8:128, :], [], [])

        if stop <= 3:
            return _finish()
        S.stage = 'attn'
        wq, rwq = wload(w_q.rearrange("p (k n) -> p k n", k=NCH), [128, NCH, 1024])
        wo_, rwo = wload(w_o.rearrange("p (k n) -> p k n", k=NCH), [128, NCH, 1024])
        A1 = modT[:, 8:16, :]; B1 = modT[:, 0:8, :]; G1 = modT[:, 16:24, :]

        def attn_phase(groups, NKMAX, sample, shared=None):
            if sample:
                S.barrier(engines=('pe', 'act', 'dve', 'sp', 'pool'))
                AR.reset()
            NBUF = 2 if sample else 6
            pb_pool[0] = (2, 3, 4, 5, 6, 7)
            ncol = 512 if not sample else 128
            if shared is not None:
                hT, X = shared
            else:
                hT = AR.bf16([128, NCH, ncol])
                X = [AR.f32([128, ncol]), AR.f32([128, ncol]), AR.f32([128, ncol])]
            NT = 1 if sample else 2
            Qf2 = [AR.f32([128, 1024]) for _ in range(NT)]; QT2 = [AR.bf16([128, 8, 128]) for _ in range(NT)]
            Of2 = [AR.f32([128, 1024]) for _ in range(NT)]; OT = AR.bf16([128, NCH, ncol])
            sm2 = [AR.f32([128, 6, 16]) for _ in range(NT)]
            Sm = [AR.f32([128, NKMAX * 128]) for _ in range(NBUF)]
            ET = [AR.bf16([128, NKMAX, 128]) for _ in range(NBUF)]
            if sample:
                KTc = AR.bf16([128, 2, 2048]); Vc = AR.bf16([128, 16, 256])
                ckf = [AR.f32([128, 256]), AR.f32([128, 256])]
                dma('pool', Vc[:, :, :], cv.rearrange("s k d -> k s d"), [], ['Vc'])
                for s in range(16):
                    cb = ckf[s % 2]; cr = ('ckf', s % 2)
                    dma('sp', cb[:, :], ck[s], [], [cr])
                    for gp in range(2):
                        bt = pbank()
                        tr(pbs[bt][:, 0:128], cb[:, gp * 128:(gp + 1) * 128], [cr], [('pb', bt)])
                        cp('act' if gp == 0 else 'dve', KTc[:, gp, s * 128:(s + 1) * 128], pbs[bt][:, 0:128], [('pb', bt)], [('KTc', s)])
            for g in groups:
                t0, n = GROUPS[g]
                norm_group(g, A1, B1, [('mod', 0), ('mod', 1)], hT, lambda c: ('hT', c), X)
                for tl in range(n // 128):
                    t = t0 // 128 + tl
                    tp = t % NT
                    Qf = Qf2[tp]; QT = QT2[tp]; Of = Of2[tp]; sm = sm2[tp]
                    RQf = 'Qf%d' % tp; RQT = 'QT%d' % tp; ROf = 'Of%d' % tp; RS = 'sm%d' % tp
                    for hf in range(2):
                        bq = pbank()
                        for k in range(NCH):
                            mm(pbs[bq][:, 0:512], hT[:, k, tl * 128:(tl + 1) * 128], wq[:, k, hf * 512:(hf + 1) * 512], k == 0, k == NCH - 1,
                               rwq + [('hT', k)], [('pb', bq)])
                        act(Qf.rearrange("p (j h d) -> p j h d", h=2, d=64)[:, hf * 4:(hf + 1) * 4, :, :],
                            pbs[bq][:, 0:512].rearrange("p (h b d) -> p b h d", h=2, b=4), AF.Identity, [('pb', bq)], [RQf], scale=0.125)
                    rope(Qf.rearrange("p (h d) -> p h d", d=64), RQf, t, 16)
                    for hf in range(2):
                        bt = pbank()
                        for jj in range(4):
                            j = hf * 4 + jj
                            tr(pbs[bt][:, jj * 128:(jj + 1) * 128], Qf[:, j * 128:(j + 1) * 128], [RQf], [('pb', bt)])
                        cp('act', QT[:, hf * 4:hf * 4 + 4, :], pbs[bt][:, :].rearrange("p (j n) -> p j n", n=128), [('pb', bt)], [RQT])
                    if not sample:
                        chunks = []
                        if t > 0:
                            chunks.append((lambda gp, t=t: KT[:, gp, (t - 1) * 128:t * 128], ('KT', t - 1), lambda kvh, t=t: Vb[:, t - 1, kvh * 64:(kvh + 1) * 64], ('Vb', t - 1), maskp[:, 0:128], None))
                        chunks.append((lambda gp, t=t: KT[:, gp, t * 128:(t + 1) * 128], ('KT', t), lambda kvh, t=t: Vb[:, t, kvh * 64:(kvh + 1) * 64], ('Vb', t), maskp[:, 128:256], None))
                    else:
                        chunks = []
                        for s in range(16):
                            chunks.append((lambda gp, s=s: KTc[:, gp, s * 128:(s + 1) * 128], ('KTc', s), lambda kvh, s=s: Vc[:, s, kvh * 64:(kvh + 1) * 64], 'Vc', masks[:, 0:128], s))
                        chunks.append((lambda gp: KT[:, gp, 2048:2176], ('KT', 16), lambda kvh: Vb[:, 16, kvh * 64:(kvh + 1) * 64], ('Vb', 16), masksn[:, :], None))
                    nk = len(chunks)
                    bo2 = [0, 1]
                    for h in range(16):
                        kvh = h // 4; gp = kvh // 2; half = kvh % 2
                        j = 4 * (kvh // 2) + h % 4
                        p0 = half * 64
                        Sb = Sm[h % NBUF]; sres = ('Sm', h % NBUF)
                        for ci in range(0, nk, 4):
                            bs = pbank()
                            for cj in range(ci, min(ci + 4, nk)):
                                kfn, kres, vfn, vres, mk, rs_ = chunks[cj]
                                mm(pbs[bs][:, (cj - ci) * 128:(cj - ci + 1) * 128], QT[p0:p0 + 64, j, :], kfn(gp)[p0:p0 + 64, :], True, True,
                                   [RQT, kres], [('pb', bs)])
                            if (not sample) and nk == 2:
                                tt(Sb[:, 0:256], pbs[bs][:, 0:256], maskp[:, 0:256], ALU.add, [('pb', bs), 'mask'], [sres])
                                continue
                            for cj in range(ci, min(ci + 4, nk)):
                                kfn, kres, vfn, vres, mk, rs_ = chunks[cj]
                                if rs_ is None:
                                    tt(Sb[:, cj * 128:(cj + 1) * 128], pbs[bs][:, (cj - ci) * 128:(cj - ci + 1) * 128], mk, ALU.add, [('pb', bs), 'mask'], [sres])
                                else:
                                    stt(Sb[:, cj * 128:(cj + 1) * 128], pbs[bs][:, (cj - ci) * 128:(cj - ci + 1) * 128], rowm[:, rs_:rs_ + 1], mk, ALU.add, ALU.add,
                                        [('pb', bs), 'mask'], [sres])
                        S.op('dve', (lambda Sb=Sb, nk=nk, h=h, sm=sm: lambda e: e.reduce_max(sm[:, 0, h:h + 1], Sb[:, 0:nk * 128], AX.X))(), reads=[sres], writes=[(RS + 'mx', h)], cost=0.08 + nk * 128 / 960.0)
                        ts(sm[:, 1, h:h + 1], sm[:, 0, h:h + 1], sinkb[:, h:h + 1], -1.0, ALU.max, ALU.mult, [(RS + 'mx', h), 'sink'], [(RS + 'nm', h)])
                        act(Sb[:, 0:nk * 128], Sb[:, 0:nk * 128], AF.Exp, [sres, (RS + 'nm', h)], [sres, (RS + 'rs', h)], bias=sm[:, 1, h:h + 1], accum=sm[:, 2, h:h + 1])
                        Eb = ET[h % NBUF]; eres = ('ET', h % NBUF)
                        for ci in range(0, nk, 4):
                            bt = pbank()
                            m_ = min(ci + 4, nk) - ci
                            for cj in range(ci, ci + m_):
                                tr(pbs[bt][:, (cj - ci) * 128:(cj - ci + 1) * 128], Sb[:, cj * 128:(cj + 1) * 128], [sres], [('pb', bt)])
                            cp('act' if (ci // 4) % 2 == 0 else 'dve', Eb[:, ci:ci + m_, :], pbs[bt][:, 0:m_ * 128].rearrange("p (j n) -> p j n", n=128), [('pb', bt)], [eres])
                        bo = bo2[h // 8]
                        for cj in range(nk):
                            kfn, kres, vfn, vres, mk, rs_ = chunks[cj]
                            mm(pbs[bo][:, (h % 8) * 64:(h % 8 + 1) * 64], Eb[:, cj, :], vfn(kvh), cj == 0, cj == nk - 1, [eres, vres], [('pb', bo)])
                        if h % 8 == 7:
                            hf = h // 8; hs = slice(hf * 8, hf * 8 + 8)
                            hl = list(range(hf * 8, hf * 8 + 8))
                            tt(sm[:, 3, hs], sinkb[:, hs], sm[:, 1, hs], ALU.add, ['sink'] + [(RS + 'nm', x) for x in hl], [(RS + 'es', hf)])
                            act(sm[:, 3, hs], sm[:, 3, hs], AF.Exp, [(RS + 'es', hf)], [(RS + 'es', hf)])
                            tt(sm[:, 4, hs], sm[:, 2, hs], sm[:, 3, hs], ALU.add, [(RS + 'es', hf)] + [(RS + 'rs', x) for x in hl], [(RS + 'den', hf)])
                            S.op('dve', (lambda sm=sm, hs=hs: lambda e: e.reciprocal(sm[:, 5, hs], sm[:, 4, hs]))(), reads=[(RS + 'den', hf)], writes=[(RS + 'rden', hf)], cost=0.1)
                            tt(Of[:, hf * 512:(hf + 1) * 512].rearrange("p (h d) -> p h d", d=64), pbs[bo][:, 0:512].rearrange("p (h d) -> p h d", d=64),
                               sm[:, 5, hs].unsqueeze(2).to_broadcast([128, 8, 64]), ALU.mult, [('pb', bo), (RS + 'rden', hf)], [ROf])
                    for hf in range(2):
                        bt = pbank()
                        for jj in range(4):
                            c = hf * 4 + jj
                            tr(pbs[bt][:, jj * 128:(jj + 1) * 128], Of[:, c * 128:(c + 1) * 128], [ROf], [('pb', bt)])
                        cp('act', OT[:, hf * 4:hf * 4 + 4, tl * 128:(tl + 1) * 128], pbs[bt][:, :].rearrange("p (j n) -> p j n", n=128), [('pb', bt)],
                           [('OT', c) for c in range(hf * 4, hf * 4 + 4)])
                for fc in range(NCH):
                    bo = pbank()
                    for c in range(NCH):
                        mm(pbs[bo][:, 0:n], wo_[:, c, fc * 128:(fc + 1) * 128], OT[:, c, 0:n], c == 0, c == NCH - 1, rwo + [('OT', c)], [('pb', bo)])
                    residual(g, fc, bo, G1, [('mod', 2)], X[2], 'X2')

        def attn_sample():
            g = 4
            t0, n = GROUPS[g]
            t = 16
            NB3 = 6
            S.barrier(engines=('pe', 'act', 'dve', 'sp', 'pool'))
            AR.reset()
            pb_pool[0] = (2, 3, 4, 5, 6, 7)
            hT = AR.bf16([128, NCH, 128]); X = [AR.f32([128, 128]) for _ in range(3)]
            Qf = AR.f32([128, 1024]); QTs = AR.bf16([128, 16, 8, 8])
            KTc = AR.bf16([128, 2, 2048]); Vc = AR.bf16([128, 16, 256])
            ckf = [AR.f32([128, 256]), AR.f32([128, 256])]
            Smb = [AR.f32([128, 136]) for _ in range(NB3)]
            Epad = [AR.f32([128, 128]) for _ in range(NB3)]
            ETb = [AR.bf16([128, 2, 128]) for _ in range(NB3)]
            Of = AR.f32([128, 1024]); OT = AR.bf16([128, NCH, 128])
            sm = AR.f32([128, 8, 16])
            rt_s = AR.f32([128, 4, 16, 8])
            for i in range(NB3):
                S.op('dve', (lambda i=i: lambda e: e.memset(Epad[i][:, :], 0.0))(), writes=[('Epad', i)])
            for s in range(16):
                dma('pool', Vc[:, s, :], cv[s], [], [('Vc', s)])
                cb = ckf[s % 2]; cr = ('ckf', s % 2)
                dma('sp', cb[:, :], ck[s], [], [cr])
                for gp in range(2):
                    bt = pbank()
                    tr(pbs[bt][:, 0:128], cb[:, gp * 128:(gp + 1) * 128], [cr], [('pb', bt)])
                    cp('act' if gp == 0 else 'dve', KTc[:, gp, s * 128:(s + 1) * 128], pbs[bt][:, 0:128], [('pb', bt)], [('KTc', s)])
            norm_group(g, A1, B1, [('mod', 0), ('mod', 1)], hT, lambda c: ('hT', c), X)
            for hf in range(2):
                bq = pbank()
                for k in range(NCH):
                    mm(pbs[bq][:, 0:512], hT[:, k, 0:128], wq[:, k, hf * 512:(hf + 1) * 512], k == 0, k == NCH - 1, rwq + [('hT', k)], [('pb', bq)])
                act(Qf.rearrange("p (j h d) -> p j h d", h=2, d=64)[:, hf * 4:(hf + 1) * 4, :, :],
                    pbs[bq][:, 0:512].rearrange("p (h b d) -> p b h d", h=2, b=4), AF.Identity, [('pb', bq)], ['Qf0'], scale=0.125)
            rope(Qf.rearrange("p (h d) -> p h d", d=64), 'Qf0', t, 16, rt_s)
            for hf in range(2):
                bt = pbank()
                for jj in range(4):
                    j = hf * 4 + jj
                    tr(pbs[bt][:, jj * 128:(jj + 1) * 128], Qf[:, j * 128:(j + 1) * 128], ['Qf0'], [('pb', bt)])
                cp('act', QTs[:, :, hf * 4:hf * 4 + 4, :], pbs[bt][:, :].rearrange("p (j s q) -> p s j q", j=4, s=16), [('pb', bt)], ['QTs'])
            bo2 = [0, 1]
            for s in range(16):
                i3 = s % NB3
                Sb = Smb[i3]; sres = ('Smb', i3); Ep = Epad[i3]; epres = ('Epad', i3); Eb = ETb[i3]; eres = ('ETb', i3)
                bs = pbank()
                for kvh in range(4):
                    gp = kvh // 2; p0 = (kvh % 2) * 64
                    lq = QTs[p0:p0 + 64, s, 4 * gp:4 * gp + 4, :].rearrange("p j q -> p (j q)")
                    mm(pbs[bs][32 * kvh:32 * kvh + 32, 0:128], lq, KTc[p0:p0 + 64, gp, s * 128:(s + 1) * 128], True, True,
                       ['QTs', ('KTc', s)], [('pb', bs)], tile_position=(p0, 32 * kvh))
                    mm(pbs[bs][32 * kvh:32 * kvh + 32, 128:136], lq, KT[p0:p0 + 64, gp, 2048 + 8 * s:2048 + 8 * s + 8], True, True,
                       ['QTs', ('KT', 16)], [('pb', bs)], tile_position=(p0, 32 * kvh))
                tt(Sb[:, 0:136], pbs[bs][:, 0:136], masks[:, 0:136], ALU.add, [('pb', bs), 'mask'], [sres])
                S.op('dve', (lambda Sb=Sb, s=s, sm=sm: lambda e: e.reduce_max(sm[:, 0, s:s + 1], Sb[:, 0:136], AX.X))(), reads=[sres], writes=[('s_mx', s)], cost=0.25)
                ts(sm[:, 1, s:s + 1], sm[:, 0, s:s + 1], sinkc[:, 0:1], -1.0, ALU.max, ALU.mult, [('s_mx', s), 'sink'], [('s_nm', s)])
                act(Sb[:, 0:128], Sb[:, 0:128], AF.Exp, [sres, ('s_nm', s)], [sres, ('s_rs', s)], bias=sm[:, 1, s:s + 1], accum=sm[:, 2, s:s + 1])
                act(Ep[:, 8 * s:8 * s + 8], Sb[:, 128:136], AF.Exp, [sres, ('s_nm', s), epres], [epres, ('s_rs2', s)], bias=sm[:, 1, s:s + 1], accum=sm[:, 3, s:s + 1])
                bt = pbank()
                tr(pbs[bt][:, 0:128], Sb[:, 0:128], [sres], [('pb', bt)])
                tr(pbs[bt][:, 128:256], Ep[:, :], [epres], [('pb', bt)])
                cp('act' if s % 2 == 0 else 'dve', Eb[:, :, :], pbs[bt][:, 0:256].rearrange("p (j n) -> p j n", n=128), [('pb', bt)], [eres])
                S.op('dve', (lambda Ep=Ep, s=s: lambda e: e.memset(Ep[:, 8 * s:8 * s + 8], 0.0))(), reads=[epres], writes=[epres], cost=0.1)
                bo = bo2[s // 8]
                for kvh in range(4):
                    o_ = pbs[bo][32 * kvh:32 * kvh + 32, (s % 8) * 64:(s % 8 + 1) * 64]
                    mm(o_, Eb[:, 0, 32 * kvh:32 * kvh + 32], Vc[:, s, kvh * 64:(kvh + 1) * 64], True, False, [eres, ('Vc', s)], [('pb', bo)], tile_position=(0, 32 * kvh))
                    mm(o_, Eb[:, 1, 32 * kvh:32 * kvh + 32], Vb[:, 16, kvh * 64:(kvh + 1) * 64], False, True, [eres, ('Vb', 16)], [('pb', bo)], tile_position=(0, 32 * kvh))
            alls = lambda nm: [(nm, s) for s in range(16)]
            tt(sm[:, 5, :], sm[:, 2, :], sm[:, 3, :], ALU.add, alls('s_rs') + alls('s_rs2'), ['s_den'])
            act(sm[:, 4, :], sm[:, 1, :], AF.Exp, alls('s_nm') + ['sink'], ['s_es'], bias=sinkc[:, 0:1])
            tt(sm[:, 5, :], sm[:, 5, :], sm[:, 4, :], ALU.add, ['s_den', 's_es'], ['s_den'])
            S.op('dve', lambda e: e.reciprocal(sm[:, 6, :], sm[:, 5, :]), reads=['s_den'], writes=['s_rden'], cost=0.1)
            Os = Qf
            for hf in range(2):
                tt(Os[:, hf * 512:(hf + 1) * 512].rearrange("p (s d) -> p s d", d=64), pbs[bo2[hf]][:, 0:512].rearrange("p (s d) -> p s d", d=64),
                   sm[:, 6, hf * 8:hf * 8 + 8].unsqueeze(2).to_broadcast([128, 8, 64]), ALU.mult, [('pb', bo2[hf]), 's_rden'], ['Qf0'])
            dma('sp', scr_o, Os[:, :], ['Qf0'], ['scr_o'])
            srcv = scr_o.rearrange("(h q) (s d) -> q s h d", q=8, d=64)
            for q in range(8):
                dma('sp', Of[q::8, :].rearrange("p (h d) -> p h d", d=64), srcv[q], ['scr_o'], ['Of0'])
            for hf in range(2):
                bt = pbank()
                for jj in range(4):
                    c = hf * 4 + jj
                    tr(pbs[bt][:, jj * 128:(jj + 1) * 128], Of[:, c * 128:(c + 1) * 128], ['Of0'], [('pb', bt)])
                cp('act', OT[:, hf * 4:hf * 4 + 4, 0:128], pbs[bt][:, :].rearrange("p (j n) -> p j n", n=128), [('pb', bt)],
                   [('OT', c) for c in range(hf * 4, hf * 4 + 4)])
            for fc in range(NCH):
                bo = pbank()
                for c in range(NCH):
                    mm(pbs[bo][:, 0:n], wo_[:, c, fc * 128:(fc + 1) * 128], OT[:, c, 0:n], c == 0, c == NCH - 1, rwo + [('OT', c)], [('pb', bo)])
                residual(g, fc, bo, G1, [('mod', 2)], X[2], 'X2')

        rowm = sb("rowm_sb", [128, 16]); masksn = sb("masksn_sb", [128, 128])
        rowmd = din("rowm", [128, 16]); masksnd = din("masksn", [128, 128])
        dma('sp', rowm[:], rowmd, [], ['mask']); dma('sp', masksn[:], masksnd, [], ['mask'])
        attn_phase([0, 1, 2, 3], 2, False, shared=(hT, X))
        if stop <= 4:
            return _finish()
        S.stage = 'attn_s'
        attn_sample()
        pb_pool[0] = (0, 1, 2, 3, 4, 5, 6, 7)

        if stop <= 5:
            return _finish()
        S.stage = 'ffn1'
        ffn_stage(1)

        if stop <= 6:
            return _finish()
        S.stage = 'final'
        S.barrier()
        AR.reset()
        yT = AR.f32([128, NCH, 512])
        X = [AR.f32([128, 512]), AR.f32([128, 512]), AR.f32([128, 512])]
        yo = [AR.f32([128, 1024]), AR.f32([128, 1024])]
        for g in range(5):
            t0, n = GROUPS[g]
            norm_group(g, None, None, 'svT', yT, lambda c: ('yT', c), X, out_f32_scale=lambda c: svT[:, SV_FG + c:SV_FG + c + 1])
            for tl in range(n // 128):
                t = t0 // 128 + tl
                yb = yo[t % 2]; yr = ('yo', t % 2)
                for hf in range(2):
                    bt = pbank()
                    for jj in range(4):
                        c = hf * 4 + jj
                        tr(pbs[bt][:, jj * 128:(jj + 1) * 128], yT[:, c, tl * 128:(tl + 1) * 128], [('yT', c)], [('pb', bt)])
                    cp('act' if hf == 0 else 'dve', yb[:, hf * 512:(hf + 1) * 512], pbs[bt][:, 0:512], [('pb', bt)], [yr])
                dst = y_p[t * 128:(t + 1) * 128, :] if t < 16 else y_s
                dma('sp', dst, yb[:, :], [yr], [])
        return _finish()


_CACHE = {}


def _consts():
    ROT = 16
    inv = (500000.0 ** (-np.arange(0, ROT, 2, dtype=np.float32) / np.float32(ROT))).astype(np.float32)
    pos = np.zeros((128, 17), np.float32)
    for t in range(16):
        pos[:, t] = t * 128 + np.arange(128)
    pos[:, 16] = 16384 + (np.arange(128) % 8)
    ang = (pos[:, :, None] * inv[None, None, :]).astype(np.float32)
    cos = np.cos(ang).astype(np.float32); sin = np.sin(ang).astype(np.float32)
    NEG = -30000.0
    q = np.arange(128)[:, None]; s = np.arange(256)[None, :]
    rel = 128 + q - s
    maskp = np.where((rel >= 0) & (rel <= 128), 0.0, NEG).astype(np.float32)
    qi = (np.arange(128) % 8)[:, None]; sq = (np.arange(128) // 8)
    k = np.arange(128)[None, :]
    masks = np.zeros((128, 136), np.float32)
    masks[:, 0:128] = np.where(k >= qi, 0.0, NEG)
    masks[:, 128:136] = np.where(np.arange(8)[None, :] <= qi, 0.0, NEG)
    rowm = np.where(sq[:, None] == np.arange(16)[None, :], 0.0, NEG).astype(np.float32)
    ks = (np.arange(128) // 8)[None, :]; kt = (np.arange(128) % 8)[None, :]
    masksn = np.where((ks == sq[:, None]) & (kt <= qi), 0.0, NEG).astype(np.float32)
    return dict(ident=np.eye(128, dtype=np.float32), cos=cos, sin=sin, maskp=maskp, masks=masks, rowm=rowm, masksn=masksn)


def kernel(x_prompt, x_sample, c_prompt, c_sample, state_conv, state_h, cache_k, cache_v,
           ada_w, ada_b, norm_g, rnn_w_in, rnn_conv_w, rnn_conv_b, rnn_gate_w, rnn_gate_b,
           rnn_lambda, rnn_w_out, kv_ada_w, kv_ada_b, kv_norm_g, w_kv, attn_w_q, attn_sinks,
           attn_w_o, ffn_w_in, ffn_w_out, final_g):
    f = lambda a: np.ascontiguousarray(np.asarray(a, dtype=np.float32))
    if 'nc' not in _CACHE:
        _CACHE['nc'] = build_program()
    nc = _CACHE['nc']
    C = _consts()
    sv = np.concatenate([f(ada_b).reshape(96, 128), f(kv_ada_b).reshape(16, 128), f(norm_g).reshape(32, 128),
                         f(kv_norm_g).reshape(8, 128), f(final_g).reshape(8, 128), f(rnn_conv_w).reshape(32, 128),
                         f(rnn_conv_b).reshape(8, 128), f(rnn_gate_b).reshape(16, 128), f(rnn_lambda).reshape(8, 128)], axis=0)
    sinks = f(attn_sinks)[0]
    def pk(w):
        k = w.shape[0] // 128
        return np.ascontiguousarray(w.reshape(k, 128, w.shape[1]).transpose(1, 0, 2).reshape(128, k * w.shape[1]))
    aw = f(ada_w)
    adaP = np.stack([np.stack([pk(aw[l][:, v * 1024:(v + 1) * 1024]) for v in range(6)]) for l in range(2)])
    kw_ = f(kv_ada_w)
    kvadaP = np.stack([pk(kw_[:, v * 1024:(v + 1) * 1024]) for v in range(2)])
    fi = f(ffn_w_in); fo = f(ffn_w_out)
    ffl = []
    for l in range(2):
        parts = []
        for (h0, hc) in FFN_GROUPS:
            parts.append(pk(fi[l][:, h0 * 128:(h0 + hc) * 128]))
            parts.append(pk(fi[l][:, DFF + h0 * 128:DFF + (h0 + hc) * 128]))
            parts.append(pk(fo[l][h0 * 128:(h0 + hc) * 128, :]))
        ffl.append(np.concatenate(parts, axis=1))
    ffnP = np.ascontiguousarray(np.stack(ffl))
    shared = dict(adaP=adaP, rnn_w_in=pk(f(rnn_w_in)[0]), gate_w=np.ascontiguousarray(f(rnn_gate_w)[0].transpose(1, 0, 2).reshape(128, 2048)),
                  rnn_w_out=pk(f(rnn_w_out)[0]), kvadaP=kvadaP, w_kv=pk(f(w_kv)), w_q=pk(f(attn_w_q)[0]), w_o=pk(f(attn_w_o)[0]), ffnP=ffnP,
                  sv=sv, ident=C['ident'], cos=C['cos'], sin=C['sin'], maskp=C['maskp'],
                  masks=C['masks'], rowm=C['rowm'], masksn=C['masksn'],
                  sinkb=np.ascontiguousarray(np.broadcast_to(sinks[None, :], (128, 16))),
                  sinkc=np.ascontiguousarray(np.repeat(sinks, 8)[:, None]))
    xp_ = f(x_prompt); xs_ = f(x_sample); cp_ = f(c_prompt); cs_ = f(c_sample)
    sc_ = f(state_conv); sh_ = f(state_h); ck_ = f(cache_k); cv_ = f(cache_v)
    in_maps = []
    for b in range(8):
        sl = slice(16 * b, 16 * b + 16)
        m = dict(shared)
        m.update(xp=xp_[b], xs=np.ascontiguousarray(xs_[sl].reshape(128, 1024)),
                 cc=np.ascontiguousarray(np.concatenate([cp_[b:b + 1], cs_[sl]], axis=0)),
                 sconv=np.ascontiguousarray(sc_[0, sl].reshape(48, 1024)), shin=np.ascontiguousarray(sh_[0, sl]),
                 ck=np.ascontiguousarray(ck_[sl].reshape(16, 128, 256)), cv=np.ascontiguousarray(cv_[sl].reshape(16, 128, 256)))
        in_maps.append(m)
    res = run_bass_kernel_spmd(nc, in_maps, core_ids=list(range(8)))
    R = res.results
    cat = lambda k: np.stack([np.asarray(r[k], dtype=np.float32) for r in R], axis=0)
    y_prompt = cat('y_p')
    y_sample = cat('y_s').reshape(128, 8, 1024)
    prompt_conv = cat('pconv').reshape(1, 8, 3, 1024)
    prompt_h = cat('ph').reshape(1, 8, 1024)
    prompt_k = cat('pk').reshape(8, 128, 4, 64)
    prompt_v = cat('pv').reshape(8, 128, 4, 64)
    sample_conv = cat('sconv_o').reshape(1, 128, 3, 1024)
    sample_h = cat('sh_o').reshape(1, 128, 1024)
    sample_k = cat('sk').reshape(128, 128, 4, 64)
    sample_v = cat('svo').reshape(128, 128, 4, 64)
    return (y_prompt, y_sample, prompt_conv, prompt_h, prompt_k, prompt_v, sample_conv, sample_h, sample_k, sample_v)
```

```python
import sys
import numpy as np
from contextlib import ExitStack
import concourse.bass as bass
import concourse.mybir as mybir
from concourse.bass_utils import run_bass_kernel_spmd

F32 = mybir.dt.float32
BF16 = mybir.dt.bfloat16
AF = mybir.ActivationFunctionType
ALU = mybir.AluOpType
AX = mybir.AxisListType

ENGS = ['pe', 'act', 'dve', 'pool', 'sp']
LOOKBACK = 3

D = 1024
NCH = 8
TP = 2048
TS = 128
T = TP + TS
DFF = 2816
NHC = 22
EPS = 1e-6
GROUPS = [(0, 512), (512, 512), (1024, 512), (1536, 512), (2048, 128)]
FFN_GROUPS = [(0, 4), (4, 4), (8, 4), (12, 4), (16, 3), (19, 3)]
SV_ADAB = 0
SV_KVB = 96
SV_NG = 112
SV_KVG = 144
SV_FG = 152
SV_CW = 160
SV_CB = 192
SV_GB = 200
SV_LM = 216
SV_ROWS = 224
RING_SLOTS = 13
SLOT = 2048
ARENA_W = 15500
TAIL_W = 4352
DUMP_PATH = None
SCHEDULE = True
STRICT_SAME = True
SAME_LAT = 0.25
PRIO_RANK = True
VERBOSE = False


class Ins:
    __slots__ = ('eng', 'fn', 'deps', 'idx', 'signaled', 'count', 'dma', 'semi', 'dval', 'waits', 'tag', 'cost', 'seq', 'bar', 'start', 'fin', 'nun', 'succ', 'stage')


class Sched:
    def __init__(self, nc, n_dma_sems=56):
        self.nc = nc
        self.n_main = 48
        self.rr_b = 0
        self.streams = {e: [] for e in ENGS}
        self.last_w = {}
        self.readers = {}
        self.n_dma_sems = n_dma_sems
        self.dma_rr = 0
        self.dma_last = [None] * n_dma_sems
        self.dma_cnt = [0] * n_dma_sems
        self.pb_acc = {}
        self.seq = 0
        self.cur_bar = {e: None for e in ENGS}
        self.since_bar = {e: [] for e in ENGS}
        self.n_bar = 0
        self.stage = 'init'

    def _new(self, eng, fn, dma):
        ins = Ins()
        ins.eng = eng; ins.fn = fn; ins.dma = dma; ins.signaled = False; ins.count = 0
        ins.semi = None; ins.dval = 0; ins.waits = None; ins.deps = []
        ins.cost = 0.1; ins.bar = None; ins.start = 0.0; ins.fin = 0.0
        ins.seq = self.seq; self.seq += 1
        ins.stage = self.stage
        ins.tag = 0
        try:
            f = sys._getframe(1)
            while f is not None and f.f_code.co_name in ('_new', 'op', 'mm', 'tr', 'act', 'tt', 'ts', 'stt', 'cp', 'dma', 'wload', 'barrier', 'finish'):
                f = f.f_back
            ins.tag = f.f_lineno if f is not None else 0
        except Exception:
            ins.tag = 0
        return ins

    def op(self, eng, fn, reads=(), writes=(), dma=False, cost=0.1, sem_b=False):
        ins = self._new(eng, fn, dma)
        ins.cost = cost
        deps = {}
        cb = self.cur_bar[eng]
        if cb is not None:
            deps[id(cb)] = (cb, 'BAR')
        for r in reads:
            w = self.last_w.get(r)
            if w is not None:
                deps[id(w)] = (w, 'RAW')
        for r in writes:
            w = self.last_w.get(r)
            if w is not None and id(w) not in deps:
                deps[id(w)] = (w, 'WAW')
            for rd in self.readers.get(r, ()):
                if id(rd) not in deps:
                    deps[id(rd)] = (rd, 'WAR')
        banks = set(r for r in list(reads) + list(writes) if isinstance(r, tuple) and r[0] == 'pb')
        for bnk in banks:
            st = self.pb_acc.get(bnk)
            if st is None:
                st = self.pb_acc[bnk] = {'eng': eng, 'cur': [], 'prev': []}
            if st['eng'] != eng:
                st['prev'] = st['cur']
                st['cur'] = []
                st['eng'] = eng
            for d in st['prev']:
                if id(d) not in deps:
                    deps[id(d)] = (d, 'PB')
            st['cur'].append(ins)
        if dma:
            half = self.n_dma_sems // 2
            if eng == 'pool':
                si = self.rr_b % half
                self.rr_b += 1
            else:
                si = half + (self.dma_rr % half)
                self.dma_rr += 1
            prev = self.dma_last[si]
            if prev is not None and id(prev) not in deps:
                deps[id(prev)] = (prev, 'SEM')
            self.dma_cnt[si] += 1
            ins.semi = si
            ins.dval = 16 * self.dma_cnt[si]
            self.dma_last[si] = ins
        ins.deps = list(deps.values())
        for r in reads:
            lst = self.readers.setdefault(r, [])
            lst.append(ins)
        for r in writes:
            self.last_w[r] = ins
            self.readers[r] = []
        ins.idx = len(self.streams[eng])
        self.streams[eng].append(ins)
        self.since_bar[eng].append(ins)
        return ins

    def barrier(self, engines=('pe', 'act', 'dve', 'sp')):
        pend = [d for d in self.dma_last if d is not None]
        prior = []
        for e in ENGS:
            prior += [x for x in self.since_bar[e] if x.fn is not None]
        self.n_bar += 1
        for e in engines:
            ins = self._new(e, None, False)
            ins.cost = 0.0
            ins.bar = self.n_bar
            deps = {}
            for d in prior:
                deps[id(d)] = (d, 'BAR')
            for d in pend:
                deps[id(d)] = (d, 'RAW')
            cb = self.cur_bar[e]
            if cb is not None:
                deps[id(cb)] = (cb, 'BAR')
            ins.deps = list(deps.values())
            ins.idx = len(self.streams[e])
            self.streams[e].append(ins)
            self.cur_bar[e] = ins
        for e in ENGS:
            if e in engines:
                self.since_bar[e] = []

    def finish(self):
        pend = [d for d in self.dma_last if d is not None]
        ins = self._new('sp', None, False)
        ins.cost = 0.0
        ins.deps = [(d, 'RAW') for d in pend] + [(d, 'BAR') for d in self.streams['sp'] if d is not ins]
        ins.idx = len(self.streams['sp'])
        self.streams['sp'].append(ins)

    def schedule(self):
        import heapq
        allins = []
        for e in ENGS:
            allins += self.streams[e]
        for ins in allins:
            ins.succ = []
            ins.nun = 0
        for ins in allins:
            seen = set()
            for d, kind in ins.deps:
                if id(d) in seen:
                    continue
                seen.add(id(d))
                d.succ.append(ins)
                ins.nun += 1
        prio = {}
        if PRIO_RANK:
            order_seq = sorted(allins, key=lambda x: -x.seq)
            rank = {}
            for ins in order_seq:
                r = 0.0
                for sc in ins.succ:
                    rs_ = rank[id(sc)]
                    if rs_ > r:
                        r = rs_
                rank[id(ins)] = r + ins.cost + (0.25 if not ins.dma else 2.0)
            for ins in allins:
                prio[id(ins)] = -rank[id(ins)]
        else:
            for ins in allins:
                prio[id(ins)] = ins.seq
        avail = {e: [] for e in ENGS}
        ready_t = {}
        for ins in allins:
            if ins.nun == 0:
                heapq.heappush(avail[ins.eng], (prio[id(ins)], id(ins), ins))
                ready_t[id(ins)] = 0.0
        free = {e: 0.0 for e in ENGS}
        dma_pipe = [0.0]
        order = {e: [] for e in ENGS}
        remaining = len(allins)
        WIN = 48
        while remaining:
            best = None
            for e in ENGS:
                h = avail[e]
                if not h:
                    continue
                cands = heapq.nsmallest(WIN, h)
                pick = None
                for c in cands:
                    if ready_t[id(c[2])] <= free[e] + 1e-9:
                        pick = c
                        break
                if pick is None:
                    pick = min(cands, key=lambda c: (ready_t[id(c[2])], c[0]))
                st = max(free[e], ready_t[id(pick[2])])
                if best is None or st < best[0] or (st == best[0] and pick[0] < best[1][0]):
                    best = (st, pick, e)
            st, pick, e = best
            ins = pick[2]
            avail[e].remove(pick)
            heapq.heapify(avail[e])
            ins.start = st
            if ins.dma:
                free[e] = st + 0.06
                t0 = max(st + 1.8, dma_pipe[0])
                ins.fin = t0 + ins.cost
                dma_pipe[0] = ins.fin
            else:
                ins.fin = st + ins.cost
                free[e] = ins.fin
            order[e].append(ins)
            remaining -= 1
            for sc in ins.succ:
                sc.nun -= 1
                if sc.eng == ins.eng and not ins.dma:
                    lat = SAME_LAT if ins.eng in ('act', 'dve') else 0.0
                else:
                    lat = 0.40
                rt = max(ready_t.get(id(sc), 0.0), ins.fin + lat)
                ready_t[id(sc)] = rt
                if sc.nun == 0:
                    heapq.heappush(avail[sc.eng], (prio[id(sc)], id(sc), sc))
        for e in ENGS:
            self.streams[e] = order[e]
            for i, ins in enumerate(order[e]):
                ins.idx = i
        self.sim_time = max(free.values())
        print("[sched] simulated time (us): %.1f" % self.sim_time, {e: len(order[e]) for e in ENGS})
        if VERBOSE:
            st = {}
            for e in ENGS:
                for ins in order[e]:
                    d = st.setdefault(ins.stage, {'t0': 1e18, 't1': 0.0, 'busy': {x: 0.0 for x in ENGS}})
                    d['t0'] = min(d['t0'], ins.start); d['t1'] = max(d['t1'], ins.fin)
                    if not ins.dma:
                        d['busy'][e] += ins.cost
            for k, d in sorted(st.items(), key=lambda kv: kv[1]['t0']):
                print("  %-10s t0=%7.1f t1=%7.1f span=%7.1f  busy pe=%6.1f act=%6.1f dve=%6.1f" % (k, d['t0'], d['t1'], d['t1'] - d['t0'], d['busy']['pe'], d['busy']['act'], d['busy']['dve']))

    def plan(self):
        for eng in ENGS:
            known_eng = {e: -1 for e in ENGS}
            known_dma = {}
            for ins in self.streams[eng]:
                waits = []
                tgt = {}
                for d, kind in ins.deps:
                    if d.dma:
                        if known_dma.get(d.semi, 0) >= d.dval:
                            continue
                        known_dma[d.semi] = d.dval
                        waits.append(d)
                    elif d.fn is None:
                        continue
                    elif d.eng == eng:
                        if eng == 'pe':
                            continue
                        if (kind == 'RAW' and ins.idx - d.idx <= LOOKBACK) or STRICT_SAME:
                            if d.idx > tgt.get(eng, (-1, None))[0]:
                                tgt[eng] = (d.idx, d)
                    else:
                        if d.idx > tgt.get(d.eng, (-1, None))[0]:
                            tgt[d.eng] = (d.idx, d)
                for e2, (ix, d) in tgt.items():
                    if known_eng[e2] >= ix:
                        continue
                    known_eng[e2] = ix
                    waits.append(d)
                for d in waits:
                    if not d.dma:
                        d.signaled = True
                ins.waits = waits
        for eng in ENGS:
            c = 0
            for ins in self.streams[eng]:
                if ins.dma:
                    continue
                if ins.fn is None:
                    ins.count = c
                    continue
                if ins.signaled:
                    c += 1
                    ins.count = c

    def dump(self, path):
        with open(path, 'w') as f:
            for eng in ENGS:
                f.write('==== %s\n' % eng)
                for ins in self.streams[eng]:
                    w = ['%s:%s' % (('dma%d' % d.semi) if d.dma else d.eng, d.dval if d.dma else '%d(c%d,L%d)' % (d.idx, d.count, d.tag)) for d in ins.waits]
                    f.write('%5d L%-4d %s%s sig=%d cnt=%d waits=%s\n' % (ins.idx, ins.tag, 'DMA(s%d,v%d) ' % (ins.semi, ins.dval) if ins.dma else '', 'NOP' if ins.fn is None else '', ins.signaled, ins.count, w))

    def emit(self):
        nc = self.nc
        if SCHEDULE:
            self.schedule()
        self.plan()
        if DUMP_PATH:
            self.dump(DUMP_PATH)
        with ExitStack() as es:
            esem = {e: es.enter_context(nc.semaphore("s_" + e)) for e in ENGS}
            dsem = [es.enter_context(nc.semaphore("d_%d" % i)) for i in range(self.n_dma_sems)]
            block = es.enter_context(nc.Block())

            def run(eng_name):
                def body(e):
                    for ins in self.streams[eng_name]:
                        for d in ins.waits:
                            if d.dma:
                                e.wait_ge(dsem[d.semi], d.dval)
                            else:
                                e.wait_ge(esem[d.eng], d.count)
                        if ins.fn is None:
                            continue
                        bi = ins.fn(e)
                        if ins.dma:
                            bi.then_inc(dsem[ins.semi], 16)
                        elif ins.signaled:
                            bi.then_inc(esem[eng_name], 1)
                return body

            block.tensor(run('pe'))
            block.scalar(run('act'))
            block.vector(run('dve'))
            block.gpsimd(run('pool'))
            block.sync(run('sp'))


class _Stop(Exception):
    pass


def build_program(stop=99, dbg=False):
    try:
        return _build(stop, dbg)
    except _Stop as e:
        return e.args[0]


def _build(stop, dbg):
    nc = bass.Bass("TRN2", target_bir_lowering=False)
    S = Sched(nc)
    din = lambda n, shp: nc.dram_tensor(n, list(shp), F32, kind="ExternalInput").ap()
    dout = lambda n, shp: nc.dram_tensor(n, list(shp), F32, kind="ExternalOutput").ap()
    dint = lambda n, shp: nc.dram_tensor(n, list(shp), F32, kind="Internal").ap()
    xp = din("xp", [TP, D]); xs = din("xs", [TS, D]); cc = din("cc", [17, D])
    sconv = din("sconv", [48, D]); shin = din("shin", [16, D])
    ck = din("ck", [16, 128, 256]); cv = din("cv", [16, 128, 256])
    adaP = din("adaP", [2, 6, 128, 8 * 1024]); rnn_w_in = din("rnn_w_in", [128, 8 * 2048]); gate_w = din("gate_w", [128, 8 * 256])
    rnn_w_out = din("rnn_w_out", [128, 8 * 1024]); kvadaP = din("kvadaP", [2, 128, 8 * 1024]); w_kv = din("w_kv", [128, 8 * 512])
    w_q = din("w_q", [128, 8 * 1024]); w_o = din("w_o", [128, 8 * 1024]); ffnP = din("ffnP", [2, 128, NHC * 3072])
    svd = din("sv", [SV_ROWS, 128]); identd = din("ident", [128, 128])
    cosd = din("cos", [128, 17, 8]); sind = din("sin", [128, 17, 8])
    maskpd = din("maskp", [128, 256]); masksd = din("masks", [128, 136])
    sinkbd = din("sinkb", [128, 16]); sinkcd = din("sinkc", [128, 1])
    y_p = dout("y_p", [TP, D]); y_s = dout("y_s", [TS, D]); pconv = dout("pconv", [3, D]); ph = dout("ph", [1, D])
    pk = dout("pk", [128, 256]); pv = dout("pv", [128, 256]); sconv_o = dout("sconv_o", [48, D]); sh_o = dout("sh_o", [16, D])
    sk = dout("sk", [16, 128, 256]); svo = dout("svo", [16, 128, 256])
    scr_v = dint("scr_v", [128, 256]); scr_o = dint("scr_o", [128, 1024])

    with ExitStack() as es:
        sb = lambda n, shp, dt=F32: es.enter_context(nc.sbuf_tensor(n, list(shp), dt))
        xT = sb("xT", [128, NCH, T])
        ring = sb("ring", [128, RING_SLOTS * SLOT], BF16)
        svT = sb("svT", [128, SV_ROWS]); ident = sb("ident_sb", [128, 128])
        ones = sb("ones", [128, 128], BF16)
        modT = sb("modT", [128, 48, 17]); kvmodT = sb("kvmodT", [128, 16, 17])
        cTc = sb("cTc", [128, NCH, 17], BF16)
        cost = sb("cost", [128, 17, 8]); sint = sb("sint", [128, 17, 8])
        maskp = sb("maskp_sb", [128, 256]); masks = sb("masks_sb", [128, 136])
        sinkb = sb("sinkb_sb", [128, 16]); sinkc = sb("sinkc_sb", [128, 1])
        rgc = sb("rgc", [128, 40])
        hstate = sb("hstate", [128, 8])
        arena = sb("arena", [128, ARENA_W + TAIL_W])
        pbs = [es.enter_context(nc.psum_tensor("pb%d" % i, [128, 512], F32)) for i in range(8)]

        class Arena:
            def __init__(self):
                self.off = 0
                self.limit = ARENA_W + TAIL_W
            def reset(self):
                self.off = 0
            def f32(self, shape):
                n = int(np.prod(shape[1:]))
                ap = arena[0:shape[0], self.off:self.off + n]
                self.off += n
                assert self.off <= self.limit, (self.off, self.limit)
                return _view(ap, shape)
            def bf16(self, shape):
                n = int(np.prod(shape[1:]))
                nw = (n + 1) // 2
                ap = arena[0:shape[0], self.off:self.off + nw].bitcast(BF16)[:, 0:n]
                self.off += nw
                assert self.off <= self.limit, (self.off, self.limit)
                return _view(ap, shape)

        def _view(ap, shape):
            if len(shape) == 2:
                return ap
            if len(shape) == 3:
                return ap.rearrange("p (a b) -> p a b", b=shape[2])
            if len(shape) == 4:
                return ap.rearrange("p (a b c) -> p a b c", b=shape[2], c=shape[3])
            raise ValueError

        AR = Arena()
        KT = arena[:, ARENA_W:ARENA_W + 2176].bitcast(BF16).rearrange("p (a b) -> p a b", b=T)
        Vb = arena[:, ARENA_W + 2176:ARENA_W + 4352].bitcast(BF16).rearrange("p (a b) -> p a b", b=256)

        def fsz(ap):
            n = 1
            for d in ap.shape[1:]:
                n *= int(d)
            return n

        def c_act(out):
            return 0.22 + fsz(out) / 1200.0

        def c_dve(out):
            return 0.08 + fsz(out) / 960.0

        def mm(out, lhsT, rhs, start, stop, r, w, **kw):
            f = 4.0 if rhs.dtype == F32 else 1.0
            cost = f * max(fsz(rhs), 64) / 2400.0 + 0.015
            S.op('pe', lambda e: e.matmul(out, lhsT, rhs, start=start, stop=stop, **kw), reads=r, writes=w, cost=cost)

        def tr(out, in_, r, w):
            n = in_.shape[0]
            cost = 0.12
            S.op('pe', lambda e: e.transpose(out, in_, ident[0:n, 0:n]), reads=list(r) + ['ident'], writes=w, cost=cost)

        def act(out, in_, func, r, w, bias=None, scale=None, accum=None):
            kw = {}
            if bias is not None: kw['bias'] = bias
            if scale is not None: kw['scale'] = scale
            if accum is not None: kw['accum_out'] = accum
            S.op('act', lambda e: e.activation(out, in_, func, **kw), reads=r, writes=w, cost=c_act(out))

        def tt(out, in0, in1, op, r, w):
            S.op('dve', lambda e: e.tensor_tensor(out, in0, in1, op), reads=r, writes=w, cost=c_dve(out))

        def ts(out, in0, s1, s2, op0, op1, r, w):
            if s2 is None:
                S.op('dve', lambda e: e.tensor_scalar(out, in0, s1, None, op0), reads=r, writes=w, cost=c_dve(out))
            else:
                S.op('dve', lambda e: e.tensor_scalar(out, in0, s1, s2, op0, op1), reads=r, writes=w, cost=c_dve(out))

        def stt(out, in0, scalar, in1, op0, op1, r, w):
            S.op('dve', lambda e: e.scalar_tensor_tensor(out, in0, scalar, in1, op0, op1), reads=r, writes=w, cost=1.2 * c_dve(out))

        def cp(eng, out, in_, r, w):
            if eng == 'act':
                S.op('act', lambda e: e.copy(out, in_), reads=r, writes=w, cost=c_act(out))
            else:
                S.op('dve', lambda e: e.tensor_copy(out, in_), reads=r, writes=w, cost=c_dve(out))

        def dma(q, out, in_, r, w, sem_b=False):
            nbytes = 4.0 * max(fsz(out), fsz(in_)) * min(int(out.shape[0]), 128)
            S.op(q, lambda e: e.dma_start(out=out, in_=in_), reads=r, writes=w, dma=True, cost=nbytes / 220e3, sem_b=sem_b)

        pb_rr = [0]
        pb_pool = [(0, 1, 2, 3, 4, 5, 6, 7)]
        def pbank():
            pool = pb_pool[0]
            i = pool[pb_rr[0] % len(pool)]
            pb_rr[0] += 1
            return i

        ring_pos = [0]
        def wload(dram_ap, shape, slot0=None):
            n = int(np.prod(shape[1:]))
            ns = (n + SLOT - 1) // SLOT
            if slot0 is not None:
                ring_pos[0] = slot0
            if ring_pos[0] + ns > RING_SLOTS:
                ring_pos[0] = 0
            s0 = ring_pos[0]
            ring_pos[0] += ns
            view = _view(ring[:, s0 * SLOT: s0 * SLOT + n], shape)
            res = [('ws', i) for i in range(s0, s0 + ns)]
            dma('pool', view, dram_ap, [], res)
            return view, res

        def ckpt(x):
            if stop <= x:
                raise _Stop(_finish())

        def _finish():
            if dbg:
                S.barrier()
                dbg_x = nc.dram_tensor("dbg_x", [128, NCH * T], F32, kind="ExternalOutput").ap()
                dma('sp', dbg_x, xT[:, :, :].rearrange("p c t -> p (c t)"), [], [])
                dbg_m = nc.dram_tensor("dbg_m", [128, 48 * 17], F32, kind="ExternalOutput").ap()
                dma('sp', dbg_m, modT[:, :, :].rearrange("p c t -> p (c t)"), [], [])
            S.finish()
            S.emit()
            return nc

        dma('sp', ident[:], identd, [], ['ident'])
        dma('sp', cost[:], cosd, [], ['rope']); dma('sp', sint[:], sind, [], ['rope'])
        dma('sp', maskp[:], maskpd, [], ['mask']); dma('sp', masks[:], masksd, [], ['mask'])
        dma('sp', sinkb[:], sinkbd, [], ['sink']); dma('sp', sinkc[:], sinkcd, [], ['sink'])
        S.op('dve', lambda e: e.memset(ones[:], 1.0), writes=['ones'])
        S.op('dve', lambda e: e.memset(hstate[:], 0.0), writes=['hstate'])
        AR.reset()
        sva = AR.f32([112, 128]); svb = AR.f32([112, 128]); cin = AR.f32([17, 1024]); csl = AR.f32([17, 1024])
        xin = [AR.f32([128, 1024]), AR.f32([128, 1024])]
        dma('sp', sva, svd[0:112, :], [], ['sva']); dma('sp', svb, svd[112:224, :], [], ['svb'])
        dma('sp', cin, cc, [], ['cin'])
        b0 = pbank()
        tr(pbs[b0][:, 0:112], sva, ['sva'], [('pb', b0)])
        tr(pbs[b0][:, 112:224], svb, ['svb'], [('pb', b0)])
        cp('dve', svT[:, :], pbs[b0][:, 0:224], [('pb', b0)], ['svT'])
        act(csl, cin, AF.Silu, ['cin'], ['csl'])
        b0 = pbank()
        for k in range(NCH):
            tr(pbs[b0][:, k * 17:(k + 1) * 17], csl[:, k * 128:(k + 1) * 128], ['csl'], [('pb', b0)])
        cp('dve', cTc[:, :, :], pbs[b0][:, 0:136].rearrange("p (k n) -> p k n", n=17), [('pb', b0)], ['cTc'])
        for t in range(17):
            xi = xin[t % 2]
            src = xp[t * 128:(t + 1) * 128, :] if t < 16 else xs
            dma('sp', xi, src, [], [('xin', t % 2)])
            for hf in range(2):
                b0 = pbank()
                for j in range(4):
                    c = hf * 4 + j
                    tr(pbs[b0][:, j * 128:(j + 1) * 128], xi[:, c * 128:(c + 1) * 128], [('xin', t % 2)], [('pb', b0)])
                cp('act' if hf == 0 else 'dve', xT[:, hf * 4:hf * 4 + 4, t * 128:(t + 1) * 128],
                   pbs[b0][:, :].rearrange("p (j n) -> p j n", n=128), [('pb', b0)],
                   [('xT', c, min(t // 4, 4)) for c in range(hf * 4, hf * 4 + 4)])
        act(rgc[:, 32:40], svT[:, SV_LM:SV_LM + 8], AF.Exp, ['svT'], ['rgc_t'], scale=-1.0)
        act(rgc[:, 32:40], rgc[:, 32:40], AF.Ln, ['rgc_t'], ['rgc_t'], bias=1.0)
        act(rgc[:, 0:8], rgc[:, 32:40], AF.Identity, ['rgc_t'], ['rgc'], scale=-8.0)
        act(rgc[:, 8:16], rgc[:, 32:40], AF.Identity, ['rgc_t'], ['rgc'], scale=-16.0)
        act(rgc[:, 16:32], svT[:, SV_GB:SV_GB + 16], AF.Identity, ['svT'], ['rgc'], scale=-1.0)

        def compute_mod(w_dram, vlist, dst, bias_row0, resfn, slot0=None):
            for v in vlist:
                bk = pbank()
                wv, wres = wload(w_dram[v].rearrange("p (k n) -> p k n", k=NCH), [128, NCH, 1024], slot0=slot0)
                for c in range(NCH):
                    o = pbs[bk][:, c * 17:(c + 1) * 17]
                    for k in range(NCH):
                        mm(o, wv[:, k, c * 128:(c + 1) * 128], cTc[:, k, :], k == 0, k == NCH - 1, wres + ['cTc'], [('pb', bk)])
                tt(dst[:, v * 8:(v + 1) * 8, :], pbs[bk][:, 0:136].rearrange("p (a b) -> p a b", b=17),
                   svT[:, bias_row0 + v * 8: bias_row0 + v * 8 + 8].unsqueeze(2).to_broadcast([128, 8, 17]), ALU.add,
                   [('pb', bk), 'svT'], [resfn(v)])

        def fold_scale(dst, sc0, g_row0, res):
            stt(dst[:, sc0:sc0 + 8, :], dst[:, sc0:sc0 + 8, :], 1.0,
                svT[:, g_row0:g_row0 + 8].unsqueeze(2).to_broadcast([128, 8, 17]), ALU.add, ALU.mult, [res, 'svT'], [res])

        def compute_mod_tail(w_dram, nvec, dst, bias_row0, res, tbs, tmp, tmpres):
            for v in range(nvec):
                bk = pbank()
                for half in range(2):
                    tb = tbs[half]; tres = ('tailb', half)
                    dma('pool', tb, w_dram[v][:, half * 4096:(half + 1) * 4096].rearrange("p (k n) -> p k n", k=4), [('hTf', 0, 0)], [tres])
                    for c in range(NCH):
                        o = pbs[bk][:, half * 136 + c * 17:half * 136 + (c + 1) * 17]
                        for k in range(4):
                            mm(o, tb[:, k, c * 128:(c + 1) * 128], cTc[:, half * 4 + k, :], k == 0, k == 3, [tres, 'cTc'], [('pb', bk)])
                tt(tmp, pbs[bk][:, 0:136].rearrange("p (a b) -> p a b", b=17),
                   svT[:, bias_row0 + v * 8: bias_row0 + v * 8 + 8].unsqueeze(2).to_broadcast([128, 8, 17]), ALU.add, [('pb', bk), 'svT'], [tmpres])
                tt(dst[:, v * 8:(v + 1) * 8, :], tmp, pbs[bk][:, 136:272].rearrange("p (a b) -> p a b", b=17), ALU.add, [('pb', bk), tmpres], [res])

        def bc_s(v):
            return v.unsqueeze(2).to_broadcast([128, 16, 8])

        def v3(ap):
            return ap.rearrange("p (s t) -> p s t", t=8)

        def norm_group(g, A, B, modres, hdst, hres, X, out_f32_scale=None, bank=None):
            t0, n = GROUPS[g]
            mr = list(modres) if isinstance(modres, list) else [modres]
            bk = pbank() if bank is None else bank
            for c in range(NCH):
                q = X[c % 2][:, :].bitcast(BF16); qr = 'X%d' % (c % 2)
                act(q[:, 0:n], xT[:, c, t0:t0 + n], AF.Square, [('xT', c, g)], [qr])
                mm(pbs[bk][:, 0:n], ones[:, :], q[:, 0:n], c == 0, c == NCH - 1, ['ones', qr], [('pb', bk)])
            rstd = X[2]
            act(rstd[:, 0:n], pbs[bk][:, 0:n], AF.Ln, [('pb', bk)], ['X2'], scale=1.0 / D, bias=EPS)
            act(rstd[:, 0:n], rstd[:, 0:n], AF.Exp, ['X2'], ['X2'], scale=-0.5)
            for c in range(NCH):
                tb = X[c % 2]; tr_ = 'X%d' % (c % 2)
                tt(tb[:, 0:n], xT[:, c, t0:t0 + n], rstd[:, 0:n], ALU.mult, [('xT', c, g), 'X2'], [tr_])
                if out_f32_scale is not None:
                    act(hdst[:, c, 0:n], tb[:, 0:n], AF.Identity, [tr_, 'svT'], [hres(c)], scale=out_f32_scale(c))
                elif g < 4:
                    act(hdst[:, c, 0:n], tb[:, 0:n], AF.Identity, [tr_] + mr, [hres(c)],
                        scale=A[:, c, 0:1], bias=B[:, c, 0:1])
                else:
                    tt(v3(tb[:, 0:n]), v3(tb[:, 0:n]), bc_s(A[:, c, 1:17]), ALU.mult, [tr_] + mr, [tr_])
                    tt(v3(hdst[:, c, 0:n]), v3(tb[:, 0:n]), bc_s(B[:, c, 1:17]), ALU.add, [tr_] + mr, [hres(c)])

        def residual(g, fc, bk, Gm, modres, tmp, tmpres):
            t0, n = GROUPS[g]
            mr = list(modres) if isinstance(modres, list) else [modres]
            if g < 4:
                stt(xT[:, fc, t0:t0 + n], pbs[bk][:, 0:n], Gm[:, fc, 0:1], xT[:, fc, t0:t0 + n], ALU.mult, ALU.add,
                    [('pb', bk), ('xT', fc, g)] + mr, [('xT', fc, g)])
            else:
                tt(v3(tmp[:, 0:n]), v3(pbs[bk][:, 0:n]), bc_s(Gm[:, fc, 1:17]), ALU.mult, [('pb', bk)] + mr, [tmpres])
                tt(xT[:, fc, t0:t0 + n], xT[:, fc, t0:t0 + n], tmp[:, 0:n], ALU.add, [tmpres, ('xT', fc, g)], [('xT', fc, g)])

        def ffn_stage(l):
            S.barrier()
            AR.reset()
            hT = AR.bf16([128, NCH, T])
            hid = [AR.bf16([128, 4, 512]), AR.bf16([128, 4, 512])]
            X = [AR.f32([128, 512]), AR.f32([128, 512]), AR.f32([128, 512])]
            A = modT[:, 32:40, :]; B = modT[:, 24:32, :]; Gm = modT[:, 40:48, :]
            for g in range(5):
                t0, n = GROUPS[g]
                norm_group(g, A, B, [('mod', 3), ('mod', 4)], hT[:, :, t0:t0 + n], (lambda g: lambda c: ('hTf', c, g))(g), X)
            if l == 0:
                tbs = [AR.bf16([128, 4, 1024]), AR.bf16([128, 4, 1024])]
                modT1 = AR.f32([128, 48, 17]); mtmp = AR.f32([128, 8, 17])
                compute_mod_tail(kvadaP, 2, kvmodT, SV_KVB, 'kvmod', tbs, mtmp, 'mtmp')
                fold_scale(kvmodT, 8, SV_KVG, 'kvmod')
                compute_mod_tail(adaP[1], 6, modT1, SV_ADAB + 48, 'modT1', tbs, mtmp, 'mtmp')
                fold_scale(modT1, 8, SV_NG + 16, 'modT1')
                fold_scale(modT1, 32, SV_NG + 24, 'modT1')
            it = 0
            for (h0, hc) in FFN_GROUPS:
                o0 = h0 * 3072
                wg, rg_ = wload(ffnP[l][:, o0:o0 + hc * 1024].rearrange("p (k n) -> p k n", k=NCH), [128, NCH, hc * 128])
                wu, ru_ = wload(ffnP[l][:, o0 + hc * 1024:o0 + hc * 2048].rearrange("p (k n) -> p k n", k=NCH), [128, NCH, hc * 128])
                wo, ro_ = wload(ffnP[l][:, o0 + hc * 2048:o0 + hc * 3072].rearrange("p (j n) -> p j n", j=hc), [128, hc, 1024])
                for g in range(5):
                    t0, n = GROUPS[g]
                    hb = hid[it % 2]; hres = ('hid', it % 2); it += 1
                    for j in range(hc):
                        bg = pbank(); bu = pbank()
                        for k in range(NCH):
                            mm(pbs[bg][:, 0:n], wg[:, k, j * 128:(j + 1) * 128], hT[:, k, t0:t0 + n], k == 0, k == NCH - 1,
                               rg_ + [('hTf', k, g)], [('pb', bg)])
                        for k in range(NCH):
                            mm(pbs[bu][:, 0:n], wu[:, k, j * 128:(j + 1) * 128], hT[:, k, t0:t0 + n], k == 0, k == NCH - 1,
                               ru_ + [('hTf', k, g)], [('pb', bu)])
                        s_ = X[j % 2]; sr = 'X%d' % (j % 2)
                        act(s_[:, 0:n], pbs[bg][:, 0:n], AF.Silu, [('pb', bg)], [sr])
                        tt(hb[:, j, 0:n], s_[:, 0:n], pbs[bu][:, 0:n], ALU.mult, [sr, ('pb', bu)], [hres])
                    for fc in range(NCH):
                        bo = pbank()
                        for j in range(hc):
                            mm(pbs[bo][:, 0:n], wo[:, j, fc * 128:(fc + 1) * 128], hb[:, j, 0:n], j == 0, j == hc - 1,
                               ro_ + [hres], [('pb', bo)])
                        residual(g, fc, bo, Gm, [('mod', 5)], X[2], 'X2')
            if l == 0:
                cp('dve', modT[:, :, :], modT1, ['modT1'], [('mod', v) for v in range(6)])

        if stop <= 0:
            return _finish()
        S.stage = 'mod0'
        mres = lambda v: ('mod', v)
        compute_mod(adaP[0], [0, 1, 2], modT, SV_ADAB, mres)
        fold_scale(modT, 8, SV_NG + 0, ('mod', 1))
        ckpt(0.2)
        S.barrier()
        AR.reset()
        hT2 = [AR.bf16([128, NCH, 512]), AR.bf16([128, NCH, 512])]
        xpad = AR.f32([128, NCH, 515])
        xpads = xpad[:, :, 256:432].rearrange("p c (s j) -> p c s j", j=11)
        oT = AR.bf16([128, NCH, 512])
        offX = AR.off
        X = [AR.f32([128, 512]), AR.f32([128, 512]), AR.f32([128, 512])]
        stio2 = arena[0:48, offX:offX + 1024]
        xc2 = [AR.f32([128, 512]), AR.f32([128, 512])]; xcb2 = [AR.bf16([128, 512]), AR.bf16([128, 512])]
        ta2 = [AR.f32([128, 512]), AR.f32([128, 512])]; tb2 = [AR.f32([128, 512]), AR.f32([128, 512])]
        ti2 = [AR.f32([128, 512]), AR.f32([128, 512])]; gq2 = [AR.f32([128, 512]), AR.f32([128, 512])]
        blk2 = [AR.f32([128, 1024]), AR.f32([128, 1024])]
        stio = blk2[0][0:48, :]; ti = ti2[0]
        h0T = AR.f32([128, NCH, 16]); tcc = AR.f32([128, 48])
        S.op('dve', lambda e: e.memset(xpad[:, :, 0:3], 0.0), writes=[('xpad', c) for c in range(NCH)])
        dma('sp', stio[0:16, :], shin, [], ['stio'])
        b0 = pbank()
        for c in range(NCH):
            tr(pbs[b0][:, c * 16:(c + 1) * 16], stio[0:16, c * 128:(c + 1) * 128], ['stio'], [('pb', b0)])
        cp('dve', h0T[:, :, :], pbs[b0][:, 0:128].rearrange("p (c s) -> p c s", s=16), [('pb', b0)], ['h0T'])
        S.barrier()

        ckpt(0.3)
        S.stage = 'rglru'
        win, rwin = wload(rnn_w_in.rearrange("p (k n) -> p k n", k=NCH), [128, NCH, 2048], slot0=0)
        wgt, rwgt = wload(gate_w.rearrange("p (k n) -> p k n", k=8), [128, 8, 256], slot0=8)
        compute_mod(adaP[0], [3, 4, 5], modT, SV_ADAB, mres, slot0=9)
        fold_scale(modT, 32, SV_NG + 8, ('mod', 4))
        wout, rwout = wload(rnn_w_out.rearrange("p (k n) -> p k n", k=NCH), [128, NCH, 1024], slot0=9)
        A1 = modT[:, 8:16, :]; B1 = modT[:, 0:8, :]; G1 = modT[:, 16:24, :]
        for g in range(5):
            t0, n = GROUPS[g]
            hT = hT2[g % 2]
            pb_pool[0] = (0, 1, 2, 3, 4, 5, 6)
            norm_group(g, A1, B1, [('mod', 0), ('mod', 1)], hT, (lambda gp: lambda c: ('hT', gp, c))(g % 2), X, bank=7)
            ckpt(0.4 + g * 0.1)
            if g == 4:
                dma('sp', stio2, sconv, [], ['X0', 'X1'])
                b0 = pbank()
                for c in range(NCH):
                    tr(pbs[b0][:, c * 48:(c + 1) * 48], stio2[:, c * 128:(c + 1) * 128], ['X0', 'X1'], [('pb', b0)])
                cp('dve', xpads[:, :, :, 0:3], pbs[b0][:, 0:384].rearrange("p (c s j) -> p c s j", s=16, j=3), [('pb', b0)],
                   [('xpad', c) for c in range(NCH)])
            for c in range(NCH):
                ckpt(0.4 + g * 0.1 + 0.01 * (c + 1))
                pz = c % 2
                xc = xc2[pz]; xcb = xcb2[pz]; ta = ta2[pz]; tb_ = tb2[pz]; ti = ti2[pz]; gq = gq2[pz]
                uu = blk2[pz][:, 0:512]; hh = blk2[pz][:, 512:1024]
                Rxc = 'xc%d' % pz; Rxcb = 'xcb%d' % pz; Rta = 'ta%d' % pz; Rtb = 'tb%d' % pz; Rti = 'ti%d' % pz
                Rgq = 'gq%d' % pz; Ruu = 'uu%d' % pz; Rhh = 'hh%d' % pz
                bx = pbank()
                for k in range(NCH):
                    mm(pbs[bx][:, 0:n], win[:, k, c * 128:(c + 1) * 128], hT[:, k, 0:n], k == 0, k == NCH - 1, rwin + [('hT', g % 2, k)], [('pb', bx)])
                w_ = (lambda c: lambda kk: svT[:, SV_CW + kk * 8 + c: SV_CW + kk * 8 + c + 1])(c)
                cb_ = svT[:, SV_CB + c:SV_CB + c + 1]
                if g < 4:
                    cp('dve', xpad[:, c, 3:3 + n], pbs[bx][:, 0:n], [('pb', bx)], [('xpad', c)])
                    act(xc[:, 0:n], pbs[bx][:, 0:n], AF.Identity, [('pb', bx), 'svT'], [Rxc], scale=w_(3), bias=cb_)
                    for kk in range(3):
                        stt(xc[:, 0:n], xpad[:, c, kk:kk + n], w_(kk), xc[:, 0:n], ALU.mult, ALU.add, [('xpad', c), Rxc, 'svT'], [Rxc])
                    cp('dve', tcc[:, 0:3], xpad[:, c, n:n + 3], [('xpad', c)], ['tcc'])
                    cp('dve', xpad[:, c, 0:3], tcc[:, 0:3], ['tcc'], [('xpad', c)])
                else:
                    cp('dve', xpads[:, c, :, 3:11], v3(pbs[bx][:, 0:n]), [('pb', bx)], [('xpad', c)])
                    act(xc[:, 0:n], pbs[bx][:, 0:n], AF.Identity, [('pb', bx), 'svT'], [Rxc], scale=w_(3), bias=cb_)
                    for kk in range(3):
                        stt(v3(xc[:, 0:n]), xpads[:, c, :, kk:kk + 8], w_(kk), v3(xc[:, 0:n]), ALU.mult, ALU.add, [('xpad', c), Rxc, 'svT'], [Rxc])
                cp('dve', xcb[:, 0:n], xc[:, 0:n], [Rxc], [Rxcb])
                ckpt(0.411)
                br = pbank(); bi = pbank()
                mm(pbs[br][:, 0:n], wgt[:, c, 0:128], xcb[:, 0:n], True, True, rwgt + [Rxcb], [('pb', br)])
                mm(pbs[bi][:, 0:n], wgt[:, c, 128:256], xcb[:, 0:n], True, True, rwgt + [Rxcb], [('pb', bi)])
                ckpt(0.412)
                act(ta[:, 0:n], pbs[br][:, 0:n], AF.Exp, [('pb', br), 'rgc'], [Rta], scale=-1.0, bias=rgc[:, 16 + 2 * c:17 + 2 * c])
                act(ta[:, 0:n], ta[:, 0:n], AF.Ln, [Rta], [Rta], bias=1.0)
                act(ta[:, 0:n], ta[:, 0:n], AF.Exp, [Rta], [Rta], scale=-1.0)
                act(tb_[:, 0:n], ta[:, 0:n], AF.Exp, [Rta, 'rgc'], [Rtb], scale=rgc[:, 8 + c:9 + c])
                act(ta[:, 0:n], ta[:, 0:n], AF.Exp, [Rta, 'rgc'], [Rta], scale=rgc[:, c:c + 1])
                act(tb_[:, 0:n], tb_[:, 0:n], AF.Ln, [Rtb], [Rtb], scale=-1.0, bias=1.0)
                act(tb_[:, 0:n], tb_[:, 0:n], AF.Exp, [Rtb], [Rtb], scale=0.5)
                ckpt(0.413)
                act(ti[:, 0:n], pbs[bi][:, 0:n], AF.Exp, [('pb', bi), 'rgc'], [Rti], scale=-1.0, bias=rgc[:, 17 + 2 * c:18 + 2 * c])
                act(ti[:, 0:n], ti[:, 0:n], AF.Ln, [Rti], [Rti], bias=1.0)
                act(ti[:, 0:n], ti[:, 0:n], AF.Exp, [Rti], [Rti], scale=-1.0)
                tt(uu[:, 0:n], tb_[:, 0:n], ti[:, 0:n], ALU.mult, [Rtb, Rti], [Ruu])
                tt(uu[:, 0:n], uu[:, 0:n], xc[:, 0:n], ALU.mult, [Ruu, Rxc], [Ruu])
                ckpt(0.414)
                if g < 4:
                    S.op('dve', (lambda c=c, n=n, hh=hh, ta=ta, uu=uu: lambda e: e.tensor_tensor_scan(hh[:, 0:n], ta[:, 0:n], uu[:, 0:n], hstate[:, c:c + 1], ALU.mult, ALU.add))(),
                         reads=[Rta, Ruu, ('hstate', c)], writes=[Rhh], cost=0.1 + n / 420.0)
                    cp('dve', hstate[:, c:c + 1], hh[:, n - 1:n], [Rhh], [('hstate', c)])
                else:
                    a3 = v3(ta[:, 0:n]); u3 = v3(uu[:, 0:n])
                    tt(tcc[:, 0:16], a3[:, :, 0], h0T[:, c, :], ALU.mult, [Rta, 'h0T'], ['tcc'])
                    tt(u3[:, :, 0], u3[:, :, 0], tcc[:, 0:16], ALU.add, [Ruu, 'tcc'], [Ruu])
                    S.op('dve', (lambda a3=a3: lambda e: e.memset(a3[:, :, 0:1], 0.0))(), reads=['tcc'], writes=[Rta])
                    S.op('dve', (lambda n=n, hh=hh, ta=ta, uu=uu: lambda e: e.tensor_tensor_scan(hh[:, 0:n], ta[:, 0:n], uu[:, 0:n], 0.0, ALU.mult, ALU.add))(),
                         reads=[Rta, Ruu], writes=[Rhh], cost=0.1 + n / 420.0)
                    cp('dve', tcc[:, 0:16], v3(hh[:, 0:n])[:, :, 7], [Rhh], ['tcc'])
                    bt = pbank()
                    tr(pbs[bt][0:16, 0:128], tcc[:, 0:16], ['tcc'], [('pb', bt)])
                    cp('dve', xpad[0:16, c, 128:256], pbs[bt][0:16, 0:128], [('pb', bt)], [('xpad', c)])
                    dma('sp', sh_o[:, c * 128:(c + 1) * 128], xpad[0:16, c, 128:256], [('xpad', c)], [])
                    cp('dve', tcc[:, 0:48].rearrange("p (s j) -> p s j", j=3), xpads[:, c, :, 8:11], [('xpad', c)], ['tcc'])
                    bt = pbank()
                    tr(pbs[bt][0:48, 0:128], tcc[:, 0:48], ['tcc'], [('pb', bt)])
                    cp('dve', xpad[0:48, c, 0:128], pbs[bt][0:48, 0:128], [('pb', bt)], [('xpad', c)])
                ckpt(0.415)
                by = pbank()
                for k in range(NCH):
                    mm(pbs[by][:, 0:n], win[:, k, 1024 + c * 128:1024 + (c + 1) * 128], hT[:, k, 0:n], k == 0, k == NCH - 1, rwin + [('hT', g % 2, k)], [('pb', by)])
                act(gq[:, 0:n], pbs[by][:, 0:n], AF.Square, [('pb', by)], [Rgq])
                ts(gq[:, 0:n], gq[:, 0:n], 0.044715, 1.0, ALU.mult, ALU.add, [Rgq], [Rgq])
                tt(gq[:, 0:n], gq[:, 0:n], pbs[by][:, 0:n], ALU.mult, [Rgq, ('pb', by)], [Rgq])
                act(gq[:, 0:n], gq[:, 0:n], AF.Exp, [Rgq], [Rgq], scale=-1.5957691216057308)
                act(gq[:, 0:n], gq[:, 0:n], AF.Ln, [Rgq], [Rgq], bias=1.0)
                act(gq[:, 0:n], gq[:, 0:n], AF.Exp, [Rgq], [Rgq], scale=-1.0)
                tt(gq[:, 0:n], gq[:, 0:n], pbs[by][:, 0:n], ALU.mult, [Rgq, ('pb', by)], [Rgq])
                tt(oT[:, c, 0:n], hh[:, 0:n], gq[:, 0:n], ALU.mult, [Rhh, Rgq], [('oT', c)])
                ckpt(0.416)
            for fc in range(NCH):
                bo = pbank()
                for c in range(NCH):
                    mm(pbs[bo][:, 0:n], wout[:, c, fc * 128:(fc + 1) * 128], oT[:, c, 0:n], c == 0, c == NCH - 1, rwout + [('oT', c)], [('pb', bo)])
                residual(g, fc, bo, G1, [('mod', 2)], X[2], 'X2')
            if g == 3:
                for hf in range(2):
                    b1 = pbank()
                    for j in range(4):
                        c = hf * 4 + j
                        tr(pbs[b1][0:3, j * 128:(j + 1) * 128], xpad[:, c, 0:3], [('xpad', c)], [('pb', b1)])
                    cp('dve', ti2[0][0:3, 0:512], pbs[b1][0:3, 0:512], [('pb', b1)], ['ti0'])
                    dma('sp', pconv[:, hf * 512:(hf + 1) * 512], ti2[0][0:3, 0:512], ['ti0'], [])
                for hf in range(2):
                    b1 = pbank()
                    for j in range(4):
                        c = hf * 4 + j
                        tr(pbs[b1][0:1, j * 128:(j + 1) * 128], hstate[:, c:c + 1], [('hstate', c)], [('pb', b1)])
                    cp('dve', ti2[0][0:1, 0:512], pbs[b1][0:1, 0:512], [('pb', b1)], ['ti0'])
                    dma('sp', ph[:, hf * 512:(hf + 1) * 512], ti2[0][0:1, 0:512], ['ti0'], [])
        for c in range(NCH):
            dma('sp', sconv_o[:, c * 128:(c + 1) * 128], xpad[0:48, c, 0:128], [('xpad', c)], [])

        if stop <= 1:
            return _finish()
        pb_pool[0] = (0, 1, 2, 3, 4, 5, 6, 7)
        S.stage = 'ffn0'
        ffn_stage(0)

        if stop <= 2:
            return _finish()
        S.stage = 'kv'
        S.barrier()
        AR.reset()
        AR.limit = ARENA_W
        hT = AR.bf16([128, NCH, 512])
        X = [AR.f32([128, 512]), AR.f32([128, 512]), AR.f32([128, 512])]
        Kf = [AR.f32([128, 256]), AR.f32([128, 256])]; Vf = [AR.f32([128, 256]), AR.f32([128, 256])]
        rt_kv = AR.f32([128, 4, 16, 8])

        def rope(buf3, bres, t, H, rt_=None):
            rt = rt_ if rt_ is not None else rt_kv
            cs = cost[:, t, :].unsqueeze(1).to_broadcast([128, H, 8])
            sn = sint[:, t, :].unsqueeze(1).to_broadcast([128, H, 8])
            x1 = buf3[:, :, 0:8]; x2 = buf3[:, :, 8:16]
            t1 = rt[:, 0, 0:H, :]; t2 = rt[:, 1, 0:H, :]; t3 = rt[:, 2, 0:H, :]; t4 = rt[:, 3, 0:H, :]
            tt(t1, x1, cs, ALU.mult, [bres, 'rope'], ['rt1'])
            tt(t2, x2, sn, ALU.mult, [bres, 'rope'], ['rt2'])
            tt(t3, x2, cs, ALU.mult, [bres, 'rope'], ['rt3'])
            tt(t4, x1, sn, ALU.mult, [bres, 'rope'], ['rt4'])
            tt(x1, t1, t2, ALU.subtract, ['rt1', 'rt2'], [bres])
            tt(x2, t3, t4, ALU.add, ['rt3', 'rt4'], [bres])

        wkv, rkv = wload(w_kv.rearrange("p (k n) -> p k n", k=NCH), [128, NCH, 512])
        for g in range(5):
            t0, n = GROUPS[g]
            pb_pool[0] = (0, 1, 2, 3, 4, 5, 6)
            norm_group(g, kvmodT[:, 8:16, :], kvmodT[:, 0:8, :], 'kvmod', hT, lambda c: ('hT', c), X, bank=7)
            for tl in range(n // 128):
                t = t0 // 128 + tl
                bk = pbank()
                for k in range(NCH):
                    mm(pbs[bk][:, 0:512], hT[:, k, tl * 128:(tl + 1) * 128], wkv[:, k, :], k == 0, k == NCH - 1, rkv + [('hT', k)], [('pb', bk)])
                kf = Kf[t % 2]; vf = Vf[t % 2]; kr = ('Kf', t % 2); vr = ('Vf', t % 2)
                cp('act', kf[:, :], pbs[bk][:, 0:256], [('pb', bk)], [kr])
                cp('act', vf[:, :], pbs[bk][:, 256:512], [('pb', bk)], [vr])
                rope(kf.rearrange("p (h d) -> p h d", d=64), kr, t, 4)
                cp('dve', Vb[:, t, :], vf[:, :], [vr], [('Vb', t)])
                for gp in range(2):
                    bt = pbank()
                    tr(pbs[bt][:, 0:128], kf[:, gp * 128:(gp + 1) * 128], [kr], [('pb', bt)])
                    cp('act', KT[:, gp, t * 128:(t + 1) * 128], pbs[bt][:, 0:128], [('pb', bt)], [('KT', t)])
                if t == 15:
                    dma('sp', pk, kf[:, :], [kr], []); dma('sp', pv, vf[:, :], [vr], [])
                if t == 16:
                    for s in range(16):
                        dma('sp', sk[s, 120:128, :], kf[8 * s:8 * s + 8, :], [kr], [])
                        dma('sp', svo[s, 120:128, :], vf[8 * s:8 * s + 8, :], [vr], [])
        dma('sp', sk[:, 0:120, :], ck[:, 8:128, :], [], [])
        dma('sp', svo[:, 0:120, :], cv[:, 8:128, :], [], [])

        if stop <= 3:
            return _finish()
        S.stage = 'attn'
        wq, rwq = wload(w_q.rearrange("p (k n) -> p k n", k=NCH), [128, NCH, 1024])
        wo_, rwo = wload(w_o.rearrange("p (k n) -> p k n", k=NCH), [128, NCH, 1024])
        A1 = modT[:, 8:16, :]; B1 = modT[:, 0:8, :]; G1 = modT[:, 16:24, :]

        def attn_phase(groups, NKMAX, sample, shared=None):
            if sample:
                S.barrier(engines=('pe', 'act', 'dve', 'sp', 'pool'))
                AR.reset()
            NBUF = 2 if sample else 6
            pb_pool[0] = (2, 3, 4, 5, 6, 7)
            ncol = 512 if not sample else 128
            if shared is not None:
                hT, X = shared
            else:
                hT = AR.bf16([128, NCH, ncol])
                X = [AR.f32([128, ncol]), AR.f32([128, ncol]), AR.f32([128, ncol])]
            NT = 1 if sample else 2
            Qf2 = [AR.f32([128, 1024]) for _ in range(NT)]; QT2 = [AR.bf16([128, 8, 128]) for _ in range(NT)]
            Of2 = [AR.f32([128, 1024]) for _ in range(NT)]; OT = AR.bf16([128, NCH, ncol])
            sm2 = [AR.f32([128, 6, 16]) for _ in range(NT)]
            Sm = [AR.f32([128, NKMAX * 128]) for _ in range(NBUF)]
            ET = [AR.bf16([128, NKMAX, 128]) for _ in range(NBUF)]
            if sample:
                KTc = AR.bf16([128, 2, 2048]); Vc = AR.bf16([128, 16, 256])
                ckf = [AR.f32([128, 256]), AR.f32([128, 256])]
                dma('pool', Vc[:, :, :], cv.rearrange("s k d -> k s d"), [], ['Vc'])
                for s in range(16):
                    cb = ckf[s % 2]; cr = ('ckf', s % 2)
                    dma('sp', cb[:, :], ck[s], [], [cr])
                    for gp in range(2):
                        bt = pbank()
                        tr(pbs[bt][:, 0:128], cb[:, gp * 128:(gp + 1) * 128], [cr], [('pb', bt)])
                        cp('act' if gp == 0 else 'dve', KTc[:, gp, s * 128:(s + 1) * 128], pbs[bt][:, 0:128], [('pb', bt)], [('KTc', s)])
            for g in groups:
                t0, n = GROUPS[g]
                norm_group(g, A1, B1, [('mod', 0), ('mod', 1)], hT, lambda c: ('hT', c), X)
                for tl in range(n // 128):
                    t = t0 // 128 + tl
                    tp = t % NT
                    Qf = Qf2[tp]; QT = QT2[tp]; Of = Of2[tp]; sm = sm2[tp]
                    RQf = 'Qf%d' % tp; RQT = 'QT%d' % tp; ROf = 'Of%d' % tp; RS = 'sm%d' % tp
                    for hf in range(2):
                        bq = pbank()
                        for k in range(NCH):
                            mm(pbs[bq][:, 0:512], hT[:, k, tl * 128:(tl + 1) * 128], wq[:, k, hf * 512:(hf + 1) * 512], k == 0, k == NCH - 1,
                               rwq + [('hT', k)], [('pb', bq)])
                        act(Qf.rearrange("p (j h d) -> p j h d", h=2, d=64)[:, hf * 4:(hf + 1) * 4, :, :],
                            pbs[bq][:, 0:512].rearrange("p (h b d) -> p b h d", h=2, b=4), AF.Identity, [('pb', bq)], [RQf], scale=0.125)
                    rope(Qf.rearrange("p (h d) -> p h d", d=64), RQf, t, 16)
                    for hf in range(2):
                        bt = pbank()
                        for jj in range(4):
                            j = hf * 4 + jj
                            tr(pbs[bt][:, jj * 128:(jj + 1) * 128], Qf[:, j * 128:(j + 1) * 128], [RQf], [('pb', bt)])
                        cp('act', QT[:, hf * 4:hf * 4 + 4, :], pbs[bt][:, :].rearrange("p (j n) -> p j n", n=128), [('pb', bt)], [RQT])
                    if not sample:
                        chunks = []
                        if t > 0:
                            chunks.append((lambda gp, t=t: KT[:, gp, (t - 1) * 128:t * 128], ('KT', t - 1), lambda kvh, t=t: Vb[:, t - 1, kvh * 64:(kvh + 1) * 64], ('Vb', t - 1), maskp[:, 0:128], None))
                        chunks.append((lambda gp, t=t: KT[:, gp, t * 128:(t + 1) * 128], ('KT', t), lambda kvh, t=t: Vb[:, t, kvh * 64:(kvh + 1) * 64], ('Vb', t), maskp[:, 128:256], None))
                    else:
                        chunks = []
                        for s in range(16):
                            chunks.append((lambda gp, s=s: KTc[:, gp, s * 128:(s + 1) * 128], ('KTc', s), lambda kvh, s=s: Vc[:, s, kvh * 64:(kvh + 1) * 64], 'Vc', masks[:, 0:128], s))
                        chunks.append((lambda gp: KT[:, gp, 2048:2176], ('KT', 16), lambda kvh: Vb[:, 16, kvh * 64:(kvh + 1) * 64], ('Vb', 16), masksn[:, :], None))
                    nk = len(chunks)
                    bo2 = [0, 1]
                    for h in range(16):
                        kvh = h // 4; gp = kvh // 2; half = kvh % 2
                        j = 4 * (kvh // 2) + h % 4
                        p0 = half * 64
                        Sb = Sm[h % NBUF]; sres = ('Sm', h % NBUF)
                        for ci in range(0, nk, 4):
                            bs = pbank()
                            for cj in range(ci, min(ci + 4, nk)):
                                kfn, kres, vfn, vres, mk, rs_ = chunks[cj]
                                mm(pbs[bs][:, (cj - ci) * 128:(cj - ci + 1) * 128], QT[p0:p0 + 64, j, :], kfn(gp)[p0:p0 + 64, :], True, True,
                                   [RQT, kres], [('pb', bs)])
                            if (not sample) and nk == 2:
                                tt(Sb[:, 0:256], pbs[bs][:, 0:256], maskp[:, 0:256], ALU.add, [('pb', bs), 'mask'], [sres])
                                continue
                            for cj in range(ci, min(ci + 4, nk)):
                                kfn, kres, vfn, vres, mk, rs_ = chunks[cj]
                                if rs_ is None:
                                    tt(Sb[:, cj * 128:(cj + 1) * 128], pbs[bs][:, (cj - ci) * 128:(cj - ci + 1) * 128], mk, ALU.add, [('pb', bs), 'mask'], [sres])
                                else:
                                    stt(Sb[:, cj * 128:(cj + 1) * 128], pbs[bs][:, (cj - ci) * 128:(cj - ci + 1) * 128], rowm[:, rs_:rs_ + 1], mk, ALU.add, ALU.add,
                                        [('pb', bs), 'mask'], [sres])
                        S.op('dve', (lambda Sb=Sb, nk=nk, h=h, sm=sm: lambda e: e.reduce_max(sm[:, 0, h:h + 1], Sb[:, 0:nk * 128], AX.X))(), reads=[sres], writes=[(RS + 'mx', h)], cost=0.08 + nk * 128 / 960.0)
                        ts(sm[:, 1, h:h + 1], sm[:, 0, h:h + 1], sinkb[:, h:h + 1], -1.0, ALU.max, ALU.mult, [(RS + 'mx', h), 'sink'], [(RS + 'nm', h)])
                        act(Sb[:, 0:nk * 128], Sb[:, 0:nk * 128], AF.Exp, [sres, (RS + 'nm', h)], [sres, (RS + 'rs', h)], bias=sm[:, 1, h:h + 1], accum=sm[:, 2, h:h + 1])
                        Eb = ET[h % NBUF]; eres = ('ET', h % NBUF)
                        for ci in range(0, nk, 4):
                            bt = pbank()
                            m_ = min(ci + 4, nk) - ci
                            for cj in range(ci, ci + m_):
                                tr(pbs[bt][:, (cj - ci) * 128:(cj - ci + 1) * 128], Sb[:, cj * 128:(cj + 1) * 128], [sres], [('pb', bt)])
                            cp('act' if (ci // 4) % 2 == 0 else 'dve', Eb[:, ci:ci + m_, :], pbs[bt][:, 0:m_ * 128].rearrange("p (j n) -> p j n", n=128), [('pb', bt)], [eres])
                        bo = bo2[h // 8]
                        for cj in range(nk):
                            kfn, kres, vfn, vres, mk, rs_ = chunks[cj]
                            mm(pbs[bo][:, (h % 8) * 64:(h % 8 + 1) * 64], Eb[:, cj, :], vfn(kvh), cj == 0, cj == nk - 1, [eres, vres], [('pb', bo)])
                        if h % 8 == 7:
                            hf = h // 8; hs = slice(hf * 8, hf * 8 + 8)
                            hl = list(range(hf * 8, hf * 8 + 8))
                            tt(sm[:, 3, hs], sinkb[:, hs], sm[:, 1, hs], ALU.add, ['sink'] + [(RS + 'nm', x) for x in hl], [(RS + 'es', hf)])
                            act(sm[:, 3, hs], sm[:, 3, hs], AF.Exp, [(RS + 'es', hf)], [(RS + 'es', hf)])
                            tt(sm[:, 4, hs], sm[:, 2, hs], sm[:, 3, hs], ALU.add, [(RS + 'es', hf)] + [(RS + 'rs', x) for x in hl], [(RS + 'den', hf)])
                            S.op('dve', (lambda sm=sm, hs=hs: lambda e: e.reciprocal(sm[:, 5, hs], sm[:, 4, hs]))(), reads=[(RS + 'den', hf)], writes=[(RS + 'rden', hf)], cost=0.1)
                            tt(Of[:, hf * 512:(hf + 1) * 512].rearrange("p (h d) -> p h d", d=64), pbs[bo][:, 0:512].rearrange("p (h d) -> p h d", d=64),
                               sm[:, 5, hs].unsqueeze(2).to_broadcast([128, 8, 64]), ALU.mult, [('pb', bo), (RS + 'rden', hf)], [ROf])
                    for hf in range(2):
                        bt = pbank()
                        for jj in range(4):
                            c = hf * 4 + jj
                            tr(pbs[bt][:, jj * 128:(jj + 1) * 128], Of[:, c * 128:(c + 1) * 128], [ROf], [('pb', bt)])
                        cp('act', OT[:, hf * 4:hf * 4 + 4, tl * 128:(tl + 1) * 128], pbs[bt][:, :].rearrange("p (j n) -> p j n", n=128), [('pb', bt)],
                           [('OT', c) for c in range(hf * 4, hf * 4 + 4)])
                for fc in range(NCH):
                    bo = pbank()
                    for c in range(NCH):
                        mm(pbs[bo][:, 0:n], wo_[:, c, fc * 128:(fc + 1) * 128], OT[:, c, 0:n], c == 0, c == NCH - 1, rwo + [('OT', c)], [('pb', bo)])
                    residual(g, fc, bo, G1, [('mod', 2)], X[2], 'X2')

        def attn_sample():
            g = 4
            t0, n = GROUPS[g]
            t = 16
            NB3 = 6
            S.barrier(engines=('pe', 'act', 'dve', 'sp', 'pool'))
            AR.reset()
            pb_pool[0] = (2, 3, 4, 5, 6, 7)
            hT = AR.bf16([128, NCH, 128]); X = [AR.f32([128, 128]) for _ in range(3)]
            Qf = AR.f32([128, 1024]); QTs = AR.bf16([128, 16, 8, 8])
            KTc = AR.bf16([128, 2, 2048]); Vc = AR.bf16([128, 16, 256])
            ckf = [AR.f32([128, 256]), AR.f32([128, 256])]
            Smb = [AR.f32([128, 136]) for _ in range(NB3)]
            Epad = [AR.f32([128, 128]) for _ in range(NB3)]
            ETb = [AR.bf16([128, 2, 128]) for _ in range(NB3)]
            Of = AR.f32([128, 1024]); OT = AR.bf16([128, NCH, 128])
            sm = AR.f32([128, 8, 16])
            rt_s = AR.f32([128, 4, 16, 8])
            for i in range(NB3):
                S.op('dve', (lambda i=i: lambda e: e.memset(Epad[i][:, :], 0.0))(), writes=[('Epad', i)])
            for s in range(16):
                dma('pool', Vc[:, s, :], cv[s], [], [('Vc', s)])
                cb = ckf[s % 2]; cr = ('ckf', s % 2)
                dma('sp', cb[:, :], ck[s], [], [cr])
                for gp in range(2):
                    bt = pbank()
                    tr(pbs[bt][:, 0:128], cb[:, gp * 128:(gp + 1) * 128], [cr], [('pb', bt)])
                    cp('act' if gp == 0 else 'dve', KTc[:, gp, s * 128:(s + 1) * 128], pbs[bt][:, 0:128], [('pb', bt)], [('KTc', s)])
            norm_group(g, A1, B1, [('mod', 0), ('mod', 1)], hT, lambda c: ('hT', c), X)
            for hf in range(2):
                bq = pbank()
                for k in range(NCH):
                    mm(pbs[bq][:, 0:512], hT[:, k, 0:128], wq[:, k, hf * 512:(hf + 1) * 512], k == 0, k == NCH - 1, rwq + [('hT', k)], [('pb', bq)])
                act(Qf.rearrange("p (j h d) -> p j h d", h=2, d=64)[:, hf * 4:(hf + 1) * 4, :, :],
                    pbs[bq][:, 0:512].rearrange("p (h b d) -> p b h d", h=2, b=4), AF.Identity, [('pb', bq)], ['Qf0'], scale=0.125)
            rope(Qf.rearrange("p (h d) -> p h d", d=64), 'Qf0', t, 16, rt_s)
            for hf in range(2):
                bt = pbank()
                for jj in range(4):
                    j = hf * 4 + jj
                    tr(pbs[bt][:, jj * 128:(jj + 1) * 128], Qf[:, j * 128:(j + 1) * 128], ['Qf0'], [('pb', bt)])
                cp('act', QTs[:, :, hf * 4:hf * 4 + 4, :], pbs[bt][:, :].rearrange("p (j s q) -> p s j q", j=4, s=16), [('pb', bt)], ['QTs'])
            bo2 = [0, 1]
            for s in range(16):
                i3 = s % NB3
                Sb = Smb[i3]; sres = ('Smb', i3); Ep = Epad[i3]; epres = ('Epad', i3); Eb = ETb[i3]; eres = ('ETb', i3)
                bs = pbank()
                for kvh in range(4):
                    gp = kvh // 2; p0 = (kvh % 2) * 64
                    lq = QTs[p0:p0 + 64, s, 4 * gp:4 * gp + 4, :].rearrange("p j q -> p (j q)")
                    mm(pbs[bs][32 * kvh:32 * kvh + 32, 0:128], lq, KTc[p0:p0 + 64, gp, s * 128:(s + 1) * 128], True, True,
                       ['QTs', ('KTc', s)], [('pb', bs)], tile_position=(p0, 32 * kvh))
                    mm(pbs[bs][32 * kvh:32 * kvh + 32, 128:136], lq, KT[p0:p0 + 64, gp, 2048 + 8 * s:2048 + 8 * s + 8], True, True,
                       ['QTs', ('KT', 16)], [('pb', bs)], tile_position=(p0, 32 * kvh))
                tt(Sb[:, 0:136], pbs[bs][:, 0:136], masks[:, 0:136], ALU.add, [('pb', bs), 'mask'], [sres])
                S.op('dve', (lambda Sb=Sb, s=s, sm=sm: lambda e: e.reduce_max(sm[:, 0, s:s + 1], Sb[:, 0:136], AX.X))(), reads=[sres], writes=[('s_mx', s)], cost=0.25)
                ts(sm[:, 1, s:s + 1], sm[:, 0, s:s + 1], sinkc[:, 0:1], -1.0, ALU.max, ALU.mult, [('s_mx', s), 'sink'], [('s_nm', s)])
                act(Sb[:, 0:128], Sb[:, 0:128], AF.Exp, [sres, ('s_nm', s)], [sres, ('s_rs', s)], bias=sm[:, 1, s:s + 1], accum=sm[:, 2, s:s + 1])
                act(Ep[:, 8 * s:8 * s + 8], Sb[:, 128:136], AF.Exp, [sres, ('s_nm', s), epres], [epres, ('s_rs2', s)], bias=sm[:, 1, s:s + 1], accum=sm[:, 3, s:s + 1])
                bt = pbank()
                tr(pbs[bt][:, 0:128], Sb[:, 0:128], [sres], [('pb', bt)])
                tr(pbs[bt][:, 128:256], Ep[:, :], [epres], [('pb', bt)])
                cp('act' if s % 2 == 0 else 'dve', Eb[:, :, :], pbs[bt][:, 0:256].rearrange("p (j n) -> p j n", n=128), [('pb', bt)], [eres])
                S.op('dve', (lambda Ep=Ep, s=s: lambda e: e.memset(Ep[:, 8 * s:8 * s + 8], 0.0))(), reads=[epres], writes=[epres], cost=0.1)
                bo = bo2[s // 8]
                for kvh in range(4):
                    o_ = pbs[bo][32 * kvh:32 * kvh + 32, (s % 8) * 64:(s % 8 + 1) * 64]
                    mm(o_, Eb[:, 0, 32 * kvh:32 * kvh + 32], Vc[:, s, kvh * 64:(kvh + 1) * 64], True, False, [eres, ('Vc', s)], [('pb', bo)], tile_position=(0, 32 * kvh))
                    mm(o_, Eb[:, 1, 32 * kvh:32 * kvh + 32], Vb[:, 16, kvh * 64:(kvh + 1) * 64], False, True, [eres, ('Vb', 16)], [('pb', bo)], tile_position=(0, 32 * kvh))
            alls = lambda nm: [(nm, s) for s in range(16)]
            tt(sm[:, 5, :], sm[:, 2, :], sm[:, 3, :], ALU.add, alls('s_rs') + alls('s_rs2'), ['s_den'])
            act(sm[:, 4, :], sm[:, 1, :], AF.Exp, alls('s_nm') + ['sink'], ['s_es'], bias=sinkc[:, 0:1])
            tt(sm[:, 5, :], sm[:, 5, :], sm[:, 4, :], ALU.add, ['s_den', 's_es'], ['s_den'])
            S.op('dve', lambda e: e.reciprocal(sm[:, 6, :], sm[:, 5, :]), reads=['s_den'], writes=['s_rden'], cost=0.1)
            Os = Qf
            for hf in range(2):
                tt(Os[:, hf * 512:(hf + 1) * 512].rearrange("p (s d) -> p s d", d=64), pbs[bo2[hf]][:, 0:512].rearrange("p (s d) -> p s d", d=64),
                   sm[:, 6, hf * 8:hf * 8 + 8].unsqueeze(2).to_broadcast([128, 8, 64]), ALU.mult, [('pb', bo2[hf]), 's_rden'], ['Qf0'])
            dma('sp', scr_o, Os[:, :], ['Qf0'], ['scr_o'])
            srcv = scr_o.rearrange("(h q) (s d) -> q s h d", q=8, d=64)
            for q in range(8):
                dma('sp', Of[q::8, :].rearrange("p (h d) -> p h d", d=64), srcv[q], ['scr_o'], ['Of0'])
            for hf in range(2):
                bt = pbank()
                for jj in range(4):
                    c = hf * 4 + jj
                    tr(pbs[bt][:, jj * 128:(jj + 1) * 128], Of[:, c * 128:(c + 1) * 128], ['Of0'], [('pb', bt)])
                cp('act', OT[:, hf * 4:hf * 4 + 4, 0:128], pbs[bt][:, :].rearrange("p (j n) -> p j n", n=128), [('pb', bt)],
                   [('OT', c) for c in range(hf * 4, hf * 4 + 4)])
            for fc in range(NCH):
                bo = pbank()
                for c in range(NCH):
                    mm(pbs[bo][:, 0:n], wo_[:, c, fc * 128:(fc + 1) * 128], OT[:, c, 0:n], c == 0, c == NCH - 1, rwo + [('OT', c)], [('pb', bo)])
                residual(g, fc, bo, G1, [('mod', 2)], X[2], 'X2')

        rowm = sb("rowm_sb", [128, 16]); masksn = sb("masksn_sb", [128, 128])
        rowmd = din("rowm", [128, 16]); masksnd = din("masksn", [128, 128])
        dma('sp', rowm[:], rowmd, [], ['mask']); dma('sp', masksn[:], masksnd, [], ['mask'])
        attn_phase([0, 1, 2, 3], 2, False, shared=(hT, X))
        if stop <= 4:
            return _finish()
        S.stage = 'attn_s'
        attn_sample()
        pb_pool[0] = (0, 1, 2, 3, 4, 5, 6, 7)

        if stop <= 5:
            return _finish()
        S.stage = 'ffn1'
        ffn_stage(1)

        if stop <= 6:
            return _finish()
        S.stage = 'final'
        S.barrier()
        AR.reset()
        yT = AR.f32([128, NCH, 512])
        X = [AR.f32([128, 512]), AR.f32([128, 512]), AR.f32([128, 512])]
        yo = [AR.f32([128, 1024]), AR.f32([128, 1024])]
        for g in range(5):
            t0, n = GROUPS[g]
            norm_group(g, None, None, 'svT', yT, lambda c: ('yT', c), X, out_f32_scale=lambda c: svT[:, SV_FG + c:SV_FG + c + 1])
            for tl in range(n // 128):
                t = t0 // 128 + tl
                yb = yo[t % 2]; yr = ('yo', t % 2)
                for hf in range(2):
                    bt = pbank()
                    for jj in range(4):
                        c = hf * 4 + jj
                        tr(pbs[bt][:, jj * 128:(jj + 1) * 128], yT[:, c, tl * 128:(tl + 1) * 128], [('yT', c)], [('pb', bt)])
                    cp('act' if hf == 0 else 'dve', yb[:, hf * 512:(hf + 1) * 512], pbs[bt][:, 0:512], [('pb', bt)], [yr])
                dst = y_p[t * 128:(t + 1) * 128, :] if t < 16 else y_s
                dma('sp', dst, yb[:, :], [yr], [])
        return _finish()


_CACHE = {}


def _consts():
    ROT = 16
    inv = (500000.0 ** (-np.arange(0, ROT, 2, dtype=np.float32) / np.float32(ROT))).astype(np.float32)
    pos = np.zeros((128, 17), np.float32)
    for t in range(16):
        pos[:, t] = t * 128 + np.arange(128)
    pos[:, 16] = 16384 + (np.arange(128) % 8)
    ang = (pos[:, :, None] * inv[None, None, :]).astype(np.float32)
    cos = np.cos(ang).astype(np.float32); sin = np.sin(ang).astype(np.float32)
    NEG = -30000.0
    q = np.arange(128)[:, None]; s = np.arange(256)[None, :]
    rel = 128 + q - s
    maskp = np.where((rel >= 0) & (rel <= 128), 0.0, NEG).astype(np.float32)
    qi = (np.arange(128) % 8)[:, None]; sq = (np.arange(128) // 8)
    k = np.arange(128)[None, :]
    masks = np.zeros((128, 136), np.float32)
    masks[:, 0:128] = np.where(k >= qi, 0.0, NEG)
    masks[:, 128:136] = np.where(np.arange(8)[None, :] <= qi, 0.0, NEG)
    rowm = np.where(sq[:, None] == np.arange(16)[None, :], 0.0, NEG).astype(np.float32)
    ks = (np.arange(128) // 8)[None, :]; kt = (np.arange(128) % 8)[None, :]
    masksn = np.where((ks == sq[:, None]) & (kt <= qi), 0.0, NEG).astype(np.float32)
    return dict(ident=np.eye(128, dtype=np.float32), cos=cos, sin=sin, maskp=maskp, masks=masks, rowm=rowm, masksn=masksn)


def kernel(x_prompt, x_sample, c_prompt, c_sample, state_conv, state_h, cache_k, cache_v,
           ada_w, ada_b, norm_g, rnn_w_in, rnn_conv_w, rnn_conv_b, rnn_gate_w, rnn_gate_b,
           rnn_lambda, rnn_w_out, kv_ada_w, kv_ada_b, kv_norm_g, w_kv, attn_w_q, attn_sinks,
           attn_w_o, ffn_w_in, ffn_w_out, final_g):
    f = lambda a: np.ascontiguousarray(np.asarray(a, dtype=np.float32))
    if 'nc' not in _CACHE:
        _CACHE['nc'] = build_program()
    nc = _CACHE['nc']
    C = _consts()
    sv = np.concatenate([f(ada_b).reshape(96, 128), f(kv_ada_b).reshape(16, 128), f(norm_g).reshape(32, 128),
                         f(kv_norm_g).reshape(8, 128), f(final_g).reshape(8, 128), f(rnn_conv_w).reshape(32, 128),
                         f(rnn_conv_b).reshape(8, 128), f(rnn_gate_b).reshape(16, 128), f(rnn_lambda).reshape(8, 128)], axis=0)
    sinks = f(attn_sinks)[0]
    def pk(w):
        k = w.shape[0] // 128
        return np.ascontiguousarray(w.reshape(k, 128, w.shape[1]).transpose(1, 0, 2).reshape(128, k * w.shape[1]))
    aw = f(ada_w)
    adaP = np.stack([np.stack([pk(aw[l][:, v * 1024:(v + 1) * 1024]) for v in range(6)]) for l in range(2)])
    kw_ = f(kv_ada_w)
    kvadaP = np.stack([pk(kw_[:, v * 1024:(v + 1) * 1024]) for v in range(2)])
    fi = f(ffn_w_in); fo = f(ffn_w_out)
    ffl = []
    for l in range(2):
        parts = []
        for (h0, hc) in FFN_GROUPS:
            parts.append(pk(fi[l][:, h0 * 128:(h0 + hc) * 128]))
            parts.append(pk(fi[l][:, DFF + h0 * 128:DFF + (h0 + hc) * 128]))
            parts.append(pk(fo[l][h0 * 128:(h0 + hc) * 128, :]))
        ffl.append(np.concatenate(parts, axis=1))
    ffnP = np.ascontiguousarray(np.stack(ffl))
    shared = dict(adaP=adaP, rnn_w_in=pk(f(rnn_w_in)[0]), gate_w=np.ascontiguousarray(f(rnn_gate_w)[0].transpose(1, 0, 2).reshape(128, 2048)),
                  rnn_w_out=pk(f(rnn_w_out)[0]), kvadaP=kvadaP, w_kv=pk(f(w_kv)), w_q=pk(f(attn_w_q)[0]), w_o=pk(f(attn_w_o)[0]), ffnP=ffnP,
                  sv=sv, ident=C['ident'], cos=C['cos'], sin=C['sin'], maskp=C['maskp'],
                  masks=C['masks'], rowm=C['rowm'], masksn=C['masksn'],
                  sinkb=np.ascontiguousarray(np.broadcast_to(sinks[None, :], (128, 16))),
                  sinkc=np.ascontiguousarray(np.repeat(sinks, 8)[:, None]))
    xp_ = f(x_prompt); xs_ = f(x_sample); cp_ = f(c_prompt); cs_ = f(c_sample)
    sc_ = f(state_conv); sh_ = f(state_h); ck_ = f(cache_k); cv_ = f(cache_v)
    in_maps = []
    for b in range(8):
        sl = slice(16 * b, 16 * b + 16)
        m = dict(shared)
        m.update(xp=xp_[b], xs=np.ascontiguousarray(xs_[sl].reshape(128, 1024)),
                 cc=np.ascontiguousarray(np.concatenate([cp_[b:b + 1], cs_[sl]], axis=0)),
                 sconv=np.ascontiguousarray(sc_[0, sl].reshape(48, 1024)), shin=np.ascontiguousarray(sh_[0, sl]),
                 ck=np.ascontiguousarray(ck_[sl].reshape(16, 128, 256)), cv=np.ascontiguousarray(cv_[sl].reshape(16, 128, 256)))
        in_maps.append(m)
    res = run_bass_kernel_spmd(nc, in_maps, core_ids=list(range(8)))
    R = res.results
    cat = lambda k: np.stack([np.asarray(r[k], dtype=np.float32) for r in R], axis=0)
    y_prompt = cat('y_p')
    y_sample = cat('y_s').reshape(128, 8, 1024)
    prompt_conv = cat('pconv').reshape(1, 8, 3, 1024)
    prompt_h = cat('ph').reshape(1, 8, 1024)
    prompt_k = cat('pk').reshape(8, 128, 4, 64)
    prompt_v = cat('pv').reshape(8, 128, 4, 64)
    sample_conv = cat('sconv_o').reshape(1, 128, 3, 1024)
    sample_h = cat('sh_o').reshape(1, 128, 1024)
    sample_k = cat('sk').reshape(128, 128, 4, 64)
    sample_v = cat('svo').reshape(128, 128, 4, 64)
    return (y_prompt, y_sample, prompt_conv, prompt_h, prompt_k, prompt_v, sample_conv, sample_h, sample_k, sample_v)
```

```python
import sys
import numpy as np
from contextlib import ExitStack
import concourse.bass as bass
import concourse.mybir as mybir
from concourse.bass_utils import run_bass_kernel_spmd

F32 = mybir.dt.float32
BF16 = mybir.dt.bfloat16
AF = mybir.ActivationFunctionType
ALU = mybir.AluOpType
AX = mybir.AxisListType

ENGS = ['pe', 'act', 'dve', 'pool', 'sp']
LOOKBACK = 3

D = 1024
NCH = 8
TP = 2048
TS = 128
T = TP + TS
DFF = 2816
NHC = 22
EPS = 1e-6
GROUPS = [(0, 512), (512, 512), (1024, 512), (1536, 512), (2048, 128)]
FFN_GROUPS = [(0, 4), (4, 4), (8, 4), (12, 4), (16, 3), (19, 3)]
SV_ADAB = 0
SV_KVB = 96
SV_NG = 112
SV_KVG = 144
SV_FG = 152
SV_CW = 160
SV_CB = 192
SV_GB = 200
SV_LM = 216
SV_ROWS = 224
RING_SLOTS = 13
SLOT = 2048
ARENA_W = 15500
TAIL_W = 4352
DUMP_PATH = None
SCHEDULE = True
STRICT_SAME = True
SAME_LAT = 0.25
PRIO_RANK = True
VERBOSE = False


class Ins:
    __slots__ = ('eng', 'fn', 'deps', 'idx', 'signaled', 'count', 'dma', 'semi', 'dval', 'waits', 'tag', 'cost', 'seq', 'bar', 'start', 'fin', 'nun', 'succ', 'stage')


class Sched:
    def __init__(self, nc, n_dma_sems=56):
        self.nc = nc
        self.n_main = 48
        self.rr_b = 0
        self.streams = {e: [] for e in ENGS}
        self.last_w = {}
        self.readers = {}
        self.n_dma_sems = n_dma_sems
        self.dma_rr = 0
        self.dma_last = [None] * n_dma_sems
        self.dma_cnt = [0] * n_dma_sems
        self.pb_acc = {}
        self.seq = 0
        self.cur_bar = {e: None for e in ENGS}
        self.since_bar = {e: [] for e in ENGS}
        self.n_bar = 0
        self.stage = 'init'

    def _new(self, eng, fn, dma):
        ins = Ins()
        ins.eng = eng; ins.fn = fn; ins.dma = dma; ins.signaled = False; ins.count = 0
        ins.semi = None; ins.dval = 0; ins.waits = None; ins.deps = []
        ins.cost = 0.1; ins.bar = None; ins.start = 0.0; ins.fin = 0.0
        ins.seq = self.seq; self.seq += 1
        ins.stage = self.stage
        ins.tag = 0
        try:
            f = sys._getframe(1)
            while f is not None and f.f_code.co_name in ('_new', 'op', 'mm', 'tr', 'act', 'tt', 'ts', 'stt', 'cp', 'dma', 'wload', 'barrier', 'finish'):
                f = f.f_back
            ins.tag = f.f_lineno if f is not None else 0
        except Exception:
            ins.tag = 0
        return ins

    def op(self, eng, fn, reads=(), writes=(), dma=False, cost=0.1, sem_b=False):
        ins = self._new(eng, fn, dma)
        ins.cost = cost
        deps = {}
        cb = self.cur_bar[eng]
        if cb is not None:
            deps[id(cb)] = (cb, 'BAR')
        for r in reads:
            w = self.last_w.get(r)
            if w is not None:
                deps[id(w)] = (w, 'RAW')
        for r in writes:
            w = self.last_w.get(r)
            if w is not None and id(w) not in deps:
                deps[id(w)] = (w, 'WAW')
            for rd in self.readers.get(r, ()):
                if id(rd) not in deps:
                    deps[id(rd)] = (rd, 'WAR')
        banks = set(r for r in list(reads) + list(writes) if isinstance(r, tuple) and r[0] == 'pb')
        for bnk in banks:
            st = self.pb_acc.get(bnk)
            if st is None:
                st = self.pb_acc[bnk] = {'eng': eng, 'cur': [], 'prev': []}
            if st['eng'] != eng:
                st['prev'] = st['cur']
                st['cur'] = []
                st['eng'] = eng
            for d in st['prev']:
                if id(d) not in deps:
                    deps[id(d)] = (d, 'PB')
            st['cur'].append(ins)
        if dma:
            half = self.n_dma_sems // 2
            if eng == 'pool':
                si = self.rr_b % half
                self.rr_b += 1
            else:
                si = half + (self.dma_rr % half)
                self.dma_rr += 1
            prev = self.dma_last[si]
            if prev is not None and id(prev) not in deps:
                deps[id(prev)] = (prev, 'SEM')
            self.dma_cnt[si] += 1
            ins.semi = si
            ins.dval = 16 * self.dma_cnt[si]
            self.dma_last[si] = ins
        ins.deps = list(deps.values())
        for r in reads:
            lst = self.readers.setdefault(r, [])
            lst.append(ins)
        for r in writes:
            self.last_w[r] = ins
            self.readers[r] = []
        ins.idx = len(self.streams[eng])
        self.streams[eng].append(ins)
        self.since_bar[eng].append(ins)
        return ins

    def barrier(self, engines=('pe', 'act', 'dve', 'sp')):
        pend = [d for d in self.dma_last if d is not None]
        prior = []
        for e in ENGS:
            prior += [x for x in self.since_bar[e] if x.fn is not None]
        self.n_bar += 1
        for e in engines:
            ins = self._new(e, None, False)
            ins.cost = 0.0
            ins.bar = self.n_bar
            deps = {}
            for d in prior:
                deps[id(d)] = (d, 'BAR')
            for d in pend:
                deps[id(d)] = (d, 'RAW')
            cb = self.cur_bar[e]
            if cb is not None:
                deps[id(cb)] = (cb, 'BAR')
            ins.deps = list(deps.values())
            ins.idx = len(self.streams[e])
            self.streams[e].append(ins)
            self.cur_bar[e] = ins
        for e in ENGS:
            if e in engines:
                self.since_bar[e] = []

    def finish(self):
        pend = [d for d in self.dma_last if d is not None]
        ins = self._new('sp', None, False)
        ins.cost = 0.0
        ins.deps = [(d, 'RAW') for d in pend] + [(d, 'BAR') for d in self.streams['sp'] if d is not ins]
        ins.idx = len(self.streams['sp'])
        self.streams['sp'].append(ins)

    def schedule(self):
        import heapq
        allins = []
        for e in ENGS:
            allins += self.streams[e]
        for ins in allins:
            ins.succ = []
            ins.nun = 0
        for ins in allins:
            seen = set()
            for d, kind in ins.deps:
                if id(d) in seen:
                    continue
                seen.add(id(d))
                d.succ.append(ins)
                ins.nun += 1
        prio = {}
        if PRIO_RANK:
            order_seq = sorted(allins, key=lambda x: -x.seq)
            rank = {}
            for ins in order_seq:
                r = 0.0
                for sc in ins.succ:
                    rs_ = rank[id(sc)]
                    if rs_ > r:
                        r = rs_
                rank[id(ins)] = r + ins.cost + (0.25 if not ins.dma else 2.0)
            for ins in allins:
                prio[id(ins)] = -rank[id(ins)]
        else:
            for ins in allins:
                prio[id(ins)] = ins.seq
        avail = {e: [] for e in ENGS}
        ready_t = {}
        for ins in allins:
            if ins.nun == 0:
                heapq.heappush(avail[ins.eng], (prio[id(ins)], id(ins), ins))
                ready_t[id(ins)] = 0.0
        free = {e: 0.0 for e in ENGS}
        dma_pipe = [0.0]
        order = {e: [] for e in ENGS}
        remaining = len(allins)
        WIN = 48
        while remaining:
            best = None
            for e in ENGS:
                h = avail[e]
                if not h:
                    continue
                cands = heapq.nsmallest(WIN, h)
                pick = None
                for c in cands:
                    if ready_t[id(c[2])] <= free[e] + 1e-9:
                        pick = c
                        break
                if pick is None:
                    pick = min(cands, key=lambda c: (ready_t[id(c[2])], c[0]))
                st = max(free[e], ready_t[id(pick[2])])
                if best is None or st < best[0] or (st == best[0] and pick[0] < best[1][0]):
                    best = (st, pick, e)
            st, pick, e = best
            ins = pick[2]
            avail[e].remove(pick)
            heapq.heapify(avail[e])
            ins.start = st
            if ins.dma:
                free[e] = st + 0.06
                t0 = max(st + 1.8, dma_pipe[0])
                ins.fin = t0 + ins.cost
                dma_pipe[0] = ins.fin
            else:
                ins.fin = st + ins.cost
                free[e] = ins.fin
            order[e].append(ins)
            remaining -= 1
            for sc in ins.succ:
                sc.nun -= 1
                if sc.eng == ins.eng and not ins.dma:
                    lat = SAME_LAT if ins.eng in ('act', 'dve') else 0.0
                else:
                    lat = 0.38
                rt = max(ready_t.get(id(sc), 0.0), ins.fin + lat)
                ready_t[id(sc)] = rt
                if sc.nun == 0:
                    heapq.heappush(avail[sc.eng], (prio[id(sc)], id(sc), sc))
        for e in ENGS:
            self.streams[e] = order[e]
            for i, ins in enumerate(order[e]):
                ins.idx = i
        self.sim_time = max(free.values())
        print("[sched] simulated time (us): %.1f" % self.sim_time, {e: len(order[e]) for e in ENGS})
        if VERBOSE:
            st = {}
            for e in ENGS:
                for ins in order[e]:
                    d = st.setdefault(ins.stage, {'t0': 1e18, 't1': 0.0, 'busy': {x: 0.0 for x in ENGS}})
                    d['t0'] = min(d['t0'], ins.start); d['t1'] = max(d['t1'], ins.fin)
                    if not ins.dma:
                        d['busy'][e] += ins.cost
            for k, d in sorted(st.items(), key=lambda kv: kv[1]['t0']):
                print("  %-10s t0=%7.1f t1=%7.1f span=%7.1f  busy pe=%6.1f act=%6.1f dve=%6.1f" % (k, d['t0'], d['t1'], d['t1'] - d['t0'], d['busy']['pe'], d['busy']['act'], d['busy']['dve']))

    def plan(self):
        for eng in ENGS:
            known_eng = {e: -1 for e in ENGS}
            known_dma = {}
            for ins in self.streams[eng]:
                waits = []
                tgt = {}
                for d, kind in ins.deps:
                    if d.dma:
                        if known_dma.get(d.semi, 0) >= d.dval:
                            continue
                        known_dma[d.semi] = d.dval
                        waits.append(d)
                    elif d.fn is None:
                        continue
                    elif d.eng == eng:
                        if eng == 'pe':
                            continue
                        if (kind == 'RAW' and ins.idx - d.idx <= LOOKBACK) or STRICT_SAME:
                            if d.idx > tgt.get(eng, (-1, None))[0]:
                                tgt[eng] = (d.idx, d)
                    else:
                        if d.idx > tgt.get(d.eng, (-1, None))[0]:
                            tgt[d.eng] = (d.idx, d)
                for e2, (ix, d) in tgt.items():
                    if known_eng[e2] >= ix:
                        continue
                    known_eng[e2] = ix
                    waits.append(d)
                for d in waits:
                    if not d.dma:
                        d.signaled = True
                ins.waits = waits
        for eng in ENGS:
            c = 0
            for ins in self.streams[eng]:
                if ins.dma:
                    continue
                if ins.fn is None:
                    ins.count = c
                    continue
                if ins.signaled:
                    c += 1
                    ins.count = c

    def dump(self, path):
        with open(path, 'w') as f:
            for eng in ENGS:
                f.write('==== %s\n' % eng)
                for ins in self.streams[eng]:
                    w = ['%s:%s' % (('dma%d' % d.semi) if d.dma else d.eng, d.dval if d.dma else '%d(c%d,L%d)' % (d.idx, d.count, d.tag)) for d in ins.waits]
                    f.write('%5d L%-4d %s%s sig=%d cnt=%d waits=%s\n' % (ins.idx, ins.tag, 'DMA(s%d,v%d) ' % (ins.semi, ins.dval) if ins.dma else '', 'NOP' if ins.fn is None else '', ins.signaled, ins.count, w))

    def emit(self):
        nc = self.nc
        if SCHEDULE:
            self.schedule()
        self.plan()
        if DUMP_PATH:
            self.dump(DUMP_PATH)
        with ExitStack() as es:
            esem = {e: es.enter_context(nc.semaphore("s_" + e)) for e in ENGS}
            dsem = [es.enter_context(nc.semaphore("d_%d" % i)) for i in range(self.n_dma_sems)]
            block = es.enter_context(nc.Block())

            def run(eng_name):
                def body(e):
                    for ins in self.streams[eng_name]:
                        for d in ins.waits:
                            if d.dma:
                                e.wait_ge(dsem[d.semi], d.dval)
                            else:
                                e.wait_ge(esem[d.eng], d.count)
                        if ins.fn is None:
                            continue
                        bi = ins.fn(e)
                        if ins.dma:
                            bi.then_inc(dsem[ins.semi], 16)
                        elif ins.signaled:
                            bi.then_inc(esem[eng_name], 1)
                return body

            block.tensor(run('pe'))
            block.scalar(run('act'))
            block.vector(run('dve'))
            block.gpsimd(run('pool'))
            block.sync(run('sp'))


class _Stop(Exception):
    pass


def build_program(stop=99, dbg=False):
    try:
        return _build(stop, dbg)
    except _Stop as e:
        return e.args[0]


def _build(stop, dbg):
    nc = bass.Bass("TRN2", target_bir_lowering=False)
    S = Sched(nc)
    din = lambda n, shp: nc.dram_tensor(n, list(shp), F32, kind="ExternalInput").ap()
    dout = lambda n, shp: nc.dram_tensor(n, list(shp), F32, kind="ExternalOutput").ap()
    dint = lambda n, shp: nc.dram_tensor(n, list(shp), F32, kind="Internal").ap()
    xp = din("xp", [TP, D]); xs = din("xs", [TS, D]); cc = din("cc", [17, D])
    sconv = din("sconv", [48, D]); shin = din("shin", [16, D])
    ck = din("ck", [16, 128, 256]); cv = din("cv", [16, 128, 256])
    adaP = din("adaP", [2, 6, 128, 8 * 1024]); rnn_w_in = din("rnn_w_in", [128, 8 * 2048]); gate_w = din("gate_w", [128, 8 * 256])
    rnn_w_out = din("rnn_w_out", [128, 8 * 1024]); kvadaP = din("kvadaP", [2, 128, 8 * 1024]); w_kv = din("w_kv", [128, 8 * 512])
    w_q = din("w_q", [128, 8 * 1024]); w_o = din("w_o", [128, 8 * 1024]); ffnP = din("ffnP", [2, 128, NHC * 3072])
    svd = din("sv", [SV_ROWS, 128]); identd = din("ident", [128, 128])
    cosd = din("cos", [128, 17, 8]); sind = din("sin", [128, 17, 8])
    maskpd = din("maskp", [128, 256]); masksd = din("masks", [128, 136])
    sinkbd = din("sinkb", [128, 16]); sinkcd = din("sinkc", [128, 1])
    y_p = dout("y_p", [TP, D]); y_s = dout("y_s", [TS, D]); pconv = dout("pconv", [3, D]); ph = dout("ph", [1, D])
    pk = dout("pk", [128, 256]); pv = dout("pv", [128, 256]); sconv_o = dout("sconv_o", [48, D]); sh_o = dout("sh_o", [16, D])
    sk = dout("sk", [16, 128, 256]); svo = dout("svo", [16, 128, 256])
    scr_v = dint("scr_v", [128, 256]); scr_o = dint("scr_o", [128, 1024])

    with ExitStack() as es:
        sb = lambda n, shp, dt=F32: es.enter_context(nc.sbuf_tensor(n, list(shp), dt))
        xT = sb("xT", [128, NCH, T])
        ring = sb("ring", [128, RING_SLOTS * SLOT], BF16)
        svT = sb("svT", [128, SV_ROWS]); ident = sb("ident_sb", [128, 128])
        ones = sb("ones", [128, 128], BF16)
        modT = sb("modT", [128, 48, 17]); kvmodT = sb("kvmodT", [128, 16, 17])
        cTc = sb("cTc", [128, NCH, 17], BF16)
        cost = sb("cost", [128, 17, 8]); sint = sb("sint", [128, 17, 8])
        maskp = sb("maskp_sb", [128, 256]); masks = sb("masks_sb", [128, 136])
        sinkb = sb("sinkb_sb", [128, 16]); sinkc = sb("sinkc_sb", [128, 1])
        rgc = sb("rgc", [128, 40])
        hstate = sb("hstate", [128, 8])
        arena = sb("arena", [128, ARENA_W + TAIL_W])
        pbs = [es.enter_context(nc.psum_tensor("pb%d" % i, [128, 512], F32)) for i in range(8)]

        class Arena:
            def __init__(self):
                self.off = 0
                self.limit = ARENA_W + TAIL_W
            def reset(self):
                self.off = 0
            def f32(self, shape):
                n = int(np.prod(shape[1:]))
                ap = arena[0:shape[0], self.off:self.off + n]
                self.off += n
                assert self.off <= self.limit, (self.off, self.limit)
                return _view(ap, shape)
            def bf16(self, shape):
                n = int(np.prod(shape[1:]))
                nw = (n + 1) // 2
                ap = arena[0:shape[0], self.off:self.off + nw].bitcast(BF16)[:, 0:n]
                self.off += nw
                assert self.off <= self.limit, (self.off, self.limit)
                return _view(ap, shape)

        def _view(ap, shape):
            if len(shape) == 2:
                return ap
            if len(shape) == 3:
                return ap.rearrange("p (a b) -> p a b", b=shape[2])
            if len(shape) == 4:
                return ap.rearrange("p (a b c) -> p a b c", b=shape[2], c=shape[3])
            raise ValueError

        AR = Arena()
        KT = arena[:, ARENA_W:ARENA_W + 2176].bitcast(BF16).rearrange("p (a b) -> p a b", b=T)
        Vb = arena[:, ARENA_W + 2176:ARENA_W + 4352].bitcast(BF16).rearrange("p (a b) -> p a b", b=256)

        def fsz(ap):
            n = 1
            for d in ap.shape[1:]:
                n *= int(d)
            return n

        def c_act(out):
            return 0.22 + fsz(out) / 1200.0

        def c_dve(out):
            return 0.08 + fsz(out) / 960.0

        def mm(out, lhsT, rhs, start, stop, r, w, **kw):
            f = 4.0 if rhs.dtype == F32 else 1.0
            cost = f * max(fsz(rhs), 64) / 2400.0 + 0.015
            S.op('pe', lambda e: e.matmul(out, lhsT, rhs, start=start, stop=stop, **kw), reads=r, writes=w, cost=cost)

        def tr(out, in_, r, w):
            n = in_.shape[0]
            cost = 0.12
            S.op('pe', lambda e: e.transpose(out, in_, ident[0:n, 0:n]), reads=list(r) + ['ident'], writes=w, cost=cost)

        def act(out, in_, func, r, w, bias=None, scale=None, accum=None):
            kw = {}
            if bias is not None: kw['bias'] = bias
            if scale is not None: kw['scale'] = scale
            if accum is not None: kw['accum_out'] = accum
            S.op('act', lambda e: e.activation(out, in_, func, **kw), reads=r, writes=w, cost=c_act(out))

        def tt(out, in0, in1, op, r, w):
            S.op('dve', lambda e: e.tensor_tensor(out, in0, in1, op), reads=r, writes=w, cost=c_dve(out))

        def ts(out, in0, s1, s2, op0, op1, r, w):
            if s2 is None:
                S.op('dve', lambda e: e.tensor_scalar(out, in0, s1, None, op0), reads=r, writes=w, cost=c_dve(out))
            else:
                S.op('dve', lambda e: e.tensor_scalar(out, in0, s1, s2, op0, op1), reads=r, writes=w, cost=c_dve(out))

        def stt(out, in0, scalar, in1, op0, op1, r, w):
            S.op('dve', lambda e: e.scalar_tensor_tensor(out, in0, scalar, in1, op0, op1), reads=r, writes=w, cost=1.2 * c_dve(out))

        def cp(eng, out, in_, r, w):
            if eng == 'act':
                S.op('act', lambda e: e.copy(out, in_), reads=r, writes=w, cost=c_act(out))
            else:
                S.op('dve', lambda e: e.tensor_copy(out, in_), reads=r, writes=w, cost=c_dve(out))

        def dma(q, out, in_, r, w, sem_b=False):
            nbytes = 4.0 * max(fsz(out), fsz(in_)) * min(int(out.shape[0]), 128)
            S.op(q, lambda e: e.dma_start(out=out, in_=in_), reads=r, writes=w, dma=True, cost=nbytes / 300e3, sem_b=sem_b)

        pb_rr = [0]
        pb_pool = [(0, 1, 2, 3, 4, 5, 6, 7)]
        def pbank():
            pool = pb_pool[0]
            i = pool[pb_rr[0] % len(pool)]
            pb_rr[0] += 1
            return i

        ring_pos = [0]
        def wload(dram_ap, shape, slot0=None):
            n = int(np.prod(shape[1:]))
            ns = (n + SLOT - 1) // SLOT
            if slot0 is not None:
                ring_pos[0] = slot0
            if ring_pos[0] + ns > RING_SLOTS:
                ring_pos[0] = 0
            s0 = ring_pos[0]
            ring_pos[0] += ns
            view = _view(ring[:, s0 * SLOT: s0 * SLOT + n], shape)
            res = [('ws', i) for i in range(s0, s0 + ns)]
            dma('pool', view, dram_ap, [], res)
            return view, res

        def ckpt(x):
            if stop <= x:
                raise _Stop(_finish())

        def _finish():
            if dbg:
                S.barrier()
                dbg_x = nc.dram_tensor("dbg_x", [128, NCH * T], F32, kind="ExternalOutput").ap()
                dma('sp', dbg_x, xT[:, :, :].rearrange("p c t -> p (c t)"), [], [])
                dbg_m = nc.dram_tensor("dbg_m", [128, 48 * 17], F32, kind="ExternalOutput").ap()
                dma('sp', dbg_m, modT[:, :, :].rearrange("p c t -> p (c t)"), [], [])
            S.finish()
            S.emit()
            return nc

        dma('sp', ident[:], identd, [], ['ident'])
        dma('sp', cost[:], cosd, [], ['rope']); dma('sp', sint[:], sind, [], ['rope'])
        dma('sp', maskp[:], maskpd, [], ['mask']); dma('sp', masks[:], masksd, [], ['mask'])
        dma('sp', sinkb[:], sinkbd, [], ['sink']); dma('sp', sinkc[:], sinkcd, [], ['sink'])
        S.op('dve', lambda e: e.memset(ones[:], 1.0), writes=['ones'])
        S.op('dve', lambda e: e.memset(hstate[:], 0.0), writes=['hstate'])
        AR.reset()
        sva = AR.f32([112, 128]); svb = AR.f32([112, 128]); cin = AR.f32([17, 1024]); csl = AR.f32([17, 1024])
        xin = [AR.f32([128, 1024]), AR.f32([128, 1024])]
        dma('sp', sva, svd[0:112, :], [], ['sva']); dma('sp', svb, svd[112:224, :], [], ['svb'])
        dma('sp', cin, cc, [], ['cin'])
        b0 = pbank()
        tr(pbs[b0][:, 0:112], sva, ['sva'], [('pb', b0)])
        tr(pbs[b0][:, 112:224], svb, ['svb'], [('pb', b0)])
        cp('dve', svT[:, :], pbs[b0][:, 0:224], [('pb', b0)], ['svT'])
        act(csl, cin, AF.Silu, ['cin'], ['csl'])
        b0 = pbank()
        for k in range(NCH):
            tr(pbs[b0][:, k * 17:(k + 1) * 17], csl[:, k * 128:(k + 1) * 128], ['csl'], [('pb', b0)])
        cp('dve', cTc[:, :, :], pbs[b0][:, 0:136].rearrange("p (k n) -> p k n", n=17), [('pb', b0)], ['cTc'])
        for t in range(17):
            xi = xin[t % 2]
            src = xp[t * 128:(t + 1) * 128, :] if t < 16 else xs
            dma('sp', xi, src, [], [('xin', t % 2)])
            for hf in range(2):
                b0 = pbank()
                for j in range(4):
                    c = hf * 4 + j
                    tr(pbs[b0][:, j * 128:(j + 1) * 128], xi[:, c * 128:(c + 1) * 128], [('xin', t % 2)], [('pb', b0)])
                cp('act' if hf == 0 else 'dve', xT[:, hf * 4:hf * 4 + 4, t * 128:(t + 1) * 128],
                   pbs[b0][:, :].rearrange("p (j n) -> p j n", n=128), [('pb', b0)],
                   [('xT', c, min(t // 4, 4)) for c in range(hf * 4, hf * 4 + 4)])
        act(rgc[:, 32:40], svT[:, SV_LM:SV_LM + 8], AF.Exp, ['svT'], ['rgc_t'], scale=-1.0)
        act(rgc[:, 32:40], rgc[:, 32:40], AF.Ln, ['rgc_t'], ['rgc_t'], bias=1.0)
        act(rgc[:, 0:8], rgc[:, 32:40], AF.Identity, ['rgc_t'], ['rgc'], scale=-8.0)
        act(rgc[:, 8:16], rgc[:, 32:40], AF.Identity, ['rgc_t'], ['rgc'], scale=-16.0)
        act(rgc[:, 16:32], svT[:, SV_GB:SV_GB + 16], AF.Identity, ['svT'], ['rgc'], scale=-1.0)

        def compute_mod(w_dram, vlist, dst, bias_row0, resfn, slot0=None):
            for v in vlist:
                bk = pbank()
                wv, wres = wload(w_dram[v].rearrange("p (k n) -> p k n", k=NCH), [128, NCH, 1024], slot0=slot0)
                for c in range(NCH):
                    o = pbs[bk][:, c * 17:(c + 1) * 17]
                    for k in range(NCH):
                        mm(o, wv[:, k, c * 128:(c + 1) * 128], cTc[:, k, :], k == 0, k == NCH - 1, wres + ['cTc'], [('pb', bk)])
                tt(dst[:, v * 8:(v + 1) * 8, :], pbs[bk][:, 0:136].rearrange("p (a b) -> p a b", b=17),
                   svT[:, bias_row0 + v * 8: bias_row0 + v * 8 + 8].unsqueeze(2).to_broadcast([128, 8, 17]), ALU.add,
                   [('pb', bk), 'svT'], [resfn(v)])

        def fold_scale(dst, sc0, g_row0, res):
            stt(dst[:, sc0:sc0 + 8, :], dst[:, sc0:sc0 + 8, :], 1.0,
                svT[:, g_row0:g_row0 + 8].unsqueeze(2).to_broadcast([128, 8, 17]), ALU.add, ALU.mult, [res, 'svT'], [res])

        def compute_mod_tail(w_dram, nvec, dst, bias_row0, res, tbs, tmp, tmpres):
            for v in range(nvec):
                bk = pbank()
                for half in range(2):
                    tb = tbs[half]; tres = ('tailb', half)
                    dma('pool', tb, w_dram[v][:, half * 4096:(half + 1) * 4096].rearrange("p (k n) -> p k n", k=4), [('hTf', 0, 0)], [tres])
                    for c in range(NCH):
                        o = pbs[bk][:, half * 136 + c * 17:half * 136 + (c + 1) * 17]
                        for k in range(4):
                            mm(o, tb[:, k, c * 128:(c + 1) * 128], cTc[:, half * 4 + k, :], k == 0, k == 3, [tres, 'cTc'], [('pb', bk)])
                tt(tmp, pbs[bk][:, 0:136].rearrange("p (a b) -> p a b", b=17),
                   svT[:, bias_row0 + v * 8: bias_row0 + v * 8 + 8].unsqueeze(2).to_broadcast([128, 8, 17]), ALU.add, [('pb', bk), 'svT'], [tmpres])
                tt(dst[:, v * 8:(v + 1) * 8, :], tmp, pbs[bk][:, 136:272].rearrange("p (a b) -> p a b", b=17), ALU.add, [('pb', bk), tmpres], [res])

        def bc_s(v):
            return v.unsqueeze(2).to_broadcast([128, 16, 8])

        def v3(ap):
            return ap.rearrange("p (s t) -> p s t", t=8)

        def norm_group(g, A, B, modres, hdst, hres, X, out_f32_scale=None, bank=None):
            t0, n = GROUPS[g]
            mr = list(modres) if isinstance(modres, list) else [modres]
            bk = pbank() if bank is None else bank
            for c in range(NCH):
                q = X[c % 2][:, :].bitcast(BF16); qr = 'X%d' % (c % 2)
                act(q[:, 0:n], xT[:, c, t0:t0 + n], AF.Square, [('xT', c, g)], [qr])
                mm(pbs[bk][:, 0:n], ones[:, :], q[:, 0:n], c == 0, c == NCH - 1, ['ones', qr], [('pb', bk)])
            rstd = X[2]
            act(rstd[:, 0:n], pbs[bk][:, 0:n], AF.Ln, [('pb', bk)], ['X2'], scale=1.0 / D, bias=EPS)
            act(rstd[:, 0:n], rstd[:, 0:n], AF.Exp, ['X2'], ['X2'], scale=-0.5)
            for c in range(NCH):
                tb = X[c % 2]; tr_ = 'X%d' % (c % 2)
                tt(tb[:, 0:n], xT[:, c, t0:t0 + n], rstd[:, 0:n], ALU.mult, [('xT', c, g), 'X2'], [tr_])
                if out_f32_scale is not None:
                    act(hdst[:, c, 0:n], tb[:, 0:n], AF.Identity, [tr_, 'svT'], [hres(c)], scale=out_f32_scale(c))
                elif g < 4:
                    act(hdst[:, c, 0:n], tb[:, 0:n], AF.Identity, [tr_] + mr, [hres(c)],
                        scale=A[:, c, 0:1], bias=B[:, c, 0:1])
                else:
                    tt(v3(tb[:, 0:n]), v3(tb[:, 0:n]), bc_s(A[:, c, 1:17]), ALU.mult, [tr_] + mr, [tr_])
                    tt(v3(hdst[:, c, 0:n]), v3(tb[:, 0:n]), bc_s(B[:, c, 1:17]), ALU.add, [tr_] + mr, [hres(c)])

        def residual(g, fc, bk, Gm, modres, tmp, tmpres):
            t0, n = GROUPS[g]
            mr = list(modres) if isinstance(modres, list) else [modres]
            if g < 4:
                stt(xT[:, fc, t0:t0 + n], pbs[bk][:, 0:n], Gm[:, fc, 0:1], xT[:, fc, t0:t0 + n], ALU.mult, ALU.add,
                    [('pb', bk), ('xT', fc, g)] + mr, [('xT', fc, g)])
            else:
                tt(v3(tmp[:, 0:n]), v3(pbs[bk][:, 0:n]), bc_s(Gm[:, fc, 1:17]), ALU.mult, [('pb', bk)] + mr, [tmpres])
                tt(xT[:, fc, t0:t0 + n], xT[:, fc, t0:t0 + n], tmp[:, 0:n], ALU.add, [tmpres, ('xT', fc, g)], [('xT', fc, g)])

        def ffn_stage(l):
            S.barrier()
            AR.reset()
            hT = AR.bf16([128, NCH, T])
            hid = [AR.bf16([128, 4, 512]), AR.bf16([128, 4, 512])]
            X = [AR.f32([128, 512]), AR.f32([128, 512]), AR.f32([128, 512])]
            A = modT[:, 32:40, :]; B = modT[:, 24:32, :]; Gm = modT[:, 40:48, :]
            for g in range(5):
                t0, n = GROUPS[g]
                norm_group(g, A, B, [('mod', 3), ('mod', 4)], hT[:, :, t0:t0 + n], (lambda g: lambda c: ('hTf', c, g))(g), X)
            if l == 0:
                tbs = [AR.bf16([128, 4, 1024]), AR.bf16([128, 4, 1024])]
                modT1 = AR.f32([128, 48, 17]); mtmp = AR.f32([128, 8, 17])
                compute_mod_tail(kvadaP, 2, kvmodT, SV_KVB, 'kvmod', tbs, mtmp, 'mtmp')
                fold_scale(kvmodT, 8, SV_KVG, 'kvmod')
                compute_mod_tail(adaP[1], 6, modT1, SV_ADAB + 48, 'modT1', tbs, mtmp, 'mtmp')
                fold_scale(modT1, 8, SV_NG + 16, 'modT1')
                fold_scale(modT1, 32, SV_NG + 24, 'modT1')
            it = 0
            for (h0, hc) in FFN_GROUPS:
                o0 = h0 * 3072
                wg, rg_ = wload(ffnP[l][:, o0:o0 + hc * 1024].rearrange("p (k n) -> p k n", k=NCH), [128, NCH, hc * 128])
                wu, ru_ = wload(ffnP[l][:, o0 + hc * 1024:o0 + hc * 2048].rearrange("p (k n) -> p k n", k=NCH), [128, NCH, hc * 128])
                wo, ro_ = wload(ffnP[l][:, o0 + hc * 2048:o0 + hc * 3072].rearrange("p (j n) -> p j n", j=hc), [128, hc, 1024])
                for g in range(5):
                    t0, n = GROUPS[g]
                    hb = hid[it % 2]; hres = ('hid', it % 2); it += 1
                    for j in range(hc):
                        bg = pbank(); bu = pbank()
                        for k in range(NCH):
                            mm(pbs[bg][:, 0:n], wg[:, k, j * 128:(j + 1) * 128], hT[:, k, t0:t0 + n], k == 0, k == NCH - 1,
                               rg_ + [('hTf', k, g)], [('pb', bg)])
                        for k in range(NCH):
                            mm(pbs[bu][:, 0:n], wu[:, k, j * 128:(j + 1) * 128], hT[:, k, t0:t0 + n], k == 0, k == NCH - 1,
                               ru_ + [('hTf', k, g)], [('pb', bu)])
                        s_ = X[j % 2]; sr = 'X%d' % (j % 2)
                        act(s_[:, 0:n], pbs[bg][:, 0:n], AF.Silu, [('pb', bg)], [sr])
                        tt(hb[:, j, 0:n], s_[:, 0:n], pbs[bu][:, 0:n], ALU.mult, [sr, ('pb', bu)], [hres])
                    for fc in range(NCH):
                        bo = pbank()
                        for j in range(hc):
                            mm(pbs[bo][:, 0:n], wo[:, j, fc * 128:(fc + 1) * 128], hb[:, j, 0:n], j == 0, j == hc - 1,
                               ro_ + [hres], [('pb', bo)])
                        residual(g, fc, bo, Gm, [('mod', 5)], X[2], 'X2')
            if l == 0:
                cp('dve', modT[:, :, :], modT1, ['modT1'], [('mod', v) for v in range(6)])

        if stop <= 0:
            return _finish()
        S.stage = 'mod0'
        mres = lambda v: ('mod', v)
        compute_mod(adaP[0], [0, 1, 2], modT, SV_ADAB, mres)
        fold_scale(modT, 8, SV_NG + 0, ('mod', 1))
        ckpt(0.2)
        S.barrier()
        AR.reset()
        hT2 = [AR.bf16([128, NCH, 512]), AR.bf16([128, NCH, 512])]
        xpad = AR.f32([128, NCH, 515])
        xpads = xpad[:, :, 256:432].rearrange("p c (s j) -> p c s j", j=11)
        oT = AR.bf16([128, NCH, 512])
        offX = AR.off
        X = [AR.f32([128, 512]), AR.f32([128, 512]), AR.f32([128, 512])]
        stio2 = arena[0:48, offX:offX + 1024]
        xc2 = [AR.f32([128, 512]), AR.f32([128, 512])]; xcb2 = [AR.bf16([128, 512]), AR.bf16([128, 512])]
        ta2 = [AR.f32([128, 512]), AR.f32([128, 512])]; tb2 = [AR.f32([128, 512]), AR.f32([128, 512])]
        ti2 = [AR.f32([128, 512]), AR.f32([128, 512])]; gq2 = [AR.f32([128, 512]), AR.f32([128, 512])]
        blk2 = [AR.f32([128, 1024]), AR.f32([128, 1024])]
        stio = blk2[0][0:48, :]; ti = ti2[0]
        h0T = AR.f32([128, NCH, 16]); tcc = AR.f32([128, 48])
        S.op('dve', lambda e: e.memset(xpad[:, :, 0:3], 0.0), writes=[('xpad', c) for c in range(NCH)])
        dma('sp', stio[0:16, :], shin, [], ['stio'])
        b0 = pbank()
        for c in range(NCH):
            tr(pbs[b0][:, c * 16:(c + 1) * 16], stio[0:16, c * 128:(c + 1) * 128], ['stio'], [('pb', b0)])
        cp('dve', h0T[:, :, :], pbs[b0][:, 0:128].rearrange("p (c s) -> p c s", s=16), [('pb', b0)], ['h0T'])
        S.barrier()

        ckpt(0.3)
        S.stage = 'rglru'
        win, rwin = wload(rnn_w_in.rearrange("p (k n) -> p k n", k=NCH), [128, NCH, 2048], slot0=0)
        wgt, rwgt = wload(gate_w.rearrange("p (k n) -> p k n", k=8), [128, 8, 256], slot0=8)
        compute_mod(adaP[0], [3, 4, 5], modT, SV_ADAB, mres, slot0=9)
        fold_scale(modT, 32, SV_NG + 8, ('mod', 4))
        wout, rwout = wload(rnn_w_out.rearrange("p (k n) -> p k n", k=NCH), [128, NCH, 1024], slot0=9)
        A1 = modT[:, 8:16, :]; B1 = modT[:, 0:8, :]; G1 = modT[:, 16:24, :]
        for g in range(5):
            t0, n = GROUPS[g]
            hT = hT2[g % 2]
            pb_pool[0] = (0, 1, 2, 3, 4, 5, 6)
            norm_group(g, A1, B1, [('mod', 0), ('mod', 1)], hT, (lambda gp: lambda c: ('hT', gp, c))(g % 2), X, bank=7)
            ckpt(0.4 + g * 0.1)
            if g == 4:
                dma('sp', stio2, sconv, [], ['X0', 'X1'])
                b0 = pbank()
                for c in range(NCH):
                    tr(pbs[b0][:, c * 48:(c + 1) * 48], stio2[:, c * 128:(c + 1) * 128], ['X0', 'X1'], [('pb', b0)])
                cp('dve', xpads[:, :, :, 0:3], pbs[b0][:, 0:384].rearrange("p (c s j) -> p c s j", s=16, j=3), [('pb', b0)],
                   [('xpad', c) for c in range(NCH)])
            for c in range(NCH):
                ckpt(0.4 + g * 0.1 + 0.01 * (c + 1))
                pz = c % 2
                xc = xc2[pz]; xcb = xcb2[pz]; ta = ta2[pz]; tb_ = tb2[pz]; ti = ti2[pz]; gq = gq2[pz]
                uu = blk2[pz][:, 0:512]; hh = blk2[pz][:, 512:1024]
                Rxc = 'xc%d' % pz; Rxcb = 'xcb%d' % pz; Rta = 'ta%d' % pz; Rtb = 'tb%d' % pz; Rti = 'ti%d' % pz
                Rgq = 'gq%d' % pz; Ruu = 'uu%d' % pz; Rhh = 'hh%d' % pz
                bx = pbank()
                for k in range(NCH):
                    mm(pbs[bx][:, 0:n], win[:, k, c * 128:(c + 1) * 128], hT[:, k, 0:n], k == 0, k == NCH - 1, rwin + [('hT', g % 2, k)], [('pb', bx)])
                w_ = (lambda c: lambda kk: svT[:, SV_CW + kk * 8 + c: SV_CW + kk * 8 + c + 1])(c)
                cb_ = svT[:, SV_CB + c:SV_CB + c + 1]
                if g < 4:
                    cp('dve', xpad[:, c, 3:3 + n], pbs[bx][:, 0:n], [('pb', bx)], [('xpad', c)])
                    act(xc[:, 0:n], pbs[bx][:, 0:n], AF.Identity, [('pb', bx), 'svT'], [Rxc], scale=w_(3), bias=cb_)
                    for kk in range(3):
                        stt(xc[:, 0:n], xpad[:, c, kk:kk + n], w_(kk), xc[:, 0:n], ALU.mult, ALU.add, [('xpad', c), Rxc, 'svT'], [Rxc])
                    cp('dve', tcc[:, 0:3], xpad[:, c, n:n + 3], [('xpad', c)], ['tcc'])
                    cp('dve', xpad[:, c, 0:3], tcc[:, 0:3], ['tcc'], [('xpad', c)])
                else:
                    cp('dve', xpads[:, c, :, 3:11], v3(pbs[bx][:, 0:n]), [('pb', bx)], [('xpad', c)])
                    act(xc[:, 0:n], pbs[bx][:, 0:n], AF.Identity, [('pb', bx), 'svT'], [Rxc], scale=w_(3), bias=cb_)
                    for kk in range(3):
                        stt(v3(xc[:, 0:n]), xpads[:, c, :, kk:kk + 8], w_(kk), v3(xc[:, 0:n]), ALU.mult, ALU.add, [('xpad', c), Rxc, 'svT'], [Rxc])
                cp('dve', xcb[:, 0:n], xc[:, 0:n], [Rxc], [Rxcb])
                ckpt(0.411)
                br = pbank(); bi = pbank()
                mm(pbs[br][:, 0:n], wgt[:, c, 0:128], xcb[:, 0:n], True, True, rwgt + [Rxcb], [('pb', br)])
                mm(pbs[bi][:, 0:n], wgt[:, c, 128:256], xcb[:, 0:n], True, True, rwgt + [Rxcb], [('pb', bi)])
                ckpt(0.412)
                act(ta[:, 0:n], pbs[br][:, 0:n], AF.Exp, [('pb', br), 'rgc'], [Rta], scale=-1.0, bias=rgc[:, 16 + 2 * c:17 + 2 * c])
                act(ta[:, 0:n], ta[:, 0:n], AF.Ln, [Rta], [Rta], bias=1.0)
                act(ta[:, 0:n], ta[:, 0:n], AF.Exp, [Rta], [Rta], scale=-1.0)
                act(tb_[:, 0:n], ta[:, 0:n], AF.Exp, [Rta, 'rgc'], [Rtb], scale=rgc[:, 8 + c:9 + c])
                act(ta[:, 0:n], ta[:, 0:n], AF.Exp, [Rta, 'rgc'], [Rta], scale=rgc[:, c:c + 1])
                act(tb_[:, 0:n], tb_[:, 0:n], AF.Ln, [Rtb], [Rtb], scale=-1.0, bias=1.0)
                act(tb_[:, 0:n], tb_[:, 0:n], AF.Exp, [Rtb], [Rtb], scale=0.5)
                ckpt(0.413)
                act(ti[:, 0:n], pbs[bi][:, 0:n], AF.Exp, [('pb', bi), 'rgc'], [Rti], scale=-1.0, bias=rgc[:, 17 + 2 * c:18 + 2 * c])
                act(ti[:, 0:n], ti[:, 0:n], AF.Ln, [Rti], [Rti], bias=1.0)
                act(ti[:, 0:n], ti[:, 0:n], AF.Exp, [Rti], [Rti], scale=-1.0)
                tt(uu[:, 0:n], tb_[:, 0:n], ti[:, 0:n], ALU.mult, [Rtb, Rti], [Ruu])
                tt(uu[:, 0:n], uu[:, 0:n], xc[:, 0:n], ALU.mult, [Ruu, Rxc], [Ruu])
                ckpt(0.414)
                if g < 4:
                    S.op('dve', (lambda c=c, n=n, hh=hh, ta=ta, uu=uu: lambda e: e.tensor_tensor_scan(hh[:, 0:n], ta[:, 0:n], uu[:, 0:n], hstate[:, c:c + 1], ALU.mult, ALU.add))(),
                         reads=[Rta, Ruu, ('hstate', c)], writes=[Rhh], cost=0.1 + n / 420.0)
                    cp('dve', hstate[:, c:c + 1], hh[:, n - 1:n], [Rhh], [('hstate', c)])
                else:
                    a3 = v3(ta[:, 0:n]); u3 = v3(uu[:, 0:n])
                    tt(tcc[:, 0:16], a3[:, :, 0], h0T[:, c, :], ALU.mult, [Rta, 'h0T'], ['tcc'])
                    tt(u3[:, :, 0], u3[:, :, 0], tcc[:, 0:16], ALU.add, [Ruu, 'tcc'], [Ruu])
                    S.op('dve', (lambda a3=a3: lambda e: e.memset(a3[:, :, 0:1], 0.0))(), reads=['tcc'], writes=[Rta])
                    S.op('dve', (lambda n=n, hh=hh, ta=ta, uu=uu: lambda e: e.tensor_tensor_scan(hh[:, 0:n], ta[:, 0:n], uu[:, 0:n], 0.0, ALU.mult, ALU.add))(),
                         reads=[Rta, Ruu], writes=[Rhh], cost=0.1 + n / 420.0)
                    cp('dve', tcc[:, 0:16], v3(hh[:, 0:n])[:, :, 7], [Rhh], ['tcc'])
                    bt = pbank()
                    tr(pbs[bt][0:16, 0:128], tcc[:, 0:16], ['tcc'], [('pb', bt)])
                    cp('dve', xpad[0:16, c, 128:256], pbs[bt][0:16, 0:128], [('pb', bt)], [('xpad', c)])
                    dma('sp', sh_o[:, c * 128:(c + 1) * 128], xpad[0:16, c, 128:256], [('xpad', c)], [])
                    cp('dve', tcc[:, 0:48].rearrange("p (s j) -> p s j", j=3), xpads[:, c, :, 8:11], [('xpad', c)], ['tcc'])
                    bt = pbank()
                    tr(pbs[bt][0:48, 0:128], tcc[:, 0:48], ['tcc'], [('pb', bt)])
                    cp('dve', xpad[0:48, c, 0:128], pbs[bt][0:48, 0:128], [('pb', bt)], [('xpad', c)])
                ckpt(0.415)
                by = pbank()
                for k in range(NCH):
                    mm(pbs[by][:, 0:n], win[:, k, 1024 + c * 128:1024 + (c + 1) * 128], hT[:, k, 0:n], k == 0, k == NCH - 1, rwin + [('hT', g % 2, k)], [('pb', by)])
                act(gq[:, 0:n], pbs[by][:, 0:n], AF.Square, [('pb', by)], [Rgq])
                ts(gq[:, 0:n], gq[:, 0:n], 0.044715, 1.0, ALU.mult, ALU.add, [Rgq], [Rgq])
                tt(gq[:, 0:n], gq[:, 0:n], pbs[by][:, 0:n], ALU.mult, [Rgq, ('pb', by)], [Rgq])
                act(gq[:, 0:n], gq[:, 0:n], AF.Exp, [Rgq], [Rgq], scale=-1.5957691216057308)
                act(gq[:, 0:n], gq[:, 0:n], AF.Ln, [Rgq], [Rgq], bias=1.0)
                act(gq[:, 0:n], gq[:, 0:n], AF.Exp, [Rgq], [Rgq], scale=-1.0)
                tt(gq[:, 0:n], gq[:, 0:n], pbs[by][:, 0:n], ALU.mult, [Rgq, ('pb', by)], [Rgq])
                tt(oT[:, c, 0:n], hh[:, 0:n], gq[:, 0:n], ALU.mult, [Rhh, Rgq], [('oT', c)])
                ckpt(0.416)
            for fc in range(NCH):
                bo = pbank()
                for c in range(NCH):
                    mm(pbs[bo][:, 0:n], wout[:, c, fc * 128:(fc + 1) * 128], oT[:, c, 0:n], c == 0, c == NCH - 1, rwout + [('oT', c)], [('pb', bo)])
                residual(g, fc, bo, G1, [('mod', 2)], X[2], 'X2')
            if g == 3:
                for hf in range(2):
                    b1 = pbank()
                    for j in range(4):
                        c = hf * 4 + j
                        tr(pbs[b1][0:3, j * 128:(j + 1) * 128], xpad[:, c, 0:3], [('xpad', c)], [('pb', b1)])
                    cp('dve', ti2[0][0:3, 0:512], pbs[b1][0:3, 0:512], [('pb', b1)], ['ti0'])
                    dma('sp', pconv[:, hf * 512:(hf + 1) * 512], ti2[0][0:3, 0:512], ['ti0'], [])
                for hf in range(2):
                    b1 = pbank()
                    for j in range(4):
                        c = hf * 4 + j
                        tr(pbs[b1][0:1, j * 128:(j + 1) * 128], hstate[:, c:c + 1], [('hstate', c)], [('pb', b1)])
                    cp('dve', ti2[0][0:1, 0:512], pbs[b1][0:1, 0:512], [('pb', b1)], ['ti0'])
                    dma('sp', ph[:, hf * 512:(hf + 1) * 512], ti2[0][0:1, 0:512], ['ti0'], [])
        for c in range(NCH):
            dma('sp', sconv_o[:, c * 128:(c + 1) * 128], xpad[0:48, c, 0:128], [('xpad', c)], [])

        if stop <= 1:
            return _finish()
        pb_pool[0] = (0, 1, 2, 3, 4, 5, 6, 7)
        S.stage = 'ffn0'
        ffn_stage(0)

        if stop <= 2:
            return _finish()
        S.stage = 'kv'
        S.barrier()
        AR.reset()
        AR.limit = ARENA_W
        hT = AR.bf16([128, NCH, 512])
        X = [AR.f32([128, 512]), AR.f32([128, 512]), AR.f32([128, 512])]
        Kf = [AR.f32([128, 256]), AR.f32([128, 256])]; Vf = [AR.f32([128, 256]), AR.f32([128, 256])]
        rt_kv = AR.f32([128, 4, 16, 8])

        def rope(buf3, bres, t, H, rt_=None):
            rt = rt_ if rt_ is not None else rt_kv
            cs = cost[:, t, :].unsqueeze(1).to_broadcast([128, H, 8])
            sn = sint[:, t, :].unsqueeze(1).to_broadcast([128, H, 8])
            x1 = buf3[:, :, 0:8]; x2 = buf3[:, :, 8:16]
            t1 = rt[:, 0, 0:H, :]; t2 = rt[:, 1, 0:H, :]; t3 = rt[:, 2, 0:H, :]; t4 = rt[:, 3, 0:H, :]
            tt(t1, x1, cs, ALU.mult, [bres, 'rope'], ['rt1'])
            tt(t2, x2, sn, ALU.mult, [bres, 'rope'], ['rt2'])
            tt(t3, x2, cs, ALU.mult, [bres, 'rope'], ['rt3'])
            tt(t4, x1, sn, ALU.mult, [bres, 'rope'], ['rt4'])
            tt(x1, t1, t2, ALU.subtract, ['rt1', 'rt2'], [bres])
            tt(x2, t3, t4, ALU.add, ['rt3', 'rt4'], [bres])

        wkv, rkv = wload(w_kv.rearrange("p (k n) -> p k n", k=NCH), [128, NCH, 512])
        for g in range(5):
            t0, n = GROUPS[g]
            pb_pool[0] = (0, 1, 2, 3, 4, 5, 6)
            norm_group(g, kvmodT[:, 8:16, :], kvmodT[:, 0:8, :], 'kvmod', hT, lambda c: ('hT', c), X, bank=7)
            for tl in range(n // 128):
                t = t0 // 128 + tl
                bk = pbank()
                for k in range(NCH):
                    mm(pbs[bk][:, 0:512], hT[:, k, tl * 128:(tl + 1) * 128], wkv[:, k, :], k == 0, k == NCH - 1, rkv + [('hT', k)], [('pb', bk)])
                kf = Kf[t % 2]; vf = Vf[t % 2]; kr = ('Kf', t % 2); vr = ('Vf', t % 2)
                cp('act', kf[:, :], pbs[bk][:, 0:256], [('pb', bk)], [kr])
                cp('act', vf[:, :], pbs[bk][:, 256:512], [('pb', bk)], [vr])
                rope(kf.rearrange("p (h d) -> p h d", d=64), kr, t, 4)
                cp('dve', Vb[:, t, :], vf[:, :], [vr], [('Vb', t)])
                for gp in range(2):
                    bt = pbank()
                    tr(pbs[bt][:, 0:128], kf[:, gp * 128:(gp + 1) * 128], [kr], [('pb', bt)])
                    cp('act', KT[:, gp, t * 128:(t + 1) * 128], pbs[bt][:, 0:128], [('pb', bt)], [('KT', t)])
                if t == 15:
                    dma('sp', pk, kf[:, :], [kr], []); dma('sp', pv, vf[:, :], [vr], [])
                if t == 16:
                    for s in range(16):
                        dma('sp', sk[s, 120:128, :], kf[8 * s:8 * s + 8, :], [kr], [])
                        dma('sp', svo[s, 120:128, :], vf[8 * s:8 * s + 8, :], [vr], [])
        dma('sp', sk[:, 0:120, :], ck[:, 8:128, :], [], [])
        dma('sp', svo[:, 0:120, :], cv[:, 8:128, :], [], [])

        if stop <= 3:
            return _finish()
        S.stage = 'attn'
        wq, rwq = wload(w_q.rearrange("p (k n) -> p k n", k=NCH), [128, NCH, 1024])
        wo_, rwo = wload(w_o.rearrange("p (k n) -> p k n", k=NCH), [128, NCH, 1024])
        A1 = modT[:, 8:16, :]; B1 = modT[:, 0:8, :]; G1 = modT[:, 16:24, :]

        def attn_phase(groups, NKMAX, sample, shared=None):
            if sample:
                S.barrier(engines=('pe', 'act', 'dve', 'sp', 'pool'))
                AR.reset()
            NBUF = 2 if sample else 6
            pb_pool[0] = (2, 3, 4, 5, 6, 7)
            ncol = 512 if not sample else 128
            if shared is not None:
                hT, X = shared
            else:
                hT = AR.bf16([128, NCH, ncol])
                X = [AR.f32([128, ncol]), AR.f32([128, ncol]), AR.f32([128, ncol])]
            NT = 1 if sample else 2
            Qf2 = [AR.f32([128, 1024]) for _ in range(NT)]; QT2 = [AR.bf16([128, 8, 128]) for _ in range(NT)]
            Of2 = [AR.f32([128, 1024]) for _ in range(NT)]; OT = AR.bf16([128, NCH, ncol])
            sm2 = [AR.f32([128, 6, 16]) for _ in range(NT)]
            Sm = [AR.f32([128, NKMAX * 128]) for _ in range(NBUF)]
            ET = [AR.bf16([128, NKMAX, 128]) for _ in range(NBUF)]
            if sample:
                KTc = AR.bf16([128, 2, 2048]); Vc = AR.bf16([128, 16, 256])
                ckf = [AR.f32([128, 256]), AR.f32([128, 256])]
                dma('pool', Vc[:, :, :], cv.rearrange("s k d -> k s d"), [], ['Vc'])
                for s in range(16):
                    cb = ckf[s % 2]; cr = ('ckf', s % 2)
                    dma('sp', cb[:, :], ck[s], [], [cr])
                    for gp in range(2):
                        bt = pbank()
                        tr(pbs[bt][:, 0:128], cb[:, gp * 128:(gp + 1) * 128], [cr], [('pb', bt)])
                        cp('act' if gp == 0 else 'dve', KTc[:, gp, s * 128:(s + 1) * 128], pbs[bt][:, 0:128], [('pb', bt)], [('KTc', s)])
            for g in groups:
                t0, n = GROUPS[g]
                norm_group(g, A1, B1, [('mod', 0), ('mod', 1)], hT, lambda c: ('hT', c), X)
                for tl in range(n // 128):
                    t = t0 // 128 + tl
                    tp = t % NT
                    Qf = Qf2[tp]; QT = QT2[tp]; Of = Of2[tp]; sm = sm2[tp]
                    RQf = 'Qf%d' % tp; RQT = 'QT%d' % tp; ROf = 'Of%d' % tp; RS = 'sm%d' % tp
                    for hf in range(2):
                        bq = pbank()
                        for k in range(NCH):
                            mm(pbs[bq][:, 0:512], hT[:, k, tl * 128:(tl + 1) * 128], wq[:, k, hf * 512:(hf + 1) * 512], k == 0, k == NCH - 1,
                               rwq + [('hT', k)], [('pb', bq)])
                        act(Qf.rearrange("p (j h d) -> p j h d", h=2, d=64)[:, hf * 4:(hf + 1) * 4, :, :],
                            pbs[bq][:, 0:512].rearrange("p (h b d) -> p b h d", h=2, b=4), AF.Identity, [('pb', bq)], [RQf], scale=0.125)
                    rope(Qf.rearrange("p (h d) -> p h d", d=64), RQf, t, 16)
                    for hf in range(2):
                        bt = pbank()
                        for jj in range(4):
                            j = hf * 4 + jj
                            tr(pbs[bt][:, jj * 128:(jj + 1) * 128], Qf[:, j * 128:(j + 1) * 128], [RQf], [('pb', bt)])
                        cp('act', QT[:, hf * 4:hf * 4 + 4, :], pbs[bt][:, :].rearrange("p (j n) -> p j n", n=128), [('pb', bt)], [RQT])
                    if not sample:
                        chunks = []
                        if t > 0:
                            chunks.append((lambda gp, t=t: KT[:, gp, (t - 1) * 128:t * 128], ('KT', t - 1), lambda kvh, t=t: Vb[:, t - 1, kvh * 64:(kvh + 1) * 64], ('Vb', t - 1), maskp[:, 0:128], None))
                        chunks.append((lambda gp, t=t: KT[:, gp, t * 128:(t + 1) * 128], ('KT', t), lambda kvh, t=t: Vb[:, t, kvh * 64:(kvh + 1) * 64], ('Vb', t), maskp[:, 128:256], None))
                    else:
                        chunks = []
                        for s in range(16):
                            chunks.append((lambda gp, s=s: KTc[:, gp, s * 128:(s + 1) * 128], ('KTc', s), lambda kvh, s=s: Vc[:, s, kvh * 64:(kvh + 1) * 64], 'Vc', masks[:, 0:128], s))
                        chunks.append((lambda gp: KT[:, gp, 2048:2176], ('KT', 16), lambda kvh: Vb[:, 16, kvh * 64:(kvh + 1) * 64], ('Vb', 16), masksn[:, :], None))
                    nk = len(chunks)
                    bo2 = [0, 1]
                    for h in range(16):
                        kvh = h // 4; gp = kvh // 2; half = kvh % 2
                        j = 4 * (kvh // 2) + h % 4
                        p0 = half * 64
                        Sb = Sm[h % NBUF]; sres = ('Sm', h % NBUF)
                        for ci in range(0, nk, 4):
                            bs = pbank()
                            for cj in range(ci, min(ci + 4, nk)):
                                kfn, kres, vfn, vres, mk, rs_ = chunks[cj]
                                mm(pbs[bs][:, (cj - ci) * 128:(cj - ci + 1) * 128], QT[p0:p0 + 64, j, :], kfn(gp)[p0:p0 + 64, :], True, True,
                                   [RQT, kres], [('pb', bs)])
                            if (not sample) and nk == 2:
                                tt(Sb[:, 0:256], pbs[bs][:, 0:256], maskp[:, 0:256], ALU.add, [('pb', bs), 'mask'], [sres])
                                continue
                            for cj in range(ci, min(ci + 4, nk)):
                                kfn, kres, vfn, vres, mk, rs_ = chunks[cj]
                                if rs_ is None:
                                    tt(Sb[:, cj * 128:(cj + 1) * 128], pbs[bs][:, (cj - ci) * 128:(cj - ci + 1) * 128], mk, ALU.add, [('pb', bs), 'mask'], [sres])
                                else:
                                    stt(Sb[:, cj * 128:(cj + 1) * 128], pbs[bs][:, (cj - ci) * 128:(cj - ci + 1) * 128], rowm[:, rs_:rs_ + 1], mk, ALU.add, ALU.add,
                                        [('pb', bs), 'mask'], [sres])
                        S.op('dve', (lambda Sb=Sb, nk=nk, h=h, sm=sm: lambda e: e.reduce_max(sm[:, 0, h:h + 1], Sb[:, 0:nk * 128], AX.X))(), reads=[sres], writes=[(RS + 'mx', h)], cost=0.08 + nk * 128 / 960.0)
                        ts(sm[:, 1, h:h + 1], sm[:, 0, h:h + 1], sinkb[:, h:h + 1], -1.0, ALU.max, ALU.mult, [(RS + 'mx', h), 'sink'], [(RS + 'nm', h)])
                        act(Sb[:, 0:nk * 128], Sb[:, 0:nk * 128], AF.Exp, [sres, (RS + 'nm', h)], [sres, (RS + 'rs', h)], bias=sm[:, 1, h:h + 1], accum=sm[:, 2, h:h + 1])
                        Eb = ET[h % NBUF]; eres = ('ET', h % NBUF)
                        for ci in range(0, nk, 4):
                            bt = pbank()
                            m_ = min(ci + 4, nk) - ci
                            for cj in range(ci, ci + m_):
                                tr(pbs[bt][:, (cj - ci) * 128:(cj - ci + 1) * 128], Sb[:, cj * 128:(cj + 1) * 128], [sres], [('pb', bt)])
                            cp('act' if (ci // 4) % 2 == 0 else 'dve', Eb[:, ci:ci + m_, :], pbs[bt][:, 0:m_ * 128].rearrange("p (j n) -> p j n", n=128), [('pb', bt)], [eres])
                        bo = bo2[h // 8]
                        for cj in range(nk):
                            kfn, kres, vfn, vres, mk, rs_ = chunks[cj]
                            mm(pbs[bo][:, (h % 8) * 64:(h % 8 + 1) * 64], Eb[:, cj, :], vfn(kvh), cj == 0, cj == nk - 1, [eres, vres], [('pb', bo)])
                        if h % 8 == 7:
                            hf = h // 8; hs = slice(hf * 8, hf * 8 + 8)
                            hl = list(range(hf * 8, hf * 8 + 8))
                            tt(sm[:, 3, hs], sinkb[:, hs], sm[:, 1, hs], ALU.add, ['sink'] + [(RS + 'nm', x) for x in hl], [(RS + 'es', hf)])
                            act(sm[:, 3, hs], sm[:, 3, hs], AF.Exp, [(RS + 'es', hf)], [(RS + 'es', hf)])
                            tt(sm[:, 4, hs], sm[:, 2, hs], sm[:, 3, hs], ALU.add, [(RS + 'es', hf)] + [(RS + 'rs', x) for x in hl], [(RS + 'den', hf)])
                            S.op('dve', (lambda sm=sm, hs=hs: lambda e: e.reciprocal(sm[:, 5, hs], sm[:, 4, hs]))(), reads=[(RS + 'den', hf)], writes=[(RS + 'rden', hf)], cost=0.1)
                            tt(Of[:, hf * 512:(hf + 1) * 512].rearrange("p (h d) -> p h d", d=64), pbs[bo][:, 0:512].rearrange("p (h d) -> p h d", d=64),
                               sm[:, 5, hs].unsqueeze(2).to_broadcast([128, 8, 64]), ALU.mult, [('pb', bo), (RS + 'rden', hf)], [ROf])
                    for hf in range(2):
                        bt = pbank()
                        for jj in range(4):
                            c = hf * 4 + jj
                            tr(pbs[bt][:, jj * 128:(jj + 1) * 128], Of[:, c * 128:(c + 1) * 128], [ROf], [('pb', bt)])
                        cp('act', OT[:, hf * 4:hf * 4 + 4, tl * 128:(tl + 1) * 128], pbs[bt][:, :].rearrange("p (j n) -> p j n", n=128), [('pb', bt)],
                           [('OT', c) for c in range(hf * 4, hf * 4 + 4)])
                for fc in range(NCH):
                    bo = pbank()
                    for c in range(NCH):
                        mm(pbs[bo][:, 0:n], wo_[:, c, fc * 128:(fc + 1) * 128], OT[:, c, 0:n], c == 0, c == NCH - 1, rwo + [('OT', c)], [('pb', bo)])
                    residual(g, fc, bo, G1, [('mod', 2)], X[2], 'X2')

        def attn_sample():
            g = 4
            t0, n = GROUPS[g]
            t = 16
            NB3 = 6
            S.barrier(engines=('pe', 'act', 'dve', 'sp', 'pool'))
            AR.reset()
            pb_pool[0] = (2, 3, 4, 5, 6, 7)
            hT = AR.bf16([128, NCH, 128]); X = [AR.f32([128, 128]) for _ in range(3)]
            Qf = AR.f32([128, 1024]); QTs = AR.bf16([128, 16, 8, 8])
            KTc = AR.bf16([128, 2, 2048]); Vc = AR.bf16([128, 16, 256])
            ckf = [AR.f32([128, 256]), AR.f32([128, 256])]
            Smb = [AR.f32([128, 136]) for _ in range(NB3)]
            Epad = [AR.f32([128, 128]) for _ in range(NB3)]
            ETb = [AR.bf16([128, 2, 128]) for _ in range(NB3)]
            Of = AR.f32([128, 1024]); OT = AR.bf16([128, NCH, 128])
            sm = AR.f32([128, 8, 16])
            rt_s = AR.f32([128, 4, 16, 8])
            for i in range(NB3):
                S.op('dve', (lambda i=i: lambda e: e.memset(Epad[i][:, :], 0.0))(), writes=[('Epad', i)])
            for s in range(16):
                dma('pool', Vc[:, s, :], cv[s], [], [('Vc', s)])
                cb = ckf[s % 2]; cr = ('ckf', s % 2)
                dma('sp', cb[:, :], ck[s], [], [cr])
                for gp in range(2):
                    bt = pbank()
                    tr(pbs[bt][:, 0:128], cb[:, gp * 128:(gp + 1) * 128], [cr], [('pb', bt)])
                    cp('act' if gp == 0 else 'dve', KTc[:, gp, s * 128:(s + 1) * 128], pbs[bt][:, 0:128], [('pb', bt)], [('KTc', s)])
            norm_group(g, A1, B1, [('mod', 0), ('mod', 1)], hT, lambda c: ('hT', c), X)
            for hf in range(2):
                bq = pbank()
                for k in range(NCH):
                    mm(pbs[bq][:, 0:512], hT[:, k, 0:128], wq[:, k, hf * 512:(hf + 1) * 512], k == 0, k == NCH - 1, rwq + [('hT', k)], [('pb', bq)])
                act(Qf.rearrange("p (j h d) -> p j h d", h=2, d=64)[:, hf * 4:(hf + 1) * 4, :, :],
                    pbs[bq][:, 0:512].rearrange("p (h b d) -> p b h d", h=2, b=4), AF.Identity, [('pb', bq)], ['Qf0'], scale=0.125)
            rope(Qf.rearrange("p (h d) -> p h d", d=64), 'Qf0', t, 16, rt_s)
            for hf in range(2):
                bt = pbank()
                for jj in range(4):
                    j = hf * 4 + jj
                    tr(pbs[bt][:, jj * 128:(jj + 1) * 128], Qf[:, j * 128:(j + 1) * 128], ['Qf0'], [('pb', bt)])
                cp('act', QTs[:, :, hf * 4:hf * 4 + 4, :], pbs[bt][:, :].rearrange("p (j s q) -> p s j q", j=4, s=16), [('pb', bt)], ['QTs'])
            bo2 = [0, 1]
            for s in range(16):
                i3 = s % NB3
                Sb = Smb[i3]; sres = ('Smb', i3); Ep = Epad[i3]; epres = ('Epad', i3); Eb = ETb[i3]; eres = ('ETb', i3)
                bs = pbank()
                for kvh in range(4):
                    gp = kvh // 2; p0 = (kvh % 2) * 64
                    lq = QTs[p0:p0 + 64, s, 4 * gp:4 * gp + 4, :].rearrange("p j q -> p (j q)")
                    mm(pbs[bs][32 * kvh:32 * kvh + 32, 0:128], lq, KTc[p0:p0 + 64, gp, s * 128:(s + 1) * 128], True, True,
                       ['QTs', ('KTc', s)], [('pb', bs)], tile_position=(p0, 32 * kvh))
                    mm(pbs[bs][32 * kvh:32 * kvh + 32, 128:136], lq, KT[p0:p0 + 64, gp, 2048 + 8 * s:2048 + 8 * s + 8], True, True,
                       ['QTs', ('KT', 16)], [('pb', bs)], tile_position=(p0, 32 * kvh))
                tt(Sb[:, 0:136], pbs[bs][:, 0:136], masks[:, 0:136], ALU.add, [('pb', bs), 'mask'], [sres])
                S.op('dve', (lambda Sb=Sb, s=s, sm=sm: lambda e: e.reduce_max(sm[:, 0, s:s + 1], Sb[:, 0:136], AX.X))(), reads=[sres], writes=[('s_mx', s)], cost=0.25)
                ts(sm[:, 1, s:s + 1], sm[:, 0, s:s + 1], sinkc[:, 0:1], -1.0, ALU.max, ALU.mult, [('s_mx', s), 'sink'], [('s_nm', s)])
                act(Sb[:, 0:128], Sb[:, 0:128], AF.Exp, [sres, ('s_nm', s)], [sres, ('s_rs', s)], bias=sm[:, 1, s:s + 1], accum=sm[:, 2, s:s + 1])
                act(Ep[:, 8 * s:8 * s + 8], Sb[:, 128:136], AF.Exp, [sres, ('s_nm', s), epres], [epres, ('s_rs2', s)], bias=sm[:, 1, s:s + 1], accum=sm[:, 3, s:s + 1])
                bt = pbank()
                tr(pbs[bt][:, 0:128], Sb[:, 0:128], [sres], [('pb', bt)])
                tr(pbs[bt][:, 128:256], Ep[:, :], [epres], [('pb', bt)])
                cp('act' if s % 2 == 0 else 'dve', Eb[:, :, :], pbs[bt][:, 0:256].rearrange("p (j n) -> p j n", n=128), [('pb', bt)], [eres])
                S.op('dve', (lambda Ep=Ep, s=s: lambda e: e.memset(Ep[:, 8 * s:8 * s + 8], 0.0))(), reads=[epres], writes=[epres], cost=0.1)
                bo = bo2[s // 8]
                for kvh in range(4):
                    o_ = pbs[bo][32 * kvh:32 * kvh + 32, (s % 8) * 64:(s % 8 + 1) * 64]
                    mm(o_, Eb[:, 0, 32 * kvh:32 * kvh + 32], Vc[:, s, kvh * 64:(kvh + 1) * 64], True, False, [eres, ('Vc', s)], [('pb', bo)], tile_position=(0, 32 * kvh))
                    mm(o_, Eb[:, 1, 32 * kvh:32 * kvh + 32], Vb[:, 16, kvh * 64:(kvh + 1) * 64], False, True, [eres, ('Vb', 16)], [('pb', bo)], tile_position=(0, 32 * kvh))
            alls = lambda nm: [(nm, s) for s in range(16)]
            tt(sm[:, 5, :], sm[:, 2, :], sm[:, 3, :], ALU.add, alls('s_rs') + alls('s_rs2'), ['s_den'])
            act(sm[:, 4, :], sm[:, 1, :], AF.Exp, alls('s_nm') + ['sink'], ['s_es'], bias=sinkc[:, 0:1])
            tt(sm[:, 5, :], sm[:, 5, :], sm[:, 4, :], ALU.add, ['s_den', 's_es'], ['s_den'])
            S.op('dve', lambda e: e.reciprocal(sm[:, 6, :], sm[:, 5, :]), reads=['s_den'], writes=['s_rden'], cost=0.1)
            Os = Qf
            for hf in range(2):
                tt(Os[:, hf * 512:(hf + 1) * 512].rearrange("p (s d) -> p s d", d=64), pbs[bo2[hf]][:, 0:512].rearrange("p (s d) -> p s d", d=64),
                   sm[:, 6, hf * 8:hf * 8 + 8].unsqueeze(2).to_broadcast([128, 8, 64]), ALU.mult, [('pb', bo2[hf]), 's_rden'], ['Qf0'])
            dma('sp', scr_o, Os[:, :], ['Qf0'], ['scr_o'])
            srcv = scr_o.rearrange("(h q) (s d) -> q s h d", q=8, d=64)
            for q in range(8):
                dma('sp', Of[q::8, :].rearrange("p (h d) -> p h d", d=64), srcv[q], ['scr_o'], ['Of0'])
            for hf in range(2):
                bt = pbank()
                for jj in range(4):
                    c = hf * 4 + jj
                    tr(pbs[bt][:, jj * 128:(jj + 1) * 128], Of[:, c * 128:(c + 1) * 128], ['Of0'], [('pb', bt)])
                cp('act', OT[:, hf * 4:hf * 4 + 4, 0:128], pbs[bt][:, :].rearrange("p (j n) -> p j n", n=128), [('pb', bt)],
                   [('OT', c) for c in range(hf * 4, hf * 4 + 4)])
            for fc in range(NCH):
                bo = pbank()
                for c in range(NCH):
                    mm(pbs[bo][:, 0:n], wo_[:, c, fc * 128:(fc + 1) * 128], OT[:, c, 0:n], c == 0, c == NCH - 1, rwo + [('OT', c)], [('pb', bo)])
                residual(g, fc, bo, G1, [('mod', 2)], X[2], 'X2')

        rowm = sb("rowm_sb", [128, 16]); masksn = sb("masksn_sb", [128, 128])
        rowmd = din("rowm", [128, 16]); masksnd = din("masksn", [128, 128])
        dma('sp', rowm[:], rowmd, [], ['mask']); dma('sp', masksn[:], masksnd, [], ['mask'])
        attn_phase([0, 1, 2, 3], 2, False, shared=(hT, X))
        if stop <= 4:
            return _finish()
        S.stage = 'attn_s'
        attn_sample()
        pb_pool[0] = (0, 1, 2, 3, 4, 5, 6, 7)

        if stop <= 5:
            return _finish()
        S.stage = 'ffn1'
        ffn_stage(1)

        if stop <= 6:
            return _finish()
        S.stage = 'final'
        S.barrier()
        AR.reset()
        yT = AR.f32([128, NCH, 512])
        X = [AR.f32([128, 512]), AR.f32([128, 512]), AR.f32([128, 512])]
        yo = [AR.f32([128, 1024]), AR.f32([128, 1024])]
        for g in range(5):
            t0, n = GROUPS[g]
            norm_group(g, None, None, 'svT', yT, lambda c: ('yT', c), X, out_f32_scale=lambda c: svT[:, SV_FG + c:SV_FG + c + 1])
            for tl in range(n // 128):
                t = t0 // 128 + tl
                yb = yo[t % 2]; yr = ('yo', t % 2)
                for hf in range(2):
                    bt = pbank()
                    for jj in range(4):
                        c = hf * 4 + jj
                        tr(pbs[bt][:, jj * 128:(jj + 1) * 128], yT[:, c, tl * 128:(tl + 1) * 128], [('yT', c)], [('pb', bt)])
                    cp('act' if hf == 0 else 'dve', yb[:, hf * 512:(hf + 1) * 512], pbs[bt][:, 0:512], [('pb', bt)], [yr])
                dst = y_p[t * 128:(t + 1) * 128, :] if t < 16 else y_s
                dma('sp', dst, yb[:, :], [yr], [])
        return _finish()


_CACHE = {}


def _consts():
    ROT = 16
    inv = (500000.0 ** (-np.arange(0, ROT, 2, dtype=np.float32) / np.float32(ROT))).astype(np.float32)
    pos = np.zeros((128, 17), np.float32)
    for t in range(16):
        pos[:, t] = t * 128 + np.arange(128)
    pos[:, 16] = 16384 + (np.arange(128) % 8)
    ang = (pos[:, :, None] * inv[None, None, :]).astype(np.float32)
    cos = np.cos(ang).astype(np.float32); sin = np.sin(ang).astype(np.float32)
    NEG = -30000.0
    q = np.arange(128)[:, None]; s = np.arange(256)[None, :]
    rel = 128 + q - s
    maskp = np.where((rel >= 0) & (rel <= 128), 0.0, NEG).astype(np.float32)
    qi = (np.arange(128) % 8)[:, None]; sq = (np.arange(128) // 8)
    k = np.arange(128)[None, :]
    masks = np.zeros((128, 136), np.float32)
    masks[:, 0:128] = np.where(k >= qi, 0.0, NEG)
    masks[:, 128:136] = np.where(np.arange(8)[None, :] <= qi, 0.0, NEG)
    rowm = np.where(sq[:, None] == np.arange(16)[None, :], 0.0, NEG).astype(np.float32)
    ks = (np.arange(128) // 8)[None, :]; kt = (np.arange(128) % 8)[None, :]
    masksn = np.where((ks == sq[:, None]) & (kt <= qi), 0.0, NEG).astype(np.float32)
    return dict(ident=np.eye(128, dtype=np.float32), cos=cos, sin=sin, maskp=maskp, masks=masks, rowm=rowm, masksn=masksn)


def kernel(x_prompt, x_sample, c_prompt, c_sample, state_conv, state_h, cache_k, cache_v,
           ada_w, ada_b, norm_g, rnn_w_in, rnn_conv_w, rnn_conv_b, rnn_gate_w, rnn_gate_b,
           rnn_lambda, rnn_w_out, kv_ada_w, kv_ada_b, kv_norm_g, w_kv, attn_w_q, attn_sinks,
           attn_w_o, ffn_w_in, ffn_w_out, final_g):
    f = lambda a: np.ascontiguousarray(np.asarray(a, dtype=np.float32))
    if 'nc' not in _CACHE:
        _CACHE['nc'] = build_program()
    nc = _CACHE['nc']
    C = _consts()
    sv = np.concatenate([f(ada_b).reshape(96, 128), f(kv_ada_b).reshape(16, 128), f(norm_g).reshape(32, 128),
                         f(kv_norm_g).reshape(8, 128), f(final_g).reshape(8, 128), f(rnn_conv_w).reshape(32, 128),
                         f(rnn_conv_b).reshape(8, 128), f(rnn_gate_b).reshape(16, 128), f(rnn_lambda).reshape(8, 128)], axis=0)
    sinks = f(attn_sinks)[0]
    def pk(w):
        k = w.shape[0] // 128
        return np.ascontiguousarray(w.reshape(k, 128, w.shape[1]).transpose(1, 0, 2).reshape(128, k * w.shape[1]))
    aw = f(ada_w)
    adaP = np.stack([np.stack([pk(aw[l][:, v * 1024:(v + 1) * 1024]) for v in range(6)]) for l in range(2)])
    kw_ = f(kv_ada_w)
    kvadaP = np.stack([pk(kw_[:, v * 1024:(v + 1) * 1024]) for v in range(2)])
    fi = f(ffn_w_in); fo = f(ffn_w_out)
    ffl = []
    for l in range(2):
        parts = []
        for (h0, hc) in FFN_GROUPS:
            parts.append(pk(fi[l][:, h0 * 128:(h0 + hc) * 128]))
            parts.append(pk(fi[l][:, DFF + h0 * 128:DFF + (h0 + hc) * 128]))
            parts.append(pk(fo[l][h0 * 128:(h0 + hc) * 128, :]))
        ffl.append(np.concatenate(parts, axis=1))
    ffnP = np.ascontiguousarray(np.stack(ffl))
    shared = dict(adaP=adaP, rnn_w_in=pk(f(rnn_w_in)[0]), gate_w=np.ascontiguousarray(f(rnn_gate_w)[0].transpose(1, 0, 2).reshape(128, 2048)),
                  rnn_w_out=pk(f(rnn_w_out)[0]), kvadaP=kvadaP, w_kv=pk(f(w_kv)), w_q=pk(f(attn_w_q)[0]), w_o=pk(f(attn_w_o)[0]), ffnP=ffnP,
                  sv=sv, ident=C['ident'], cos=C['cos'], sin=C['sin'], maskp=C['maskp'],
                  masks=C['masks'], rowm=C['rowm'], masksn=C['masksn'],
                  sinkb=np.ascontiguousarray(np.broadcast_to(sinks[None, :], (128, 16))),
                  sinkc=np.ascontiguousarray(np.repeat(sinks, 8)[:, None]))
    xp_ = f(x_prompt); xs_ = f(x_sample); cp_ = f(c_prompt); cs_ = f(c_sample)
    sc_ = f(state_conv); sh_ = f(state_h); ck_ = f(cache_k); cv_ = f(cache_v)
    in_maps = []
    for b in range(8):
        sl = slice(16 * b, 16 * b + 16)
        m = dict(shared)
        m.update(xp=xp_[b], xs=np.ascontiguousarray(xs_[sl].reshape(128, 1024)),
                 cc=np.ascontiguousarray(np.concatenate([cp_[b:b + 1], cs_[sl]], axis=0)),
                 sconv=np.ascontiguousarray(sc_[0, sl].reshape(48, 1024)), shin=np.ascontiguousarray(sh_[0, sl]),
                 ck=np.ascontiguousarray(ck_[sl].reshape(16, 128, 256)), cv=np.ascontiguousarray(cv_[sl].reshape(16, 128, 256)))
        in_maps.append(m)
    res = run_bass_kernel_spmd(nc, in_maps, core_ids=list(range(8)))
    R = res.results
    cat = lambda k: np.stack([np.asarray(r[k], dtype=np.float32) for r in R], axis=0)
    y_prompt = cat('y_p')
    y_sample = cat('y_s').reshape(128, 8, 1024)
    prompt_conv = cat('pconv').reshape(1, 8, 3, 1024)
    prompt_h = cat('ph').reshape(1, 8, 1024)
    prompt_k = cat('pk').reshape(8, 128, 4, 64)
    prompt_v = cat('pv').reshape(8, 128, 4, 64)
    sample_conv = cat('sconv_o').reshape(1, 128, 3, 1024)
    sample_h = cat('sh_o').reshape(1, 128, 1024)
    sample_k = cat('sk').reshape(128, 128, 4, 64)
    sample_v = cat('svo').reshape(128, 128, 4, 64)
    return (y_prompt, y_sample, prompt_conv, prompt_h, prompt_k, prompt_v, sample_conv, sample_h, sample_k, sample_v)
```

```python
import sys
import numpy as np
from contextlib import ExitStack
import concourse.bass as bass
import concourse.mybir as mybir
from concourse.bass_utils import run_bass_kernel_spmd

F32 = mybir.dt.float32
BF16 = mybir.dt.bfloat16
AF = mybir.ActivationFunctionType
ALU = mybir.AluOpType
AX = mybir.AxisListType

ENGS = ['pe', 'act', 'dve', 'pool', 'sp']
LOOKBACK = 3

D = 1024
NCH = 8
TP = 2048
TS = 128
T = TP + TS
DFF = 2816
NHC = 22
EPS = 1e-6
GROUPS = [(0, 512), (512, 512), (1024, 512), (1536, 512), (2048, 128)]
FFN_GROUPS = [(0, 4), (4, 4), (8, 4), (12, 4), (16, 3), (19, 3)]
SV_ADAB = 0
SV_KVB = 96
SV_NG = 112
SV_KVG = 144
SV_FG = 152
SV_CW = 160
SV_CB = 192
SV_GB = 200
SV_LM = 216
SV_ROWS = 224
RING_SLOTS = 13
SLOT = 2048
ARENA_W = 15500
TAIL_W = 4352
DUMP_PATH = None
SCHEDULE = True
STRICT_SAME = True
SAME_LAT = 0.25
PRIO_RANK = True
VERBOSE = False


class Ins:
    __slots__ = ('eng', 'fn', 'deps', 'idx', 'signaled', 'count', 'dma', 'semi', 'dval', 'waits', 'tag', 'cost', 'seq', 'bar', 'start', 'fin', 'nun', 'succ', 'stage')


class Sched:
    def __init__(self, nc, n_dma_sems=56):
        self.nc = nc
        self.n_main = 48
        self.rr_b = 0
        self.streams = {e: [] for e in ENGS}
        self.last_w = {}
        self.readers = {}
        self.n_dma_sems = n_dma_sems
        self.dma_rr = 0
        self.dma_last = [None] * n_dma_sems
        self.dma_cnt = [0] * n_dma_sems
        self.pb_acc = {}
        self.seq = 0
        self.cur_bar = {e: None for e in ENGS}
        self.since_bar = {e: [] for e in ENGS}
        self.n_bar = 0
        self.stage = 'init'

    def _new(self, eng, fn, dma):
        ins = Ins()
        ins.eng = eng; ins.fn = fn; ins.dma = dma; ins.signaled = False; ins.count = 0
        ins.semi = None; ins.dval = 0; ins.waits = None; ins.deps = []
        ins.cost = 0.1; ins.bar = None; ins.start = 0.0; ins.fin = 0.0
        ins.seq = self.seq; self.seq += 1
        ins.stage = self.stage
        ins.tag = 0
        try:
            f = sys._getframe(1)
            while f is not None and f.f_code.co_name in ('_new', 'op', 'mm', 'tr', 'act', 'tt', 'ts', 'stt', 'cp', 'dma', 'wload', 'barrier', 'finish'):
                f = f.f_back
            ins.tag = f.f_lineno if f is not None else 0
        except Exception:
            ins.tag = 0
        return ins

    def op(self, eng, fn, reads=(), writes=(), dma=False, cost=0.1, sem_b=False):
        ins = self._new(eng, fn, dma)
        ins.cost = cost
        deps = {}
        cb = self.cur_bar[eng]
        if cb is not None:
            deps[id(cb)] = (cb, 'BAR')
        for r in reads:
            w = self.last_w.get(r)
            if w is not None:
                deps[id(w)] = (w, 'RAW')
        for r in writes:
            w = self.last_w.get(r)
            if w is not None and id(w) not in deps:
                deps[id(w)] = (w, 'WAW')
            for rd in self.readers.get(r, ()):
                if id(rd) not in deps:
                    deps[id(rd)] = (rd, 'WAR')
        banks = set(r for r in list(reads) + list(writes) if isinstance(r, tuple) and r[0] == 'pb')
        for bnk in banks:
            st = self.pb_acc.get(bnk)
            if st is None:
                st = self.pb_acc[bnk] = {'eng': eng, 'cur': [], 'prev': []}
            if st['eng'] != eng:
                st['prev'] = st['cur']
                st['cur'] = []
                st['eng'] = eng
            for d in st['prev']:
                if id(d) not in deps:
                    deps[id(d)] = (d, 'PB')
            st['cur'].append(ins)
        if dma:
            half = self.n_dma_sems // 2
            if eng == 'pool':
                si = self.rr_b % half
                self.rr_b += 1
            else:
                si = half + (self.dma_rr % half)
                self.dma_rr += 1
            prev = self.dma_last[si]
            if prev is not None and id(prev) not in deps:
                deps[id(prev)] = (prev, 'SEM')
            self.dma_cnt[si] += 1
            ins.semi = si
            ins.dval = 16 * self.dma_cnt[si]
            self.dma_last[si] = ins
        ins.deps = list(deps.values())
        for r in reads:
            lst = self.readers.setdefault(r, [])
            lst.append(ins)
        for r in writes:
            self.last_w[r] = ins
            self.readers[r] = []
        ins.idx = len(self.streams[eng])
        self.streams[eng].append(ins)
        self.since_bar[eng].append(ins)
        return ins

    def barrier(self, engines=('pe', 'act', 'dve', 'sp')):
        pend = [d for d in self.dma_last if d is not None]
        prior = []
        for e in ENGS:
            prior += [x for x in self.since_bar[e] if x.fn is not None]
        self.n_bar += 1
        for e in engines:
            ins = self._new(e, None, False)
            ins.cost = 0.0
            ins.bar = self.n_bar
            deps = {}
            for d in prior:
                deps[id(d)] = (d, 'BAR')
            for d in pend:
                deps[id(d)] = (d, 'RAW')
            cb = self.cur_bar[e]
            if cb is not None:
                deps[id(cb)] = (cb, 'BAR')
            ins.deps = list(deps.values())
            ins.idx = len(self.streams[e])
            self.streams[e].append(ins)
            self.cur_bar[e] = ins
        for e in ENGS:
            if e in engines:
                self.since_bar[e] = []

    def finish(self):
        pend = [d for d in self.dma_last if d is not None]
        ins = self._new('sp', None, False)
        ins.cost = 0.0
        ins.deps = [(d, 'RAW') for d in pend] + [(d, 'BAR') for d in self.streams['sp'] if d is not ins]
        ins.idx = len(self.streams['sp'])
        self.streams['sp'].append(ins)

    def schedule(self):
        import heapq
        allins = []
        for e in ENGS:
            allins += self.streams[e]
        for ins in allins:
            ins.succ = []
            ins.nun = 0
        for ins in allins:
            seen = set()
            for d, kind in ins.deps:
                if id(d) in seen:
                    continue
                seen.add(id(d))
                d.succ.append(ins)
                ins.nun += 1
        prio = {}
        if PRIO_RANK:
            order_seq = sorted(allins, key=lambda x: -x.seq)
            rank = {}
            for ins in order_seq:
                r = 0.0
                for sc in ins.succ:
                    rs_ = rank[id(sc)]
                    if rs_ > r:
                        r = rs_
                rank[id(ins)] = r + ins.cost + (0.25 if not ins.dma else 2.0)
            for ins in allins:
                prio[id(ins)] = -rank[id(ins)]
        else:
            for ins in allins:
                prio[id(ins)] = ins.seq
        avail = {e: [] for e in ENGS}
        ready_t = {}
        for ins in allins:
            if ins.nun == 0:
                heapq.heappush(avail[ins.eng], (prio[id(ins)], id(ins), ins))
                ready_t[id(ins)] = 0.0
        free = {e: 0.0 for e in ENGS}
        dma_pipe = [0.0]
        order = {e: [] for e in ENGS}
        remaining = len(allins)
        WIN = 48
        while remaining:
            best = None
            for e in ENGS:
                h = avail[e]
                if not h:
                    continue
                cands = heapq.nsmallest(WIN, h)
                pick = None
                for c in cands:
                    if ready_t[id(c[2])] <= free[e] + 1e-9:
                        pick = c
                        break
                if pick is None:
                    pick = min(cands, key=lambda c: (ready_t[id(c[2])], c[0]))
                st = max(free[e], ready_t[id(pick[2])])
                if best is None or st < best[0] or (st == best[0] and pick[0] < best[1][0]):
                    best = (st, pick, e)
            st, pick, e = best
            ins = pick[2]
            avail[e].remove(pick)
            heapq.heapify(avail[e])
            ins.start = st
            if ins.dma:
                free[e] = st + 0.06
                t0 = max(st + 1.8, dma_pipe[0])
                ins.fin = t0 + ins.cost
                dma_pipe[0] = ins.fin
            else:
                ins.fin = st + ins.cost
                free[e] = ins.fin
            order[e].append(ins)
            remaining -= 1
            for sc in ins.succ:
                sc.nun -= 1
                if sc.eng == ins.eng and not ins.dma:
                    lat = SAME_LAT if ins.eng in ('act', 'dve') else 0.0
                else:
                    lat = 0.42
                rt = max(ready_t.get(id(sc), 0.0), ins.fin + lat)
                ready_t[id(sc)] = rt
                if sc.nun == 0:
                    heapq.heappush(avail[sc.eng], (prio[id(sc)], id(sc), sc))
        for e in ENGS:
            self.streams[e] = order[e]
            for i, ins in enumerate(order[e]):
                ins.idx = i
        self.sim_time = max(free.values())
        print("[sched] simulated time (us): %.1f" % self.sim_time, {e: len(order[e]) for e in ENGS})
        if VERBOSE:
            st = {}
            for e in ENGS:
                for ins in order[e]:
                    d = st.setdefault(ins.stage, {'t0': 1e18, 't1': 0.0, 'busy': {x: 0.0 for x in ENGS}})
                    d['t0'] = min(d['t0'], ins.start); d['t1'] = max(d['t1'], ins.fin)
                    if not ins.dma:
                        d['busy'][e] += ins.cost
            for k, d in sorted(st.items(), key=lambda kv: kv[1]['t0']):
                print("  %-10s t0=%7.1f t1=%7.1f span=%7.1f  busy pe=%6.1f act=%6.1f dve=%6.1f" % (k, d['t0'], d['t1'], d['t1'] - d['t0'], d['busy']['pe'], d['busy']['act'], d['busy']['dve']))

    def plan(self):
        for eng in ENGS:
            known_eng = {e: -1 for e in ENGS}
            known_dma = {}
            for ins in self.streams[eng]:
                waits = []
                tgt = {}
                for d, kind in ins.deps:
                    if d.dma:
                        if known_dma.get(d.semi, 0) >= d.dval:
                            continue
                        known_dma[d.semi] = d.dval
                        waits.append(d)
                    elif d.fn is None:
                        continue
                    elif d.eng == eng:
                        if eng == 'pe':
                            continue
                        if (kind == 'RAW' and ins.idx - d.idx <= LOOKBACK) or STRICT_SAME:
                            if d.idx > tgt.get(eng, (-1, None))[0]:
                                tgt[eng] = (d.idx, d)
                    else:
                        if d.idx > tgt.get(d.eng, (-1, None))[0]:
                            tgt[d.eng] = (d.idx, d)
                for e2, (ix, d) in tgt.items():
                    if known_eng[e2] >= ix:
                        continue
                    known_eng[e2] = ix
                    waits.append(d)
                for d in waits:
                    if not d.dma:
                        d.signaled = True
                ins.waits = waits
        for eng in ENGS:
            c = 0
            for ins in self.streams[eng]:
                if ins.dma:
                    continue
                if ins.fn is None:
                    ins.count = c
                    continue
                if ins.signaled:
                    c += 1
                    ins.count = c

    def dump(self, path):
        with open(path, 'w') as f:
            for eng in ENGS:
                f.write('==== %s\n' % eng)
                for ins in self.streams[eng]:
                    w = ['%s:%s' % (('dma%d' % d.semi) if d.dma else d.eng, d.dval if d.dma else '%d(c%d,L%d)' % (d.idx, d.count, d.tag)) for d in ins.waits]
                    f.write('%5d L%-4d %s%s sig=%d cnt=%d waits=%s\n' % (ins.idx, ins.tag, 'DMA(s%d,v%d) ' % (ins.semi, ins.dval) if ins.dma else '', 'NOP' if ins.fn is None else '', ins.signaled, ins.count, w))

    def emit(self):
        nc = self.nc
        if SCHEDULE:
            self.schedule()
        self.plan()
        if DUMP_PATH:
            self.dump(DUMP_PATH)
        with ExitStack() as es:
            esem = {e: es.enter_context(nc.semaphore("s_" + e)) for e in ENGS}
            dsem = [es.enter_context(nc.semaphore("d_%d" % i)) for i in range(self.n_dma_sems)]
            block = es.enter_context(nc.Block())

            def run(eng_name):
                def body(e):
                    for ins in self.streams[eng_name]:
                        for d in ins.waits:
                            if d.dma:
                                e.wait_ge(dsem[d.semi], d.dval)
                            else:
                                e.wait_ge(esem[d.eng], d.count)
                        if ins.fn is None:
                            continue
                        bi = ins.fn(e)
                        if ins.dma:
                            bi.then_inc(dsem[ins.semi], 16)
                        elif ins.signaled:
                            bi.then_inc(esem[eng_name], 1)
                return body

            block.tensor(run('pe'))
            block.scalar(run('act'))
            block.vector(run('dve'))
            block.gpsimd(run('pool'))
            block.sync(run('sp'))


class _Stop(Exception):
    pass


def build_program(stop=99, dbg=False):
    try:
        return _build(stop, dbg)
    except _Stop as e:
        return e.args[0]


def _build(stop, dbg):
    nc = bass.Bass("TRN2", target_bir_lowering=False)
    S = Sched(nc)
    din = lambda n, shp: nc.dram_tensor(n, list(shp), F32, kind="ExternalInput").ap()
    dout = lambda n, shp: nc.dram_tensor(n, list(shp), F32, kind="ExternalOutput").ap()
    dint = lambda n, shp: nc.dram_tensor(n, list(shp), F32, kind="Internal").ap()
    xp = din("xp", [TP, D]); xs = din("xs", [TS, D]); cc = din("cc", [17, D])
    sconv = din("sconv", [48, D]); shin = din("shin", [16, D])
    ck = din("ck", [16, 128, 256]); cv = din("cv", [16, 128, 256])
    adaP = din("adaP", [2, 6, 128, 8 * 1024]); rnn_w_in = din("rnn_w_in", [128, 8 * 2048]); gate_w = din("gate_w", [128, 8 * 256])
    rnn_w_out = din("rnn_w_out", [128, 8 * 1024]); kvadaP = din("kvadaP", [2, 128, 8 * 1024]); w_kv = din("w_kv", [128, 8 * 512])
    w_q = din("w_q", [128, 8 * 1024]); w_o = din("w_o", [128, 8 * 1024]); ffnP = din("ffnP", [2, 128, NHC * 3072])
    svd = din("sv", [SV_ROWS, 128]); identd = din("ident", [128, 128])
    cosd = din("cos", [128, 17, 8]); sind = din("sin", [128, 17, 8])
    maskpd = din("maskp", [128, 256]); masksd = din("masks", [128, 136])
    sinkbd = din("sinkb", [128, 16]); sinkcd = din("sinkc", [128, 1])
    y_p = dout("y_p", [TP, D]); y_s = dout("y_s", [TS, D]); pconv = dout("pconv", [3, D]); ph = dout("ph", [1, D])
    pk = dout("pk", [128, 256]); pv = dout("pv", [128, 256]); sconv_o = dout("sconv_o", [48, D]); sh_o = dout("sh_o", [16, D])
    sk = dout("sk", [16, 128, 256]); svo = dout("svo", [16, 128, 256])
    scr_v = dint("scr_v", [128, 256]); scr_o = dint("scr_o", [128, 1024])

    with ExitStack() as es:
        sb = lambda n, shp, dt=F32: es.enter_context(nc.sbuf_tensor(n, list(shp), dt))
        xT = sb("xT", [128, NCH, T])
        ring = sb("ring", [128, RING_SLOTS * SLOT], BF16)
        svT = sb("svT", [128, SV_ROWS]); ident = sb("ident_sb", [128, 128])
        ones = sb("ones", [128, 128], BF16)
        modT = sb("modT", [128, 48, 17]); kvmodT = sb("kvmodT", [128, 16, 17])
        cTc = sb("cTc", [128, NCH, 17], BF16)
        cost = sb("cost", [128, 17, 8]); sint = sb("sint", [128, 17, 8])
        maskp = sb("maskp_sb", [128, 256]); masks = sb("masks_sb", [128, 136])
        sinkb = sb("sinkb_sb", [128, 16]); sinkc = sb("sinkc_sb", [128, 1])
        rgc = sb("rgc", [128, 40])
        hstate = sb("hstate", [128, 8])
        arena = sb("arena", [128, ARENA_W + TAIL_W])
        pbs = [es.enter_context(nc.psum_tensor("pb%d" % i, [128, 512], F32)) for i in range(8)]

        class Arena:
            def __init__(self):
                self.off = 0
                self.limit = ARENA_W + TAIL_W
            def reset(self):
                self.off = 0
            def f32(self, shape):
                n = int(np.prod(shape[1:]))
                ap = arena[0:shape[0], self.off:self.off + n]
                self.off += n
                assert self.off <= self.limit, (self.off, self.limit)
                return _view(ap, shape)
            def bf16(self, shape):
                n = int(np.prod(shape[1:]))
                nw = (n + 1) // 2
                ap = arena[0:shape[0], self.off:self.off + nw].bitcast(BF16)[:, 0:n]
                self.off += nw
                assert self.off <= self.limit, (self.off, self.limit)
                return _view(ap, shape)

        def _view(ap, shape):
            if len(shape) == 2:
                return ap
            if len(shape) == 3:
                return ap.rearrange("p (a b) -> p a b", b=shape[2])
            if len(shape) == 4:
                return ap.rearrange("p (a b c) -> p a b c", b=shape[2], c=shape[3])
            raise ValueError

        AR = Arena()
        KT = arena[:, ARENA_W:ARENA_W + 2176].bitcast(BF16).rearrange("p (a b) -> p a b", b=T)
        Vb = arena[:, ARENA_W + 2176:ARENA_W + 4352].bitcast(BF16).rearrange("p (a b) -> p a b", b=256)

        def fsz(ap):
            n = 1
            for d in ap.shape[1:]:
                n *= int(d)
            return n

        def c_act(out):
            return 0.22 + fsz(out) / 1200.0

        def c_dve(out):
            return 0.08 + fsz(out) / 960.0

        def mm(out, lhsT, rhs, start, stop, r, w, **kw):
            f = 4.0 if rhs.dtype == F32 else 1.0
            cost = f * max(fsz(rhs), 64) / 2400.0 + 0.015
            S.op('pe', lambda e: e.matmul(out, lhsT, rhs, start=start, stop=stop, **kw), reads=r, writes=w, cost=cost)

        def tr(out, in_, r, w):
            n = in_.shape[0]
            cost = 0.12
            S.op('pe', lambda e: e.transpose(out, in_, ident[0:n, 0:n]), reads=list(r) + ['ident'], writes=w, cost=cost)

        def act(out, in_, func, r, w, bias=None, scale=None, accum=None):
            kw = {}
            if bias is not None: kw['bias'] = bias
            if scale is not None: kw['scale'] = scale
            if accum is not None: kw['accum_out'] = accum
            S.op('act', lambda e: e.activation(out, in_, func, **kw), reads=r, writes=w, cost=c_act(out))

        def tt(out, in0, in1, op, r, w):
            S.op('dve', lambda e: e.tensor_tensor(out, in0, in1, op), reads=r, writes=w, cost=c_dve(out))

        def ts(out, in0, s1, s2, op0, op1, r, w):
            if s2 is None:
                S.op('dve', lambda e: e.tensor_scalar(out, in0, s1, None, op0), reads=r, writes=w, cost=c_dve(out))
            else:
                S.op('dve', lambda e: e.tensor_scalar(out, in0, s1, s2, op0, op1), reads=r, writes=w, cost=c_dve(out))

        def stt(out, in0, scalar, in1, op0, op1, r, w):
            S.op('dve', lambda e: e.scalar_tensor_tensor(out, in0, scalar, in1, op0, op1), reads=r, writes=w, cost=1.2 * c_dve(out))

        def cp(eng, out, in_, r, w):
            if eng == 'act':
                S.op('act', lambda e: e.copy(out, in_), reads=r, writes=w, cost=c_act(out))
            else:
                S.op('dve', lambda e: e.tensor_copy(out, in_), reads=r, writes=w, cost=c_dve(out))

        def dma(q, out, in_, r, w, sem_b=False):
            nbytes = 4.0 * max(fsz(out), fsz(in_)) * min(int(out.shape[0]), 128)
            S.op(q, lambda e: e.dma_start(out=out, in_=in_), reads=r, writes=w, dma=True, cost=nbytes / 300e3, sem_b=sem_b)

        pb_rr = [0]
        pb_pool = [(0, 1, 2, 3, 4, 5, 6, 7)]
        def pbank():
            pool = pb_pool[0]
            i = pool[pb_rr[0] % len(pool)]
            pb_rr[0] += 1
            return i

        ring_pos = [0]
        def wload(dram_ap, shape, slot0=None):
            n = int(np.prod(shape[1:]))
            ns = (n + SLOT - 1) // SLOT
            if slot0 is not None:
                ring_pos[0] = slot0
            if ring_pos[0] + ns > RING_SLOTS:
                ring_pos[0] = 0
            s0 = ring_pos[0]
            ring_pos[0] += ns
            view = _view(ring[:, s0 * SLOT: s0 * SLOT + n], shape)
            res = [('ws', i) for i in range(s0, s0 + ns)]
            dma('pool', view, dram_ap, [], res)
            return view, res

        def ckpt(x):
            if stop <= x:
                raise _Stop(_finish())

        def _finish():
            if dbg:
                S.barrier()
                dbg_x = nc.dram_tensor("dbg_x", [128, NCH * T], F32, kind="ExternalOutput").ap()
                dma('sp', dbg_x, xT[:, :, :].rearrange("p c t -> p (c t)"), [], [])
                dbg_m = nc.dram_tensor("dbg_m", [128, 48 * 17], F32, kind="ExternalOutput").ap()
                dma('sp', dbg_m, modT[:, :, :].rearrange("p c t -> p (c t)"), [], [])
            S.finish()
            S.emit()
            return nc

        dma('sp', ident[:], identd, [], ['ident'])
        dma('sp', cost[:], cosd, [], ['rope']); dma('sp', sint[:], sind, [], ['rope'])
        dma('sp', maskp[:], maskpd, [], ['mask']); dma('sp', masks[:], masksd, [], ['mask'])
        dma('sp', sinkb[:], sinkbd, [], ['sink']); dma('sp', sinkc[:], sinkcd, [], ['sink'])
        S.op('dve', lambda e: e.memset(ones[:], 1.0), writes=['ones'])
        S.op('dve', lambda e: e.memset(hstate[:], 0.0), writes=['hstate'])
        AR.reset()
        sva = AR.f32([112, 128]); svb = AR.f32([112, 128]); cin = AR.f32([17, 1024]); csl = AR.f32([17, 1024])
        xin = [AR.f32([128, 1024]), AR.f32([128, 1024])]
        dma('sp', sva, svd[0:112, :], [], ['sva']); dma('sp', svb, svd[112:224, :], [], ['svb'])
        dma('sp', cin, cc, [], ['cin'])
        b0 = pbank()
        tr(pbs[b0][:, 0:112], sva, ['sva'], [('pb', b0)])
        tr(pbs[b0][:, 112:224], svb, ['svb'], [('pb', b0)])
        cp('dve', svT[:, :], pbs[b0][:, 0:224], [('pb', b0)], ['svT'])
        act(csl, cin, AF.Silu, ['cin'], ['csl'])
        b0 = pbank()
        for k in range(NCH):
            tr(pbs[b0][:, k * 17:(k + 1) * 17], csl[:, k * 128:(k + 1) * 128], ['csl'], [('pb', b0)])
        cp('dve', cTc[:, :, :], pbs[b0][:, 0:136].rearrange("p (k n) -> p k n", n=17), [('pb', b0)], ['cTc'])
        for t in range(17):
            xi = xin[t % 2]
            src = xp[t * 128:(t + 1) * 128, :] if t < 16 else xs
            dma('sp', xi, src, [], [('xin', t % 2)])
            for hf in range(2):
                b0 = pbank()
                for j in range(4):
                    c = hf * 4 + j
                    tr(pbs[b0][:, j * 128:(j + 1) * 128], xi[:, c * 128:(c + 1) * 128], [('xin', t % 2)], [('pb', b0)])
                cp('act' if hf == 0 else 'dve', xT[:, hf * 4:hf * 4 + 4, t * 128:(t + 1) * 128],
                   pbs[b0][:, :].rearrange("p (j n) -> p j n", n=128), [('pb', b0)],
                   [('xT', c, min(t // 4, 4)) for c in range(hf * 4, hf * 4 + 4)])
        act(rgc[:, 32:40], svT[:, SV_LM:SV_LM + 8], AF.Exp, ['svT'], ['rgc_t'], scale=-1.0)
        act(rgc[:, 32:40], rgc[:, 32:40], AF.Ln, ['rgc_t'], ['rgc_t'], bias=1.0)
        act(rgc[:, 0:8], rgc[:, 32:40], AF.Identity, ['rgc_t'], ['rgc'], scale=-8.0)
        act(rgc[:, 8:16], rgc[:, 32:40], AF.Identity, ['rgc_t'], ['rgc'], scale=-16.0)
        act(rgc[:, 16:32], svT[:, SV_GB:SV_GB + 16], AF.Identity, ['svT'], ['rgc'], scale=-1.0)

        def compute_mod(w_dram, vlist, dst, bias_row0, resfn, slot0=None):
            for v in vlist:
                bk = pbank()
                wv, wres = wload(w_dram[v].rearrange("p (k n) -> p k n", k=NCH), [128, NCH, 1024], slot0=slot0)
                for c in range(NCH):
                    o = pbs[bk][:, c * 17:(c + 1) * 17]
                    for k in range(NCH):
                        mm(o, wv[:, k, c * 128:(c + 1) * 128], cTc[:, k, :], k == 0, k == NCH - 1, wres + ['cTc'], [('pb', bk)])
                tt(dst[:, v * 8:(v + 1) * 8, :], pbs[bk][:, 0:136].rearrange("p (a b) -> p a b", b=17),
                   svT[:, bias_row0 + v * 8: bias_row0 + v * 8 + 8].unsqueeze(2).to_broadcast([128, 8, 17]), ALU.add,
                   [('pb', bk), 'svT'], [resfn(v)])

        def fold_scale(dst, sc0, g_row0, res):
            stt(dst[:, sc0:sc0 + 8, :], dst[:, sc0:sc0 + 8, :], 1.0,
                svT[:, g_row0:g_row0 + 8].unsqueeze(2).to_broadcast([128, 8, 17]), ALU.add, ALU.mult, [res, 'svT'], [res])

        def compute_mod_tail(w_dram, nvec, dst, bias_row0, res, tbs, tmp, tmpres):
            for v in range(nvec):
                bk = pbank()
                for half in range(2):
                    tb = tbs[half]; tres = ('tailb', half)
                    dma('pool', tb, w_dram[v][:, half * 4096:(half + 1) * 4096].rearrange("p (k n) -> p k n", k=4), [('hTf', 0, 0)], [tres])
                    for c in range(NCH):
                        o = pbs[bk][:, half * 136 + c * 17:half * 136 + (c + 1) * 17]
                        for k in range(4):
                            mm(o, tb[:, k, c * 128:(c + 1) * 128], cTc[:, half * 4 + k, :], k == 0, k == 3, [tres, 'cTc'], [('pb', bk)])
                tt(tmp, pbs[bk][:, 0:136].rearrange("p (a b) -> p a b", b=17),
                   svT[:, bias_row0 + v * 8: bias_row0 + v * 8 + 8].unsqueeze(2).to_broadcast([128, 8, 17]), ALU.add, [('pb', bk), 'svT'], [tmpres])
                tt(dst[:, v * 8:(v + 1) * 8, :], tmp, pbs[bk][:, 136:272].rearrange("p (a b) -> p a b", b=17), ALU.add, [('pb', bk), tmpres], [res])

        def bc_s(v):
            return v.unsqueeze(2).to_broadcast([128, 16, 8])

        def v3(ap):
            return ap.rearrange("p (s t) -> p s t", t=8)

        def norm_group(g, A, B, modres, hdst, hres, X, out_f32_scale=None, bank=None):
            t0, n = GROUPS[g]
            mr = list(modres) if isinstance(modres, list) else [modres]
            bk = pbank() if bank is None else bank
            for c in range(NCH):
                q = X[c % 2][:, :].bitcast(BF16); qr = 'X%d' % (c % 2)
                act(q[:, 0:n], xT[:, c, t0:t0 + n], AF.Square, [('xT', c, g)], [qr])
                mm(pbs[bk][:, 0:n], ones[:, :], q[:, 0:n], c == 0, c == NCH - 1, ['ones', qr], [('pb', bk)])
            rstd = X[2]
            act(rstd[:, 0:n], pbs[bk][:, 0:n], AF.Ln, [('pb', bk)], ['X2'], scale=1.0 / D, bias=EPS)
            act(rstd[:, 0:n], rstd[:, 0:n], AF.Exp, ['X2'], ['X2'], scale=-0.5)
            for c in range(NCH):
                tb = X[c % 2]; tr_ = 'X%d' % (c % 2)
                tt(tb[:, 0:n], xT[:, c, t0:t0 + n], rstd[:, 0:n], ALU.mult, [('xT', c, g), 'X2'], [tr_])
                if out_f32_scale is not None:
                    act(hdst[:, c, 0:n], tb[:, 0:n], AF.Identity, [tr_, 'svT'], [hres(c)], scale=out_f32_scale(c))
                elif g < 4:
                    act(hdst[:, c, 0:n], tb[:, 0:n], AF.Identity, [tr_] + mr, [hres(c)],
                        scale=A[:, c, 0:1], bias=B[:, c, 0:1])
                else:
                    tt(v3(tb[:, 0:n]), v3(tb[:, 0:n]), bc_s(A[:, c, 1:17]), ALU.mult, [tr_] + mr, [tr_])
                    tt(v3(hdst[:, c, 0:n]), v3(tb[:, 0:n]), bc_s(B[:, c, 1:17]), ALU.add, [tr_] + mr, [hres(c)])

        def residual(g, fc, bk, Gm, modres, tmp, tmpres):
            t0, n = GROUPS[g]
            mr = list(modres) if isinstance(modres, list) else [modres]
            if g < 4:
                stt(xT[:, fc, t0:t0 + n], pbs[bk][:, 0:n], Gm[:, fc, 0:1], xT[:, fc, t0:t0 + n], ALU.mult, ALU.add,
                    [('pb', bk), ('xT', fc, g)] + mr, [('xT', fc, g)])
            else:
                tt(v3(tmp[:, 0:n]), v3(pbs[bk][:, 0:n]), bc_s(Gm[:, fc, 1:17]), ALU.mult, [('pb', bk)] + mr, [tmpres])
                tt(xT[:, fc, t0:t0 + n], xT[:, fc, t0:t0 + n], tmp[:, 0:n], ALU.add, [tmpres, ('xT', fc, g)], [('xT', fc, g)])

        def ffn_stage(l):
            S.barrier()
            AR.reset()
            hT = AR.bf16([128, NCH, T])
            hid = [AR.bf16([128, 4, 512]), AR.bf16([128, 4, 512])]
            X = [AR.f32([128, 512]), AR.f32([128, 512]), AR.f32([128, 512])]
            A = modT[:, 32:40, :]; B = modT[:, 24:32, :]; Gm = modT[:, 40:48, :]
            for g in range(5):
                t0, n = GROUPS[g]
                norm_group(g, A, B, [('mod', 3), ('mod', 4)], hT[:, :, t0:t0 + n], (lambda g: lambda c: ('hTf', c, g))(g), X)
            if l == 0:
                tbs = [AR.bf16([128, 4, 1024]), AR.bf16([128, 4, 1024])]
                modT1 = AR.f32([128, 48, 17]); mtmp = AR.f32([128, 8, 17])
                compute_mod_tail(kvadaP, 2, kvmodT, SV_KVB, 'kvmod', tbs, mtmp, 'mtmp')
                fold_scale(kvmodT, 8, SV_KVG, 'kvmod')
                compute_mod_tail(adaP[1], 6, modT1, SV_ADAB + 48, 'modT1', tbs, mtmp, 'mtmp')
                fold_scale(modT1, 8, SV_NG + 16, 'modT1')
                fold_scale(modT1, 32, SV_NG + 24, 'modT1')
            it = 0
            for (h0, hc) in FFN_GROUPS:
                o0 = h0 * 3072
                wg, rg_ = wload(ffnP[l][:, o0:o0 + hc * 1024].rearrange("p (k n) -> p k n", k=NCH), [128, NCH, hc * 128])
                wu, ru_ = wload(ffnP[l][:, o0 + hc * 1024:o0 + hc * 2048].rearrange("p (k n) -> p k n", k=NCH), [128, NCH, hc * 128])
                wo, ro_ = wload(ffnP[l][:, o0 + hc * 2048:o0 + hc * 3072].rearrange("p (j n) -> p j n", j=hc), [128, hc, 1024])
                for g in range(5):
                    t0, n = GROUPS[g]
                    hb = hid[it % 2]; hres = ('hid', it % 2); it += 1
                    for j in range(hc):
                        bg = pbank(); bu = pbank()
                        for k in range(NCH):
                            mm(pbs[bg][:, 0:n], wg[:, k, j * 128:(j + 1) * 128], hT[:, k, t0:t0 + n], k == 0, k == NCH - 1,
                               rg_ + [('hTf', k, g)], [('pb', bg)])
                        for k in range(NCH):
                            mm(pbs[bu][:, 0:n], wu[:, k, j * 128:(j + 1) * 128], hT[:, k, t0:t0 + n], k == 0, k == NCH - 1,
                               ru_ + [('hTf', k, g)], [('pb', bu)])
                        s_ = X[j % 2]; sr = 'X%d' % (j % 2)
                        act(s_[:, 0:n], pbs[bg][:, 0:n], AF.Silu, [('pb', bg)], [sr])
                        tt(hb[:, j, 0:n], s_[:, 0:n], pbs[bu][:, 0:n], ALU.mult, [sr, ('pb', bu)], [hres])
                    for fc in range(NCH):
                        bo = pbank()
                        for j in range(hc):
                            mm(pbs[bo][:, 0:n], wo[:, j, fc * 128:(fc + 1) * 128], hb[:, j, 0:n], j == 0, j == hc - 1,
                               ro_ + [hres], [('pb', bo)])
                        residual(g, fc, bo, Gm, [('mod', 5)], X[2], 'X2')
            if l == 0:
                cp('dve', modT[:, :, :], modT1, ['modT1'], [('mod', v) for v in range(6)])

        if stop <= 0:
            return _finish()
        S.stage = 'mod0'
        mres = lambda v: ('mod', v)
        compute_mod(adaP[0], [0, 1, 2], modT, SV_ADAB, mres)
        fold_scale(modT, 8, SV_NG + 0, ('mod', 1))
        ckpt(0.2)
        S.barrier()
        AR.reset()
        hT2 = [AR.bf16([128, NCH, 512]), AR.bf16([128, NCH, 512])]
        xpad = AR.f32([128, NCH, 515])
        xpads = xpad[:, :, 256:432].rearrange("p c (s j) -> p c s j", j=11)
        oT = AR.bf16([128, NCH, 512])
        offX = AR.off
        X = [AR.f32([128, 512]), AR.f32([128, 512]), AR.f32([128, 512])]
        stio2 = arena[0:48, offX:offX + 1024]
        xc2 = [AR.f32([128, 512]), AR.f32([128, 512])]; xcb2 = [AR.bf16([128, 512]), AR.bf16([128, 512])]
        ta2 = [AR.f32([128, 512]), AR.f32([128, 512])]; tb2 = [AR.f32([128, 512]), AR.f32([128, 512])]
        ti2 = [AR.f32([128, 512]), AR.f32([128, 512])]; gq2 = [AR.f32([128, 512]), AR.f32([128, 512])]
        blk2 = [AR.f32([128, 1024]), AR.f32([128, 1024])]
        stio = blk2[0][0:48, :]; ti = ti2[0]
        h0T = AR.f32([128, NCH, 16]); tcc = AR.f32([128, 48])
        S.op('dve', lambda e: e.memset(xpad[:, :, 0:3], 0.0), writes=[('xpad', c) for c in range(NCH)])
        dma('sp', stio[0:16, :], shin, [], ['stio'])
        b0 = pbank()
        for c in range(NCH):
            tr(pbs[b0][:, c * 16:(c + 1) * 16], stio[0:16, c * 128:(c + 1) * 128], ['stio'], [('pb', b0)])
        cp('dve', h0T[:, :, :], pbs[b0][:, 0:128].rearrange("p (c s) -> p c s", s=16), [('pb', b0)], ['h0T'])
        S.barrier()

        ckpt(0.3)
        S.stage = 'rglru'
        win, rwin = wload(rnn_w_in.rearrange("p (k n) -> p k n", k=NCH), [128, NCH, 2048], slot0=0)
        wgt, rwgt = wload(gate_w.rearrange("p (k n) -> p k n", k=8), [128, 8, 256], slot0=8)
        compute_mod(adaP[0], [3, 4, 5], modT, SV_ADAB, mres, slot0=9)
        fold_scale(modT, 32, SV_NG + 8, ('mod', 4))
        wout, rwout = wload(rnn_w_out.rearrange("p (k n) -> p k n", k=NCH), [128, NCH, 1024], slot0=9)
        A1 = modT[:, 8:16, :]; B1 = modT[:, 0:8, :]; G1 = modT[:, 16:24, :]
        for g in range(5):
            t0, n = GROUPS[g]
            hT = hT2[g % 2]
            pb_pool[0] = (0, 1, 2, 3, 4, 5, 6)
            norm_group(g, A1, B1, [('mod', 0), ('mod', 1)], hT, (lambda gp: lambda c: ('hT', gp, c))(g % 2), X, bank=7)
            ckpt(0.4 + g * 0.1)
            if g == 4:
                dma('sp', stio2, sconv, [], ['X0', 'X1'])
                b0 = pbank()
                for c in range(NCH):
                    tr(pbs[b0][:, c * 48:(c + 1) * 48], stio2[:, c * 128:(c + 1) * 128], ['X0', 'X1'], [('pb', b0)])
                cp('dve', xpads[:, :, :, 0:3], pbs[b0][:, 0:384].rearrange("p (c s j) -> p c s j", s=16, j=3), [('pb', b0)],
                   [('xpad', c) for c in range(NCH)])
            for c in range(NCH):
                ckpt(0.4 + g * 0.1 + 0.01 * (c + 1))
                pz = c % 2
                xc = xc2[pz]; xcb = xcb2[pz]; ta = ta2[pz]; tb_ = tb2[pz]; ti = ti2[pz]; gq = gq2[pz]
                uu = blk2[pz][:, 0:512]; hh = blk2[pz][:, 512:1024]
                Rxc = 'xc%d' % pz; Rxcb = 'xcb%d' % pz; Rta = 'ta%d' % pz; Rtb = 'tb%d' % pz; Rti = 'ti%d' % pz
                Rgq = 'gq%d' % pz; Ruu = 'uu%d' % pz; Rhh = 'hh%d' % pz
                bx = pbank()
                for k in range(NCH):
                    mm(pbs[bx][:, 0:n], win[:, k, c * 128:(c + 1) * 128], hT[:, k, 0:n], k == 0, k == NCH - 1, rwin + [('hT', g % 2, k)], [('pb', bx)])
                w_ = (lambda c: lambda kk: svT[:, SV_CW + kk * 8 + c: SV_CW + kk * 8 + c + 1])(c)
                cb_ = svT[:, SV_CB + c:SV_CB + c + 1]
                if g < 4:
                    cp('dve', xpad[:, c, 3:3 + n], pbs[bx][:, 0:n], [('pb', bx)], [('xpad', c)])
                    act(xc[:, 0:n], pbs[bx][:, 0:n], AF.Identity, [('pb', bx), 'svT'], [Rxc], scale=w_(3), bias=cb_)
                    for kk in range(3):
                        stt(xc[:, 0:n], xpad[:, c, kk:kk + n], w_(kk), xc[:, 0:n], ALU.mult, ALU.add, [('xpad', c), Rxc, 'svT'], [Rxc])
                    cp('dve', tcc[:, 0:3], xpad[:, c, n:n + 3], [('xpad', c)], ['tcc'])
                    cp('dve', xpad[:, c, 0:3], tcc[:, 0:3], ['tcc'], [('xpad', c)])
                else:
                    cp('dve', xpads[:, c, :, 3:11], v3(pbs[bx][:, 0:n]), [('pb', bx)], [('xpad', c)])
                    act(xc[:, 0:n], pbs[bx][:, 0:n], AF.Identity, [('pb', bx), 'svT'], [Rxc], scale=w_(3), bias=cb_)
                    for kk in range(3):
                        stt(v3(xc[:, 0:n]), xpads[:, c, :, kk:kk + 8], w_(kk), v3(xc[:, 0:n]), ALU.mult, ALU.add, [('xpad', c), Rxc, 'svT'], [Rxc])
                cp('dve', xcb[:, 0:n], xc[:, 0:n], [Rxc], [Rxcb])
                ckpt(0.411)
                br = pbank(); bi = pbank()
                mm(pbs[br][:, 0:n], wgt[:, c, 0:128], xcb[:, 0:n], True, True, rwgt + [Rxcb], [('pb', br)])
                mm(pbs[bi][:, 0:n], wgt[:, c, 128:256], xcb[:, 0:n], True, True, rwgt + [Rxcb], [('pb', bi)])
                ckpt(0.412)
                act(ta[:, 0:n], pbs[br][:, 0:n], AF.Exp, [('pb', br), 'rgc'], [Rta], scale=-1.0, bias=rgc[:, 16 + 2 * c:17 + 2 * c])
                act(ta[:, 0:n], ta[:, 0:n], AF.Ln, [Rta], [Rta], bias=1.0)
                act(ta[:, 0:n], ta[:, 0:n], AF.Exp, [Rta], [Rta], scale=-1.0)
                act(tb_[:, 0:n], ta[:, 0:n], AF.Exp, [Rta, 'rgc'], [Rtb], scale=rgc[:, 8 + c:9 + c])
                act(ta[:, 0:n], ta[:, 0:n], AF.Exp, [Rta, 'rgc'], [Rta], scale=rgc[:, c:c + 1])
                act(tb_[:, 0:n], tb_[:, 0:n], AF.Ln, [Rtb], [Rtb], scale=-1.0, bias=1.0)
                act(tb_[:, 0:n], tb_[:, 0:n], AF.Exp, [Rtb], [Rtb], scale=0.5)
                ckpt(0.413)
                act(ti[:, 0:n], pbs[bi][:, 0:n], AF.Exp, [('pb', bi), 'rgc'], [Rti], scale=-1.0, bias=rgc[:, 17 + 2 * c:18 + 2 * c])
                act(ti[:, 0:n], ti[:, 0:n], AF.Ln, [Rti], [Rti], bias=1.0)
                act(ti[:, 0:n], ti[:, 0:n], AF.Exp, [Rti], [Rti], scale=-1.0)
                tt(uu[:, 0:n], tb_[:, 0:n], ti[:, 0:n], ALU.mult, [Rtb, Rti], [Ruu])
                tt(uu[:, 0:n], uu[:, 0:n], xc[:, 0:n], ALU.mult, [Ruu, Rxc], [Ruu])
                ckpt(0.414)
                if g < 4:
                    S.op('dve', (lambda c=c, n=n, hh=hh, ta=ta, uu=uu: lambda e: e.tensor_tensor_scan(hh[:, 0:n], ta[:, 0:n], uu[:, 0:n], hstate[:, c:c + 1], ALU.mult, ALU.add))(),
                         reads=[Rta, Ruu, ('hstate', c)], writes=[Rhh], cost=0.1 + n / 420.0)
                    cp('dve', hstate[:, c:c + 1], hh[:, n - 1:n], [Rhh], [('hstate', c)])
                else:
                    a3 = v3(ta[:, 0:n]); u3 = v3(uu[:, 0:n])
                    tt(tcc[:, 0:16], a3[:, :, 0], h0T[:, c, :], ALU.mult, [Rta, 'h0T'], ['tcc'])
                    tt(u3[:, :, 0], u3[:, :, 0], tcc[:, 0:16], ALU.add, [Ruu, 'tcc'], [Ruu])
                    S.op('dve', (lambda a3=a3: lambda e: e.memset(a3[:, :, 0:1], 0.0))(), reads=['tcc'], writes=[Rta])
                    S.op('dve', (lambda n=n, hh=hh, ta=ta, uu=uu: lambda e: e.tensor_tensor_scan(hh[:, 0:n], ta[:, 0:n], uu[:, 0:n], 0.0, ALU.mult, ALU.add))(),
                         reads=[Rta, Ruu], writes=[Rhh], cost=0.1 + n / 420.0)
                    cp('dve', tcc[:, 0:16], v3(hh[:, 0:n])[:, :, 7], [Rhh], ['tcc'])
                    bt = pbank()
                    tr(pbs[bt][0:16, 0:128], tcc[:, 0:16], ['tcc'], [('pb', bt)])
                    cp('dve', xpad[0:16, c, 128:256], pbs[bt][0:16, 0:128], [('pb', bt)], [('xpad', c)])
                    dma('sp', sh_o[:, c * 128:(c + 1) * 128], xpad[0:16, c, 128:256], [('xpad', c)], [])
                    cp('dve', tcc[:, 0:48].rearrange("p (s j) -> p s j", j=3), xpads[:, c, :, 8:11], [('xpad', c)], ['tcc'])
                    bt = pbank()
                    tr(pbs[bt][0:48, 0:128], tcc[:, 0:48], ['tcc'], [('pb', bt)])
                    cp('dve', xpad[0:48, c, 0:128], pbs[bt][0:48, 0:128], [('pb', bt)], [('xpad', c)])
                ckpt(0.415)
                by = pbank()
                for k in range(NCH):
                    mm(pbs[by][:, 0:n], win[:, k, 1024 + c * 128:1024 + (c + 1) * 128], hT[:, k, 0:n], k == 0, k == NCH - 1, rwin + [('hT', g % 2, k)], [('pb', by)])
                act(gq[:, 0:n], pbs[by][:, 0:n], AF.Square, [('pb', by)], [Rgq])
                ts(gq[:, 0:n], gq[:, 0:n], 0.044715, 1.0, ALU.mult, ALU.add, [Rgq], [Rgq])
                tt(gq[:, 0:n], gq[:, 0:n], pbs[by][:, 0:n], ALU.mult, [Rgq, ('pb', by)], [Rgq])
                act(gq[:, 0:n], gq[:, 0:n], AF.Exp, [Rgq], [Rgq], scale=-1.5957691216057308)
                act(gq[:, 0:n], gq[:, 0:n], AF.Ln, [Rgq], [Rgq], bias=1.0)
                act(gq[:, 0:n], gq[:, 0:n], AF.Exp, [Rgq], [Rgq], scale=-1.0)
                tt(gq[:, 0:n], gq[:, 0:n], pbs[by][:, 0:n], ALU.mult, [Rgq, ('pb', by)], [Rgq])
                tt(oT[:, c, 0:n], hh[:, 0:n], gq[:, 0:n], ALU.mult, [Rhh, Rgq], [('oT', c)])
                ckpt(0.416)
            for fc in range(NCH):
                bo = pbank()
                for c in range(NCH):
                    mm(pbs[bo][:, 0:n], wout[:, c, fc * 128:(fc + 1) * 128], oT[:, c, 0:n], c == 0, c == NCH - 1, rwout + [('oT', c)], [('pb', bo)])
                residual(g, fc, bo, G1, [('mod', 2)], X[2], 'X2')
            if g == 3:
                for hf in range(2):
                    b1 = pbank()
                    for j in range(4):
                        c = hf * 4 + j
                        tr(pbs[b1][0:3, j * 128:(j + 1) * 128], xpad[:, c, 0:3], [('xpad', c)], [('pb', b1)])
                    cp('dve', ti2[0][0:3, 0:512], pbs[b1][0:3, 0:512], [('pb', b1)], ['ti0'])
                    dma('sp', pconv[:, hf * 512:(hf + 1) * 512], ti2[0][0:3, 0:512], ['ti0'], [])
                for hf in range(2):
                    b1 = pbank()
                    for j in range(4):
                        c = hf * 4 + j
                        tr(pbs[b1][0:1, j * 128:(j + 1) * 128], hstate[:, c:c + 1], [('hstate', c)], [('pb', b1)])
                    cp('dve', ti2[0][0:1, 0:512], pbs[b1][0:1, 0:512], [('pb', b1)], ['ti0'])
                    dma('sp', ph[:, hf * 512:(hf + 1) * 512], ti2[0][0:1, 0:512], ['ti0'], [])
        for c in range(NCH):
            dma('sp', sconv_o[:, c * 128:(c + 1) * 128], xpad[0:48, c, 0:128], [('xpad', c)], [])

        if stop <= 1:
            return _finish()
        pb_pool[0] = (0, 1, 2, 3, 4, 5, 6, 7)
        S.stage = 'ffn0'
        ffn_stage(0)

        if stop <= 2:
            return _finish()
        S.stage = 'kv'
        S.barrier()
        AR.reset()
        AR.limit = ARENA_W
        hT = AR.bf16([128, NCH, 512])
        X = [AR.f32([128, 512]), AR.f32([128, 512]), AR.f32([128, 512])]
        Kf = [AR.f32([128, 256]), AR.f32([128, 256])]; Vf = [AR.f32([128, 256]), AR.f32([128, 256])]
        rt_kv = AR.f32([128, 4, 16, 8])

        def rope(buf3, bres, t, H, rt_=None):
            rt = rt_ if rt_ is not None else rt_kv
            cs = cost[:, t, :].unsqueeze(1).to_broadcast([128, H, 8])
            sn = sint[:, t, :].unsqueeze(1).to_broadcast([128, H, 8])
            x1 = buf3[:, :, 0:8]; x2 = buf3[:, :, 8:16]
            t1 = rt[:, 0, 0:H, :]; t2 = rt[:, 1, 0:H, :]; t3 = rt[:, 2, 0:H, :]; t4 = rt[:, 3, 0:H, :]
            tt(t1, x1, cs, ALU.mult, [bres, 'rope'], ['rt1'])
            tt(t2, x2, sn, ALU.mult, [bres, 'rope'], ['rt2'])
            tt(t3, x2, cs, ALU.mult, [bres, 'rope'], ['rt3'])
            tt(t4, x1, sn, ALU.mult, [bres, 'rope'], ['rt4'])
            tt(x1, t1, t2, ALU.subtract, ['rt1', 'rt2'], [bres])
            tt(x2, t3, t4, ALU.add, ['rt3', 'rt4'], [bres])

        wkv, rkv = wload(w_kv.rearrange("p (k n) -> p k n", k=NCH), [128, NCH, 512])
        for g in range(5):
            t0, n = GROUPS[g]
            pb_pool[0] = (0, 1, 2, 3, 4, 5, 6)
            norm_group(g, kvmodT[:, 8:16, :], kvmodT[:, 0:8, :], 'kvmod', hT, lambda c: ('hT', c), X, bank=7)
            for tl in range(n // 128):
                t = t0 // 128 + tl
                bk = pbank()
                for k in range(NCH):
                    mm(pbs[bk][:, 0:512], hT[:, k, tl * 128:(tl + 1) * 128], wkv[:, k, :], k == 0, k == NCH - 1, rkv + [('hT', k)], [('pb', bk)])
                kf = Kf[t % 2]; vf = Vf[t % 2]; kr = ('Kf', t % 2); vr = ('Vf', t % 2)
                cp('act', kf[:, :], pbs[bk][:, 0:256], [('pb', bk)], [kr])
                cp('act', vf[:, :], pbs[bk][:, 256:512], [('pb', bk)], [vr])
                rope(kf.rearrange("p (h d) -> p h d", d=64), kr, t, 4)
                cp('dve', Vb[:, t, :], vf[:, :], [vr], [('Vb', t)])
                for gp in range(2):
                    bt = pbank()
                    tr(pbs[bt][:, 0:128], kf[:, gp * 128:(gp + 1) * 128], [kr], [('pb', bt)])
                    cp('act', KT[:, gp, t * 128:(t + 1) * 128], pbs[bt][:, 0:128], [('pb', bt)], [('KT', t)])
                if t == 15:
                    dma('sp', pk, kf[:, :], [kr], []); dma('sp', pv, vf[:, :], [vr], [])
                if t == 16:
                    for s in range(16):
                        dma('sp', sk[s, 120:128, :], kf[8 * s:8 * s + 8, :], [kr], [])
                        dma('sp', svo[s, 120:128, :], vf[8 * s:8 * s + 8, :], [vr], [])
        dma('sp', sk[:, 0:120, :], ck[:, 8:128, :], [], [])
        dma('sp', svo[:, 0:120, :], cv[:, 8:128, :], [], [])

        if stop <= 3:
            return _finish()
        S.stage = 'attn'
        wq, rwq = wload(w_q.rearrange("p (k n) -> p k n", k=NCH), [128, NCH, 1024])
        wo_, rwo = wload(w_o.rearrange("p (k n) -> p k n", k=NCH), [128, NCH, 1024])
        A1 = modT[:, 8:16, :]; B1 = modT[:, 0:8, :]; G1 = modT[:, 16:24, :]

        def attn_phase(groups, NKMAX, sample, shared=None):
            if sample:
                S.barrier(engines=('pe', 'act', 'dve', 'sp', 'pool'))
                AR.reset()
            NBUF = 2 if sample else 6
            pb_pool[0] = (2, 3, 4, 5, 6, 7)
            ncol = 512 if not sample else 128
            if shared is not None:
                hT, X = shared
            else:
                hT = AR.bf16([128, NCH, ncol])
                X = [AR.f32([128, ncol]), AR.f32([128, ncol]), AR.f32([128, ncol])]
            NT = 1 if sample else 2
            Qf2 = [AR.f32([128, 1024]) for _ in range(NT)]; QT2 = [AR.bf16([128, 8, 128]) for _ in range(NT)]
            Of2 = [AR.f32([128, 1024]) for _ in range(NT)]; OT = AR.bf16([128, NCH, ncol])
            sm2 = [AR.f32([128, 6, 16]) for _ in range(NT)]
            Sm = [AR.f32([128, NKMAX * 128]) for _ in range(NBUF)]
            ET = [AR.bf16([128, NKMAX, 128]) for _ in range(NBUF)]
            if sample:
                KTc = AR.bf16([128, 2, 2048]); Vc = AR.bf16([128, 16, 256])
                ckf = [AR.f32([128, 256]), AR.f32([128, 256])]
                dma('pool', Vc[:, :, :], cv.rearrange("s k d -> k s d"), [], ['Vc'])
                for s in range(16):
                    cb = ckf[s % 2]; cr = ('ckf', s % 2)
                    dma('sp', cb[:, :], ck[s], [], [cr])
                    for gp in range(2):
                        bt = pbank()
                        tr(pbs[bt][:, 0:128], cb[:, gp * 128:(gp + 1) * 128], [cr], [('pb', bt)])
                        cp('act' if gp == 0 else 'dve', KTc[:, gp, s * 128:(s + 1) * 128], pbs[bt][:, 0:128], [('pb', bt)], [('KTc', s)])
            for g in groups:
                t0, n = GROUPS[g]
                norm_group(g, A1, B1, [('mod', 0), ('mod', 1)], hT, lambda c: ('hT', c), X)
                for tl in range(n // 128):
                    t = t0 // 128 + tl
                    tp = t % NT
                    Qf = Qf2[tp]; QT = QT2[tp]; Of = Of2[tp]; sm = sm2[tp]
                    RQf = 'Qf%d' % tp; RQT = 'QT%d' % tp; ROf = 'Of%d' % tp; RS = 'sm%d' % tp
                    for hf in range(2):
                        bq = pbank()
                        for k in range(NCH):
                            mm(pbs[bq][:, 0:512], hT[:, k, tl * 128:(tl + 1) * 128], wq[:, k, hf * 512:(hf + 1) * 512], k == 0, k == NCH - 1,
                               rwq + [('hT', k)], [('pb', bq)])
                        act(Qf.rearrange("p (j h d) -> p j h d", h=2, d=64)[:, hf * 4:(hf + 1) * 4, :, :],
                            pbs[bq][:, 0:512].rearrange("p (h b d) -> p b h d", h=2, b=4), AF.Identity, [('pb', bq)], [RQf], scale=0.125)
                    rope(Qf.rearrange("p (h d) -> p h d", d=64), RQf, t, 16)
                    for hf in range(2):
                        bt = pbank()
                        for jj in range(4):
                            j = hf * 4 + jj
                            tr(pbs[bt][:, jj * 128:(jj + 1) * 128], Qf[:, j * 128:(j + 1) * 128], [RQf], [('pb', bt)])
                        cp('act', QT[:, hf * 4:hf * 4 + 4, :], pbs[bt][:, :].rearrange("p (j n) -> p j n", n=128), [('pb', bt)], [RQT])
                    if not sample:
                        chunks = []
                        if t > 0:
                            chunks.append((lambda gp, t=t: KT[:, gp, (t - 1) * 128:t * 128], ('KT', t - 1), lambda kvh, t=t: Vb[:, t - 1, kvh * 64:(kvh + 1) * 64], ('Vb', t - 1), maskp[:, 0:128], None))
                        chunks.append((lambda gp, t=t: KT[:, gp, t * 128:(t + 1) * 128], ('KT', t), lambda kvh, t=t: Vb[:, t, kvh * 64:(kvh + 1) * 64], ('Vb', t), maskp[:, 128:256], None))
                    else:
                        chunks = []
                        for s in range(16):
                            chunks.append((lambda gp, s=s: KTc[:, gp, s * 128:(s + 1) * 128], ('KTc', s), lambda kvh, s=s: Vc[:, s, kvh * 64:(kvh + 1) * 64], 'Vc', masks[:, 0:128], s))
                        chunks.append((lambda gp: KT[:, gp, 2048:2176], ('KT', 16), lambda kvh: Vb[:, 16, kvh * 64:(kvh + 1) * 64], ('Vb', 16), masksn[:, :], None))
                    nk = len(chunks)
                    bo2 = [0, 1]
                    for h in range(16):
                        kvh = h // 4; gp = kvh // 2; half = kvh % 2
                        j = 4 * (kvh // 2) + h % 4
                        p0 = half * 64
                        Sb = Sm[h % NBUF]; sres = ('Sm', h % NBUF)
                        for ci in range(0, nk, 4):
                            bs = pbank()
                            for cj in range(ci, min(ci + 4, nk)):
                                kfn, kres, vfn, vres, mk, rs_ = chunks[cj]
                                mm(pbs[bs][:, (cj - ci) * 128:(cj - ci + 1) * 128], QT[p0:p0 + 64, j, :], kfn(gp)[p0:p0 + 64, :], True, True,
                                   [RQT, kres], [('pb', bs)])
                            if (not sample) and nk == 2:
                                tt(Sb[:, 0:256], pbs[bs][:, 0:256], maskp[:, 0:256], ALU.add, [('pb', bs), 'mask'], [sres])
                                continue
                            for cj in range(ci, min(ci + 4, nk)):
                                kfn, kres, vfn, vres, mk, rs_ = chunks[cj]
                                if rs_ is None:
                                    tt(Sb[:, cj * 128:(cj + 1) * 128], pbs[bs][:, (cj - ci) * 128:(cj - ci + 1) * 128], mk, ALU.add, [('pb', bs), 'mask'], [sres])
                                else:
                                    stt(Sb[:, cj * 128:(cj + 1) * 128], pbs[bs][:, (cj - ci) * 128:(cj - ci + 1) * 128], rowm[:, rs_:rs_ + 1], mk, ALU.add, ALU.add,
                                        [('pb', bs), 'mask'], [sres])
                        S.op('dve', (lambda Sb=Sb, nk=nk, h=h, sm=sm: lambda e: e.reduce_max(sm[:, 0, h:h + 1], Sb[:, 0:nk * 128], AX.X))(), reads=[sres], writes=[(RS + 'mx', h)], cost=0.08 + nk * 128 / 960.0)
                        ts(sm[:, 1, h:h + 1], sm[:, 0, h:h + 1], sinkb[:, h:h + 1], -1.0, ALU.max, ALU.mult, [(RS + 'mx', h), 'sink'], [(RS + 'nm', h)])
                        act(Sb[:, 0:nk * 128], Sb[:, 0:nk * 128], AF.Exp, [sres, (RS + 'nm', h)], [sres, (RS + 'rs', h)], bias=sm[:, 1, h:h + 1], accum=sm[:, 2, h:h + 1])
                        Eb = ET[h % NBUF]; eres = ('ET', h % NBUF)
                        for ci in range(0, nk, 4):
                            bt = pbank()
                            m_ = min(ci + 4, nk) - ci
                            for cj in range(ci, ci + m_):
                                tr(pbs[bt][:, (cj - ci) * 128:(cj - ci + 1) * 128], Sb[:, cj * 128:(cj + 1) * 128], [sres], [('pb', bt)])
                            cp('act' if (ci // 4) % 2 == 0 else 'dve', Eb[:, ci:ci + m_, :], pbs[bt][:, 0:m_ * 128].rearrange("p (j n) -> p j n", n=128), [('pb', bt)], [eres])
                        bo = bo2[h // 8]
                        for cj in range(nk):
                            kfn, kres, vfn, vres, mk, rs_ = chunks[cj]
                            mm(pbs[bo][:, (h % 8) * 64:(h % 8 + 1) * 64], Eb[:, cj, :], vfn(kvh), cj == 0, cj == nk - 1, [eres, vres], [('pb', bo)])
                        if h % 8 == 7:
                            hf = h // 8; hs = slice(hf * 8, hf * 8 + 8)
                            hl = list(range(hf * 8, hf * 8 + 8))
                            tt(sm[:, 3, hs], sinkb[:, hs], sm[:, 1, hs], ALU.add, ['sink'] + [(RS + 'nm', x) for x in hl], [(RS + 'es', hf)])
                            act(sm[:, 3, hs], sm[:, 3, hs], AF.Exp, [(RS + 'es', hf)], [(RS + 'es', hf)])
                            tt(sm[:, 4, hs], sm[:, 2, hs], sm[:, 3, hs], ALU.add, [(RS + 'es', hf)] + [(RS + 'rs', x) for x in hl], [(RS + 'den', hf)])
                            S.op('dve', (lambda sm=sm, hs=hs: lambda e: e.reciprocal(sm[:, 5, hs], sm[:, 4, hs]))(), reads=[(RS + 'den', hf)], writes=[(RS + 'rden', hf)], cost=0.1)
                            tt(Of[:, hf * 512:(hf + 1) * 512].rearrange("p (h d) -> p h d", d=64), pbs[bo][:, 0:512].rearrange("p (h d) -> p h d", d=64),
                               sm[:, 5, hs].unsqueeze(2).to_broadcast([128, 8, 64]), ALU.mult, [('pb', bo), (RS + 'rden', hf)], [ROf])
                    for hf in range(2):
                        bt = pbank()
                        for jj in range(4):
                            c = hf * 4 + jj
                            tr(pbs[bt][:, jj * 128:(jj + 1) * 128], Of[:, c * 128:(c + 1) * 128], [ROf], [('pb', bt)])
                        cp('act', OT[:, hf * 4:hf * 4 + 4, tl * 128:(tl + 1) * 128], pbs[bt][:, :].rearrange("p (j n) -> p j n", n=128), [('pb', bt)],
                           [('OT', c) for c in range(hf * 4, hf * 4 + 4)])
                for fc in range(NCH):
                    bo = pbank()
                    for c in range(NCH):
                        mm(pbs[bo][:, 0:n], wo_[:, c, fc * 128:(fc + 1) * 128], OT[:, c, 0:n], c == 0, c == NCH - 1, rwo + [('OT', c)], [('pb', bo)])
                    residual(g, fc, bo, G1, [('mod', 2)], X[2], 'X2')

        def attn_sample():
            g = 4
            t0, n = GROUPS[g]
            t = 16
            NB3 = 6
            S.barrier(engines=('pe', 'act', 'dve', 'sp', 'pool'))
            AR.reset()
            pb_pool[0] = (2, 3, 4, 5, 6, 7)
            hT = AR.bf16([128, NCH, 128]); X = [AR.f32([128, 128]) for _ in range(3)]
            Qf = AR.f32([128, 1024]); QTs = AR.bf16([128, 16, 8, 8])
            KTc = AR.bf16([128, 2, 2048]); Vc = AR.bf16([128, 16, 256])
            ckf = [AR.f32([128, 256]), AR.f32([128, 256])]
            Smb = [AR.f32([128, 136]) for _ in range(NB3)]
            Epad = [AR.f32([128, 128]) for _ in range(NB3)]
            ETb = [AR.bf16([128, 2, 128]) for _ in range(NB3)]
            Of = AR.f32([128, 1024]); OT = AR.bf16([128, NCH, 128])
            sm = AR.f32([128, 8, 16])
            rt_s = AR.f32([128, 4, 16, 8])
            for i in range(NB3):
                S.op('dve', (lambda i=i: lambda e: e.memset(Epad[i][:, :], 0.0))(), writes=[('Epad', i)])
            for s in range(16):
                dma('pool', Vc[:, s, :], cv[s], [], [('Vc', s)])
                cb = ckf[s % 2]; cr = ('ckf', s % 2)
                dma('sp', cb[:, :], ck[s], [], [cr])
                for gp in range(2):
                    bt = pbank()
                    tr(pbs[bt][:, 0:128], cb[:, gp * 128:(gp + 1) * 128], [cr], [('pb', bt)])
                    cp('act' if gp == 0 else 'dve', KTc[:, gp, s * 128:(s + 1) * 128], pbs[bt][:, 0:128], [('pb', bt)], [('KTc', s)])
            norm_group(g, A1, B1, [('mod', 0), ('mod', 1)], hT, lambda c: ('hT', c), X)
            for hf in range(2):
                bq = pbank()
                for k in range(NCH):
                    mm(pbs[bq][:, 0:512], hT[:, k, 0:128], wq[:, k, hf * 512:(hf + 1) * 512], k == 0, k == NCH - 1, rwq + [('hT', k)], [('pb', bq)])
                act(Qf.rearrange("p (j h d) -> p j h d", h=2, d=64)[:, hf * 4:(hf + 1) * 4, :, :],
                    pbs[bq][:, 0:512].rearrange("p (h b d) -> p b h d", h=2, b=4), AF.Identity, [('pb', bq)], ['Qf0'], scale=0.125)
            rope(Qf.rearrange("p (h d) -> p h d", d=64), 'Qf0', t, 16, rt_s)
            for hf in range(2):
                bt = pbank()
                for jj in range(4):
                    j = hf * 4 + jj
                    tr(pbs[bt][:, jj * 128:(jj + 1) * 128], Qf[:, j * 128:(j + 1) * 128], ['Qf0'], [('pb', bt)])
                cp('act', QTs[:, :, hf * 4:hf * 4 + 4, :], pbs[bt][:, :].rearrange("p (j s q) -> p s j q", j=4, s=16), [('pb', bt)], ['QTs'])
            bo2 = [0, 1]
            for s in range(16):
                i3 = s % NB3
                Sb = Smb[i3]; sres = ('Smb', i3); Ep = Epad[i3]; epres = ('Epad', i3); Eb = ETb[i3]; eres = ('ETb', i3)
                bs = pbank()
                for kvh in range(4):
                    gp = kvh // 2; p0 = (kvh % 2) * 64
                    lq = QTs[p0:p0 + 64, s, 4 * gp:4 * gp + 4, :].rearrange("p j q -> p (j q)")
                    mm(pbs[bs][32 * kvh:32 * kvh + 32, 0:128], lq, KTc[p0:p0 + 64, gp, s * 128:(s + 1) * 128], True, True,
                       ['QTs', ('KTc', s)], [('pb', bs)], tile_position=(p0, 32 * kvh))
                    mm(pbs[bs][32 * kvh:32 * kvh + 32, 128:136], lq, KT[p0:p0 + 64, gp, 2048 + 8 * s:2048 + 8 * s + 8], True, True,
                       ['QTs', ('KT', 16)], [('pb', bs)], tile_position=(p0, 32 * kvh))
                tt(Sb[:, 0:136], pbs[bs][:, 0:136], masks[:, 0:136], ALU.add, [('pb', bs), 'mask'], [sres])
                S.op('dve', (lambda Sb=Sb, s=s, sm=sm: lambda e: e.reduce_max(sm[:, 0, s:s + 1], Sb[:, 0:136], AX.X))(), reads=[sres], writes=[('s_mx', s)], cost=0.25)
                ts(sm[:, 1, s:s + 1], sm[:, 0, s:s + 1], sinkc[:, 0:1], -1.0, ALU.max, ALU.mult, [('s_mx', s), 'sink'], [('s_nm', s)])
                act(Sb[:, 0:128], Sb[:, 0:128], AF.Exp, [sres, ('s_nm', s)], [sres, ('s_rs', s)], bias=sm[:, 1, s:s + 1], accum=sm[:, 2, s:s + 1])
                act(Ep[:, 8 * s:8 * s + 8], Sb[:, 128:136], AF.Exp, [sres, ('s_nm', s), epres], [epres, ('s_rs2', s)], bias=sm[:, 1, s:s + 1], accum=sm[:, 3, s:s + 1])
                bt = pbank()
                tr(pbs[bt][:, 0:128], Sb[:, 0:128], [sres], [('pb', bt)])
                tr(pbs[bt][:, 128:256], Ep[:, :], [epres], [('pb', bt)])
                cp('act' if s % 2 == 0 else 'dve', Eb[:, :, :], pbs[bt][:, 0:256].rearrange("p (j n) -> p j n", n=128), [('pb', bt)], [eres])
                S.op('dve', (lambda Ep=Ep, s=s: lambda e: e.memset(Ep[:, 8 * s:8 * s + 8], 0.0))(), reads=[epres], writes=[epres], cost=0.1)
                bo = bo2[s // 8]
                for kvh in range(4):
                    o_ = pbs[bo][32 * kvh:32 * kvh + 32, (s % 8) * 64:(s % 8 + 1) * 64]
                    mm(o_, Eb[:, 0, 32 * kvh:32 * kvh + 32], Vc[:, s, kvh * 64:(kvh + 1) * 64], True, False, [eres, ('Vc', s)], [('pb', bo)], tile_position=(0, 32 * kvh))
                    mm(o_, Eb[:, 1, 32 * kvh:32 * kvh + 32], Vb[:, 16, kvh * 64:(kvh + 1) * 64], False, True, [eres, ('Vb', 16)], [('pb', bo)], tile_position=(0, 32 * kvh))
            alls = lambda nm: [(nm, s) for s in range(16)]
            tt(sm[:, 5, :], sm[:, 2, :], sm[:, 3, :], ALU.add, alls('s_rs') + alls('s_rs2'), ['s_den'])
            act(sm[:, 4, :], sm[:, 1, :], AF.Exp, alls('s_nm') + ['sink'], ['s_es'], bias=sinkc[:, 0:1])
            tt(sm[:, 5, :], sm[:, 5, :], sm[:, 4, :], ALU.add, ['s_den', 's_es'], ['s_den'])
            S.op('dve', lambda e: e.reciprocal(sm[:, 6, :], sm[:, 5, :]), reads=['s_den'], writes=['s_rden'], cost=0.1)
            Os = Qf
            for hf in range(2):
                tt(Os[:, hf * 512:(hf + 1) * 512].rearrange("p (s d) -> p s d", d=64), pbs[bo2[hf]][:, 0:512].rearrange("p (s d) -> p s d", d=64),
                   sm[:, 6, hf * 8:hf * 8 + 8].unsqueeze(2).to_broadcast([128, 8, 64]), ALU.mult, [('pb', bo2[hf]), 's_rden'], ['Qf0'])
            dma('sp', scr_o, Os[:, :], ['Qf0'], ['scr_o'])
            srcv = scr_o.rearrange("(h q) (s d) -> q s h d", q=8, d=64)
            for q in range(8):
                dma('sp', Of[q::8, :].rearrange("p (h d) -> p h d", d=64), srcv[q], ['scr_o'], ['Of0'])
            for hf in range(2):
                bt = pbank()
                for jj in range(4):
                    c = hf * 4 + jj
                    tr(pbs[bt][:, jj * 128:(jj + 1) * 128], Of[:, c * 128:(c + 1) * 128], ['Of0'], [('pb', bt)])
                cp('act', OT[:, hf * 4:hf * 4 + 4, 0:128], pbs[bt][:, :].rearrange("p (j n) -> p j n", n=128), [('pb', bt)],
                   [('OT', c) for c in range(hf * 4, hf * 4 + 4)])
            for fc in range(NCH):
                bo = pbank()
                for c in range(NCH):
                    mm(pbs[bo][:, 0:n], wo_[:, c, fc * 128:(fc + 1) * 128], OT[:, c, 0:n], c == 0, c == NCH - 1, rwo + [('OT', c)], [('pb', bo)])
                residual(g, fc, bo, G1, [('mod', 2)], X[2], 'X2')

        rowm = sb("rowm_sb", [128, 16]); masksn = sb("masksn_sb", [128, 128])
        rowmd = din("rowm", [128, 16]); masksnd = din("masksn", [128, 128])
        dma('sp', rowm[:], rowmd, [], ['mask']); dma('sp', masksn[:], masksnd, [], ['mask'])
        attn_phase([0, 1, 2, 3], 2, False, shared=(hT, X))
        if stop <= 4:
            return _finish()
        S.stage = 'attn_s'
        attn_sample()
        pb_pool[0] = (0, 1, 2, 3, 4, 5, 6, 7)

        if stop <= 5:
            return _finish()
        S.stage = 'ffn1'
        ffn_stage(1)

        if stop <= 6:
            return _finish()
        S.stage = 'final'
        S.barrier()
        AR.reset()
        yT = AR.f32([128, NCH, 512])
        X = [AR.f32([128, 512]), AR.f32([128, 512]), AR.f32([128, 512])]
        yo = [AR.f32([128, 1024]), AR.f32([128, 1024])]
        for g in range(5):
            t0, n = GROUPS[g]
            norm_group(g, None, None, 'svT', yT, lambda c: ('yT', c), X, out_f32_scale=lambda c: svT[:, SV_FG + c:SV_FG + c + 1])
            for tl in range(n // 128):
                t = t0 // 128 + tl
                yb = yo[t % 2]; yr = ('yo', t % 2)
                for hf in range(2):
                    bt = pbank()
                    for jj in range(4):
                        c = hf * 4 + jj
                        tr(pbs[bt][:, jj * 128:(jj + 1) * 128], yT[:, c, tl * 128:(tl + 1) * 128], [('yT', c)], [('pb', bt)])
                    cp('act' if hf == 0 else 'dve', yb[:, hf * 512:(hf + 1) * 512], pbs[bt][:, 0:512], [('pb', bt)], [yr])
                dst = y_p[t * 128:(t + 1) * 128, :] if t < 16 else y_s
                dma('sp', dst, yb[:, :], [yr], [])
        return _finish()


_CACHE = {}


def _consts():
    ROT = 16
    inv = (500000.0 ** (-np.arange(0, ROT, 2, dtype=np.float32) / np.float32(ROT))).astype(np.float32)
    pos = np.zeros((128, 17), np.float32)
    for t in range(16):
        pos[:, t] = t * 128 + np.arange(128)
    pos[:, 16] = 16384 + (np.arange(128) % 8)
    ang = (pos[:, :, None] * inv[None, None, :]).astype(np.float32)
    cos = np.cos(ang).astype(np.float32); sin = np.sin(ang).astype(np.float32)
    NEG = -30000.0
    q = np.arange(128)[:, None]; s = np.arange(256)[None, :]
    rel = 128 + q - s
    maskp = np.where((rel >= 0) & (rel <= 128), 0.0, NEG).astype(np.float32)
    qi = (np.arange(128) % 8)[:, None]; sq = (np.arange(128) // 8)
    k = np.arange(128)[None, :]
    masks = np.zeros((128, 136), np.float32)
    masks[:, 0:128] = np.where(k >= qi, 0.0, NEG)
    masks[:, 128:136] = np.where(np.arange(8)[None, :] <= qi, 0.0, NEG)
    rowm = np.where(sq[:, None] == np.arange(16)[None, :], 0.0, NEG).astype(np.float32)
    ks = (np.arange(128) // 8)[None, :]; kt = (np.arange(128) % 8)[None, :]
    masksn = np.where((ks == sq[:, None]) & (kt <= qi), 0.0, NEG).astype(np.float32)
    return dict(ident=np.eye(128, dtype=np.float32), cos=cos, sin=sin, maskp=maskp, masks=masks, rowm=rowm, masksn=masksn)


def kernel(x_prompt, x_sample, c_prompt, c_sample, state_conv, state_h, cache_k, cache_v,
           ada_w, ada_b, norm_g, rnn_w_in, rnn_conv_w, rnn_conv_b, rnn_gate_w, rnn_gate_b,
           rnn_lambda, rnn_w_out, kv_ada_w, kv_ada_b, kv_norm_g, w_kv, attn_w_q, attn_sinks,
           attn_w_o, ffn_w_in, ffn_w_out, final_g):
    f = lambda a: np.ascontiguousarray(np.asarray(a, dtype=np.float32))
    if 'nc' not in _CACHE:
        _CACHE['nc'] = build_program()
    nc = _CACHE['nc']
    C = _consts()
    sv = np.concatenate([f(ada_b).reshape(96, 128), f(kv_ada_b).reshape(16, 128), f(norm_g).reshape(32, 128),
                         f(kv_norm_g).reshape(8, 128), f(final_g).reshape(8, 128), f(rnn_conv_w).reshape(32, 128),
                         f(rnn_conv_b).reshape(8, 128), f(rnn_gate_b).reshape(16, 128), f(rnn_lambda).reshape(8, 128)], axis=0)
    sinks = f(attn_sinks)[0]
    def pk(w):
        k = w.shape[0] // 128
        return np.ascontiguousarray(w.reshape(k, 128, w.shape[1]).transpose(1, 0, 2).reshape(128, k * w.shape[1]))
    aw = f(ada_w)
    adaP = np.stack([np.stack([pk(aw[l][:, v * 1024:(v + 1) * 1024]) for v in range(6)]) for l in range(2)])
    kw_ = f(kv_ada_w)
    kvadaP = np.stack([pk(kw_[:, v * 1024:(v + 1) * 1024]) for v in range(2)])
    fi = f(ffn_w_in); fo = f(ffn_w_out)
    ffl = []
    for l in range(2):
        parts = []
        for (h0, hc) in FFN_GROUPS:
            parts.append(pk(fi[l][:, h0 * 128:(h0 + hc) * 128]))
            parts.append(pk(fi[l][:, DFF + h0 * 128:DFF + (h0 + hc) * 128]))
            parts.append(pk(fo[l][h0 * 128:(h0 + hc) * 128, :]))
        ffl.append(np.concatenate(parts, axis=1))
    ffnP = np.ascontiguousarray(np.stack(ffl))
    shared = dict(adaP=adaP, rnn_w_in=pk(f(rnn_w_in)[0]), gate_w=np.ascontiguousarray(f(rnn_gate_w)[0].transpose(1, 0, 2).reshape(128, 2048)),
                  rnn_w_out=pk(f(rnn_w_out)[0]), kvadaP=kvadaP, w_kv=pk(f(w_kv)), w_q=pk(f(attn_w_q)[0]), w_o=pk(f(attn_w_o)[0]), ffnP=ffnP,
                  sv=sv, ident=C['ident'], cos=C['cos'], sin=C['sin'], maskp=C['maskp'],
                  masks=C['masks'], rowm=C['rowm'], masksn=C['masksn'],
                  sinkb=np.ascontiguousarray(np.broadcast_to(sinks[None, :], (128, 16))),
                  sinkc=np.ascontiguousarray(np.repeat(sinks, 8)[:, None]))
    xp_ = f(x_prompt); xs_ = f(x_sample); cp_ = f(c_prompt); cs_ = f(c_sample)
    sc_ = f(state_conv); sh_ = f(state_h); ck_ = f(cache_k); cv_ = f(cache_v)
    in_maps = []
    for b in range(8):
        sl = slice(16 * b, 16 * b + 16)
        m = dict(shared)
        m.update(xp=xp_[b], xs=np.ascontiguousarray(xs_[sl].reshape(128, 1024)),
                 cc=np.ascontiguousarray(np.concatenate([cp_[b:b + 1], cs_[sl]], axis=0)),
                 sconv=np.ascontiguousarray(sc_[0, sl].reshape(48, 1024)), shin=np.ascontiguousarray(sh_[0, sl]),
                 ck=np.ascontiguousarray(ck_[sl].reshape(16, 128, 256)), cv=np.ascontiguousarray(cv_[sl].reshape(16, 128, 256)))
        in_maps.append(m)
    res = run_bass_kernel_spmd(nc, in_maps, core_ids=list(range(8)))
    R = res.results
    cat = lambda k: np.stack([np.asarray(r[k], dtype=np.float32) for r in R], axis=0)
    y_prompt = cat('y_p')
    y_sample = cat('y_s').reshape(128, 8, 1024)
    prompt_conv = cat('pconv').reshape(1, 8, 3, 1024)
    prompt_h = cat('ph').reshape(1, 8, 1024)
    prompt_k = cat('pk').reshape(8, 128, 4, 64)
    prompt_v = cat('pv').reshape(8, 128, 4, 64)
    sample_conv = cat('sconv_o').reshape(1, 128, 3, 1024)
    sample_h = cat('sh_o').reshape(1, 128, 1024)
    sample_k = cat('sk').reshape(128, 128, 4, 64)
    sample_v = cat('svo').reshape(128, 128, 4, 64)
    return (y_prompt, y_sample, prompt_conv, prompt_h, prompt_k, prompt_v, sample_conv, sample_h, sample_k, sample_v)
```
